# Optimizing a Trainium2 kernel written in Bass

```python
import jax, jax.numpy as jnp
from jax import lax
import numpy as np

D_MODEL = 1024
BATCH = 4
SEQ = 4096
DEPTH = 1
DEC_BATCH = 32
DEC_SEQ = 64
PAST_LEN = 1024

CHUNK = 64
Q_BLOCK = 128
HEAD_DIM = 64
N_HEADS_FOX = 8
N_HEADS_DSA = 8
D_FOX = N_HEADS_FOX * HEAD_DIM
D_DSA = N_HEADS_DSA * HEAD_DIM
D_MIX = D_FOX + D_DSA
N_IDX_HEADS = 8
IDX_DIM = 32
TOPK_MAX = 256
D_FF = 4 * D_MODEL
PLE_DIM = 256
ROPE_THETA = 10000.0
LN_EPS = 1e-5
ALPHA = (2 * DEPTH) ** 0.25
BETA = (8 * DEPTH) ** -0.25
NEG = -1e30

SPLIT_POINTS = (
    D_FOX,
    2 * D_FOX,
    3 * D_FOX,
    3 * D_FOX + N_HEADS_FOX,
    3 * D_FOX + N_HEADS_FOX + D_DSA,
    3 * D_FOX + N_HEADS_FOX + 2 * D_DSA,
    3 * D_FOX + N_HEADS_FOX + 3 * D_DSA,
    3 * D_FOX + N_HEADS_FOX + 3 * D_DSA + N_IDX_HEADS * IDX_DIM,
    3 * D_FOX + N_HEADS_FOX + 3 * D_DSA + N_IDX_HEADS * IDX_DIM + IDX_DIM,
)
N_IN = 3 * D_FOX + N_HEADS_FOX + 3 * D_DSA + N_IDX_HEADS * IDX_DIM + IDX_DIM + N_IDX_HEADS

kernel_name = 'hybrid_fox_dsa_streaming_step'


def layer_norm(x, g, b):
    xf = x.astype(jnp.float32)
    mu = jnp.mean(xf, axis=-1, keepdims=True)
    var = jnp.mean(jnp.square(xf - mu), axis=-1, keepdims=True)
    return ((xf - mu) * lax.rsqrt(var + LN_EPS) * g + b).astype(x.dtype)


def rope(x, pos):
    half = x.shape[-1] // 2
    inv_freq = ROPE_THETA ** (-jnp.arange(half, dtype=jnp.float32) / half)
    ang = pos.astype(jnp.float32)[:, None] * inv_freq[None, :]
    cos = jnp.cos(ang)[None, :, None, :]
    sin = jnp.sin(ang)[None, :, None, :]
    xf = x.astype(jnp.float32)
    x1, x2 = xf[..., :half], xf[..., half:]
    return jnp.concatenate([x1 * cos - x2 * sin, x2 * cos + x1 * sin], axis=-1).astype(x.dtype)


def project_mixers(x, w_in, b_f, pos):
    B, L, _ = x.shape
    h = jnp.einsum('bld,dn->bln', x, w_in)
    fq, fk, fv, fg, dq, dk, dv, iq, ik, iw = jnp.split(h, SPLIT_POINTS, axis=-1)
    fq = fq.reshape(B, L, N_HEADS_FOX, HEAD_DIM)
    fk = fk.reshape(B, L, N_HEADS_FOX, HEAD_DIM)
    fv = fv.reshape(B, L, N_HEADS_FOX, HEAD_DIM)
    logf = jax.nn.log_sigmoid((fg + b_f).astype(jnp.float32))
    dq = rope(dq.reshape(B, L, N_HEADS_DSA, HEAD_DIM), pos)
    dk = rope(dk.reshape(B, L, N_HEADS_DSA, HEAD_DIM), pos)
    dv = dv.reshape(B, L, N_HEADS_DSA, HEAD_DIM)
    iq = rope(iq.reshape(B, L, N_IDX_HEADS, IDX_DIM), pos)
    ik = rope(ik[:, :, None, :], pos)[:, :, 0, :]
    iw = iw * (N_IDX_HEADS ** -0.5 * IDX_DIM ** -0.5)
    return fq, fk, fv, logf, dq, dk, dv, iq, ik, iw


def fox_attend(q, k, v, cum_q, cum_k, q_pos, k_pos):
    s = jnp.einsum('bqhd,blhd->bhql', q, k, preferred_element_type=jnp.float32) * HEAD_DIM ** -0.5
    decay = jnp.transpose(cum_q, (0, 2, 1))[:, :, :, None] - jnp.transpose(cum_k, (0, 2, 1))[:, :, None, :]
    mask = k_pos[None, :] <= q_pos[:, None]
    p = jax.nn.softmax(jnp.where(mask, s + decay, NEG), axis=-1)
    return jnp.einsum('bhql,blhd->bqhd', p.astype(v.dtype), v)


def _gather_rows(a, idx):
    return a[idx]


def dsa_attend(q, k, v, iq, ik, iw, q_pos, k_pos, topk):
    logits = jnp.einsum('bqhd,bld->bqhl', iq, ik, preferred_element_type=jnp.float32)
    score = jnp.einsum('bqh,bqhl->bql', iw.astype(jnp.float32), jax.nn.relu(logits))
    admissible = (k_pos[None, :] // CHUNK) <= (q_pos[:, None] // CHUNK)
    score = jnp.where(admissible[None], score, NEG)
    _, sel = lax.top_k(score, topk)
    valid = (k_pos[sel] // CHUNK) <= (q_pos[None, :, None] // CHUNK)
    kg = jax.vmap(_gather_rows)(k, sel)
    vg = jax.vmap(_gather_rows)(v, sel)
    s = jnp.einsum('bqhd,bqkhd->bhqk', q, kg, preferred_element_type=jnp.float32) * HEAD_DIM ** -0.5
    p = jax.nn.softmax(jnp.where(valid[:, None], s, NEG), axis=-1)
    return jnp.einsum('bhqk,bqkhd->bqhd', p.astype(vg.dtype), vg)


def prompt_mixer(x, w_in, b_f):
    B, L, _ = x.shape
    pos = jnp.arange(L, dtype=jnp.int32)
    fq, fk, fv, logf, dq, dk, dv, iq, ik, iw = project_mixers(x, w_in, b_f, pos)
    cum = jnp.cumsum(logf, axis=1)
    topk = min(TOPK_MAX, L // 4)
    nb = L // Q_BLOCK

    def to_blocks(a):
        return jnp.swapaxes(a.reshape((B, nb, Q_BLOCK) + a.shape[2:]), 0, 1)

    def block_fn(args):
        fq_b, cum_b, dq_b, iq_b, iw_b, pos_b = args
        o_fox = fox_attend(fq_b, fk, fv, cum_b, cum, pos_b, pos)
        o_dsa = dsa_attend(dq_b, dk, dv, iq_b, ik, iw_b, pos_b, pos, topk)
        return jnp.concatenate([o_fox.reshape(B, Q_BLOCK, D_FOX), o_dsa.reshape(B, Q_BLOCK, D_DSA)], axis=-1)

    out = lax.map(block_fn, (to_blocks(fq), to_blocks(cum), to_blocks(dq), to_blocks(iq),
                             to_blocks(iw), pos.reshape(nb, Q_BLOCK)))
    mix = jnp.swapaxes(out, 0, 1).reshape(B, L, D_MIX)
    return mix, (fk, fv, logf, dk, dv, ik)


def sample_mixer(x, w_in, b_f, c_fk, c_fv, c_logf, c_dk, c_dv, c_ik):
    B, T, _ = x.shape
    P = c_fk.shape[1]
    pos_new = P + jnp.arange(T, dtype=jnp.int32)
    k_pos = jnp.arange(P + T, dtype=jnp.int32)
    fq, fk, fv, logf, dq, dk, dv, iq, ik, iw = project_mixers(x, w_in, b_f, pos_new)
    kf = jnp.concatenate([c_fk.astype(fk.dtype), fk], axis=1)
    vf = jnp.concatenate([c_fv.astype(fv.dtype), fv], axis=1)
    cum = jnp.cumsum(jnp.concatenate([c_logf.astype(jnp.float32), logf], axis=1), axis=1)
    kd = jnp.concatenate([c_dk.astype(dk.dtype), dk], axis=1)
    vd = jnp.concatenate([c_dv.astype(dv.dtype), dv], axis=1)
    ki = jnp.concatenate([c_ik.astype(ik.dtype), ik], axis=1)
    topk = min(TOPK_MAX, (P + T) // 4)
    o_fox = fox_attend(fq, kf, vf, cum[:, P:], cum, pos_new, k_pos)
    o_dsa = dsa_attend(dq, kd, vd, iq, ki, iw, pos_new, k_pos, topk)
    mix = jnp.concatenate([o_fox.reshape(B, T, D_FOX), o_dsa.reshape(B, T, D_DSA)], axis=-1)
    return mix, (fk, fv, logf, dk, dv, ik)


def finish_layer(x, mix, pe, w_o, ln1_g, ln1_b, w_up, w_down, ln2_g, ln2_b, w_ple, w_ple_gate, b_ple_gate):
    x = layer_norm(ALPHA * x + jnp.einsum('blm,md->bld', mix, w_o), ln1_g, ln1_b)
    hid = jnp.square(jax.nn.relu(jnp.einsum('bld,df->blf', x, w_up)))
    x = layer_norm(ALPHA * x + jnp.einsum('blf,fd->bld', hid, w_down), ln2_g, ln2_b)
    gate = jax.nn.sigmoid(jnp.einsum('bld,de->ble', x, w_ple_gate) + b_ple_gate)
    return x + gate * jnp.einsum('blp,pd->bld', pe, w_ple)


def setup_inputs(seed: int = 0) -> dict:
    key = jax.random.key(seed)
    ks = jax.random.split(key, 24)
    f32 = jnp.float32
    nrm = lambda k, shape: jax.random.normal(k, shape, f32)
    col_scale = jnp.concatenate([
        jnp.ones((2 * D_FOX,), f32), jnp.full((D_FOX,), BETA, f32),
        jnp.ones((N_HEADS_FOX + 2 * D_DSA,), f32), jnp.full((D_DSA,), BETA, f32),
        jnp.ones((N_IDX_HEADS * IDX_DIM + IDX_DIM + N_IDX_HEADS,), f32)])
    return {
        'x_prompt': nrm(ks[0], (BATCH, SEQ, D_MODEL)),
        'x_sample': nrm(ks[1], (DEC_BATCH, DEC_SEQ, D_MODEL)),
        'p_prompt': nrm(ks[2], (DEPTH, BATCH, SEQ, PLE_DIM)),
        'p_sample': nrm(ks[3], (DEPTH, DEC_BATCH, DEC_SEQ, PLE_DIM)),
        'cache_fox_k': nrm(ks[4], (DEPTH, DEC_BATCH, PAST_LEN, N_HEADS_FOX, HEAD_DIM)),
        'cache_fox_v': BETA * nrm(ks[5], (DEPTH, DEC_BATCH, PAST_LEN, N_HEADS_FOX, HEAD_DIM)),
        'cache_fox_logf': jax.nn.log_sigmoid(3.0 + nrm(ks[6], (DEPTH, DEC_BATCH, PAST_LEN, N_HEADS_FOX))),
        'cache_dsa_k': nrm(ks[7], (DEPTH, DEC_BATCH, PAST_LEN, N_HEADS_DSA, HEAD_DIM)),
        'cache_dsa_v': BETA * nrm(ks[8], (DEPTH, DEC_BATCH, PAST_LEN, N_HEADS_DSA, HEAD_DIM)),
        'cache_idx_k': nrm(ks[9], (DEPTH, DEC_BATCH, PAST_LEN, IDX_DIM)),
        'w_in': nrm(ks[10], (DEPTH, D_MODEL, N_IN)) * D_MODEL ** -0.5 * col_scale,
        'b_f': 3.0 + 0.1 * nrm(ks[11], (DEPTH, N_HEADS_FOX)),
        'w_o': nrm(ks[12], (DEPTH, D_MIX, D_MODEL)) * BETA * D_MIX ** -0.5,
        'ln1_g': 1.0 + 0.02 * nrm(ks[13], (DEPTH, D_MODEL)),
        'ln1_b': 0.02 * nrm(ks[14], (DEPTH, D_MODEL)),
        'w_up': nrm(ks[15], (DEPTH, D_MODEL, D_FF)) * D_MODEL ** -0.5,
        'w_down': nrm(ks[16], (DEPTH, D_FF, D_MODEL)) * BETA * D_FF ** -0.5,
        'ln2_g': 1.0 + 0.02 * nrm(ks[17], (DEPTH, D_MODEL)),
        'ln2_b': 0.02 * nrm(ks[18], (DEPTH, D_MODEL)),
        'w_ple': nrm(ks[19], (DEPTH, PLE_DIM, D_MODEL)) * PLE_DIM ** -0.5,
        'w_ple_gate': nrm(ks[20], (DEPTH, D_MODEL, D_MODEL)) * D_MODEL ** -0.5,
        'b_ple_gate': 0.02 * nrm(ks[21], (DEPTH, D_MODEL)),
    }


def reference(x_prompt, x_sample, p_prompt, p_sample, cache_fox_k, cache_fox_v, cache_fox_logf,
              cache_dsa_k, cache_dsa_v, cache_idx_k, w_in, b_f, w_o, ln1_g, ln1_b, w_up, w_down,
              ln2_g, ln2_b, w_ple, w_ple_gate, b_ple_gate):
    y_p, y_s = x_prompt, x_sample
    new_rows = [[] for _ in range(12)]
    for i in range(DEPTH):
        mix_p, st_p = prompt_mixer(y_p, w_in[i], b_f[i])
        y_p = finish_layer(y_p, mix_p, p_prompt[i], w_o[i], ln1_g[i], ln1_b[i], w_up[i], w_down[i],
                           ln2_g[i], ln2_b[i], w_ple[i], w_ple_gate[i], b_ple_gate[i])
        mix_s, st_s = sample_mixer(y_s, w_in[i], b_f[i], cache_fox_k[i], cache_fox_v[i], cache_fox_logf[i],
                                   cache_dsa_k[i], cache_dsa_v[i], cache_idx_k[i])
        y_s = finish_layer(y_s, mix_s, p_sample[i], w_o[i], ln1_g[i], ln1_b[i], w_up[i], w_down[i],
                           ln2_g[i], ln2_b[i], w_ple[i], w_ple_gate[i], b_ple_gate[i])
        for lst, a in zip(new_rows, st_p + st_s):
            lst.append(a)
    (fox_k_p, fox_v_p, fox_logf_p, dsa_k_p, dsa_v_p, idx_k_p,
     fox_k_s, fox_v_s, fox_logf_s, dsa_k_s, dsa_v_s, idx_k_s) = [jnp.stack(a, axis=0) for a in new_rows]
    return (y_p, y_s, fox_k_p, fox_v_p, fox_logf_p, dsa_k_p, dsa_v_p, idx_k_p,
            fox_k_s, fox_v_s, fox_logf_s, dsa_k_s, dsa_v_s, idx_k_s)
```

```python
import os
import types
import contextlib
import numpy as np
import concourse.bass as bass
import concourse.mybir as mybir
from concourse.bass_utils import run_bass_kernel_spmd

F32 = mybir.dt.float32
BF16 = mybir.dt.bfloat16
ALU = mybir.AluOpType
AF = mybir.ActivationFunctionType
AX = mybir.AxisListType

D = 1024
SEQ = 4096
NB = 32
NOWN = 16
NQT = 8
SB = 4
PAST = 1024
T = 64
NBS = 9
LS = NBS * 128
KC = 2088
QC = 1288
NEG = -1.0e30
BIG = 1.0e30
NITER = 16
ALPHA = 2.0 ** 0.25
STOP = 99


class Res:
    __slots__ = ("name", "w", "rs")

    def __init__(self, name):
        self.name = name
        self.w = None
        self.rs = []


class DSem:
    __slots__ = ("h", "total", "mk", "sub")

    def __init__(self, h=None, mk=None):
        self.h = h
        self.total = 0
        self.mk = mk
        self.sub = {}

    def for_queue(self, eng, prog):
        k = "sw" if eng == "pool" else "hw"
        if k not in self.sub:
            d = DSem(self.mk(), None)
            prog.dsems.append(d)
            self.sub[k] = d
        return self.sub[k]


class Op:
    __slots__ = ("eng", "fn", "waits", "needed", "val", "dsem")

    def __init__(self, eng, fn):
        self.eng = eng
        self.fn = fn
        self.waits = []
        self.needed = False
        self.val = 0
        self.dsem = None


def _freeze(fn):
    if fn.__closure__ is None:
        return fn
    cells = []
    for c in fn.__closure__:
        try:
            cells.append(types.CellType(c.cell_contents))
        except ValueError:
            cells.append(c)
    return types.FunctionType(fn.__code__, fn.__globals__, fn.__name__, fn.__defaults__, tuple(cells))


class Prog:
    ENGS = ("pe", "act", "dve", "pool", "sp")

    def __init__(self):
        self.q = {e: [] for e in self.ENGS}
        self.dsems = []
        self.pending = {e: None for e in self.ENGS}

    def new_dsem(self, mk):
        return DSem(None, mk)

    def _mk(self, eng, fn, reads, writes):
        o = Op(eng, _freeze(fn))
        deps = []
        for r in reads:
            if r.w is not None:
                deps.append(r.w)
        for r in writes:
            if r.w is not None:
                deps.append(r.w)
            deps.extend(r.rs)
        for d in deps:
            if d.dsem is not None:
                o.waits.append((d.dsem, d.dsem.total))
            else:
                if d.eng == eng and eng == "pe":
                    continue
                d.needed = True
                o.waits.append(d)
        pb = self.pending[eng]
        if pb is not None:
            o.waits.extend(pb)
            self.pending[eng] = None
        for r in reads:
            r.rs.append(o)
        for r in writes:
            r.w = o
            r.rs = []
        self.q[eng].append(o)
        return o

    def op(self, eng, fn, reads=(), writes=()):
        return self._mk(eng, fn, reads, writes)

    def dma(self, eng, dsem, out, in_, reads=(), writes=()):
        o = self._mk(eng, lambda e: e.dma_start(out=out, in_=in_), reads, writes)
        sub = dsem.for_queue(eng, self)
        o.dsem = sub
        sub.total += 16
        return o

    def barrier(self):
        snap = []
        for e in self.ENGS:
            for o in reversed(self.q[e]):
                if o.dsem is None:
                    o.needed = True
                    snap.append(o)
                    break
        for d in self.dsems:
            if d.total:
                snap.append((d, d.total))
        for e in self.ENGS:
            self.pending[e] = list(snap) + (self.pending[e] or [])

    def emit(self, nc, block, esem):
        for e in self.ENGS:
            c = 0
            for o in self.q[e]:
                if o.dsem is None and o.needed:
                    c += 1
                    o.val = c
        engobj = {"pe": "tensor", "act": "scalar", "dve": "vector", "pool": "gpsimd", "sp": "sync"}
        final = [(d, d.total) for d in self.dsems if d.total]

        def run(ename, eng):
            waited = {}
            for o in self.q[ename]:
                for w in o.waits:
                    if isinstance(w, tuple):
                        sem, val = w[0].h, w[1]
                    else:
                        sem, val = esem[w.eng], w.val
                    key = id(sem)
                    if waited.get(key, 0) >= val:
                        continue
                    waited[key] = val
                    eng.wait_ge(sem, val)
                ins = o.fn(eng)
                if o.dsem is not None:
                    ins.then_inc(o.dsem.h, 16)
                elif o.needed:
                    ins.then_inc(esem[ename], 1)
            if ename == "sp":
                for d, tot in final:
                    eng.wait_ge(d.h, tot)

        for ename in self.ENGS:
            getattr(block, engobj[ename])(lambda eng, _n=ename: run(_n, eng))


def build():
    nc = bass.Bass("TRN2", target_bir_lowering=False)
    es = contextlib.ExitStack()
    P = Prog()

    def din(name, shape):
        return nc.dram_tensor(name, list(shape), F32, kind="ExternalInput").ap()

    def dout(name, shape):
        return nc.dram_tensor(name, list(shape), F32, kind="ExternalOutput").ap()

    def dscr(name, shape, dt=BF16):
        return nc.dram_tensor(name, list(shape), dt).ap()

    def sbt(name, shape, dt):
        return es.enter_context(nc.sbuf_tensor(name, list(shape), dt))

    nsem = [0]

    def _mksem():
        nsem[0] += 1
        return es.enter_context(nc.semaphore("ds%d" % nsem[0]))

    def dsem():
        return P.new_dsem(_mksem)

    xall = din("xall", [SEQ, D]); xown = din("xown", [NOWN * 128, D]); pown = din("pown", [NOWN * 128, 256])
    xs = din("xs", [SB * T, D]); pss = din("pss", [SB * T, 256])
    cfk = din("cfk", [SB, PAST, 512]); cfv = din("cfv", [SB, PAST, 512]); clf = din("clf", [SB, PAST, 8])
    cdk = din("cdk", [SB, PAST, 512]); cdv = din("cdv", [SB, PAST, 512]); cik = din("cik", [SB, PAST, 32])
    wk = din("wk", [D, KC]); wq = din("wq", [D, QC]); wo = din("wo", [D, D]); wup = din("wup", [D, 4 * D])
    wdn = din("wdn", [4 * D, D]); wg = din("wg", [D, D]); wple = din("wple", [256, D])
    vecs = din("vecs", [128, 8 + 5 * D])
    ropeall = din("ropeall", [SEQ, 96]); ropeown = din("ropeown", [NOWN * 128, 96]); ropes = din("ropes", [128, 96])
    consts = din("consts", [128, 384])
    fmask_d = din("fmask", [128, 2 * 4 * 256]); smask_d = din("smask", [128, 64]); pencap_d = din("pencap", [128, 2 * 2 * 512])

    o_y = dout("o_y", [NOWN * 128, D]); o_ys = dout("o_ys", [SB * T, D])
    o_fk = dout("o_fk", [SEQ, 512]); o_fv = dout("o_fv", [SEQ, 512]); o_lf = dout("o_lf", [SEQ, 8])
    o_dk = dout("o_dk", [SEQ, 512]); o_dv = dout("o_dv", [SEQ, 512]); o_ik = dout("o_ik", [SEQ, 32])
    o_sfk = dout("o_sfk", [SB * T, 512]); o_sfv = dout("o_sfv", [SB * T, 512]); o_slf = dout("o_slf", [SB * T, 8])
    o_sdk = dout("o_sdk", [SB * T, 512]); o_sdv = dout("o_sdv", [SB * T, 512]); o_sik = dout("o_sik", [SB * T, 32])

    LMAX = SEQ
    s_kt = [dscr("s_kt%d" % i, [9, 128, SEQ if i == 0 else LS]) for i in range(1 + SB)]
    s_v = [dscr("s_v%d" % i, [128, NB if i == 0 else NBS, 2048]) for i in range(1 + SB)]
    s_ikt = [a_[8] for a_ in s_kt]
    NTOK = NOWN * 128 + SB * T
    s_mix = dscr("s_mix", [8, 128, NTOK])
    r_scr = [Res("scr%d" % i) for i in range(1 + SB)]
    r_smix = Res("smix")

    Bk = [es.enter_context(nc.psum_tensor("pb%d" % i, [128, 512], F32)) for i in range(6)]
    PTf = es.enter_context(nc.psum_tensor("ptf", [128, 1024], F32))
    PT = PTf[:, :].bitcast(BF16)
    Bk = [b_[:, :] for b_ in Bk] + [PTf[:, 0:512], PTf[:, 512:1024]]
    rB = [Res("pb%d" % i) for i in range(8)]
    rPT = Res("PT")

    cst = sbt("cst", [128, 384], F32)
    identb = sbt("identb", [128, 128], BF16)
    vec = sbt("vec", [128, 8], F32)
    r_cst = Res("cst"); r_identb = Res("identb"); r_vec = Res("vec")
    identf = cst[:, 0:128]; Umat = cst[:, 128:256]; Elast = cst[:, 256:384]
    bfb = vec[:, 0:8]
    LOGF = [sbt("LOGF%d" % i, [128, NB if i == 0 else NBS, 8], F32) for i in range(1 + SB)]
    r_LOGF = [Res("LOGF%d" % i) for i in range(1 + SB)]
    CK = [sbt("CK%d" % i, [128, NB if i == 0 else NBS, 8], F32) for i in range(1 + SB)]
    CARRY = [sbt("CARRY%d" % i, [128, NB if i == 0 else NBS, 8], F32) for i in range(1 + SB)]
    r_cum = [Res("cum%d" % i) for i in range(1 + SB)]
    r_cumtmp = Res("cumtmp")
    ones32 = sbt("ones32", [128, 32], F32); r_ones32 = Res("ones32")
    IW = sbt("IW", [128, NOWN + SB, 8], F32)
    r_Q = Res("Qside")

    ARENA_BYTES = 203648
    arena = sbt("arena", [128, ARENA_BYTES // 2], BF16)
    apos = [0]

    def aview(shape, dt):
        n = 1
        for s_ in shape[1:]:
            n *= s_
        nbytes = n * (4 if dt == F32 else 2)
        nbytes = (nbytes + 63) // 64 * 64
        off = apos[0]
        apos[0] += nbytes
        assert apos[0] <= ARENA_BYTES, ("arena overflow", apos[0])
        a = arena[:, off // 2:(off + n * (4 if dt == F32 else 2)) // 2]
        if dt == F32:
            a = a.bitcast(F32)
        if len(shape) == 3:
            a = a.rearrange("p (a b) -> p a b", b=shape[2])
        elif len(shape) == 4:
            a = a.rearrange("p (a b c) -> p a b c", b=shape[2], c=shape[3])
        return a

    d_init = dsem()

    P.dma("sp", d_init, cst[:], consts[:, :], writes=[r_cst])
    P.dma("sp", d_init, vec[:], vecs[:, 0:8], writes=[r_vec])
    P.dma("pool", d_init, identb[:], consts[:, 0:128], writes=[r_identb])
    P.op("dve", lambda e: e.memset(ones32[:], 1.0), writes=[r_ones32])

    apos[0] = 0
    QTf = aview([128, 4, NTOK], BF16); QTd = aview([128, 4, NTOK], BF16); IQT = aview([128, 4, NTOK], BF16)
    APOS_AB = apos[0]
    WK = aview([128, 8, KC], BF16); WQ = aview([128, 8, QC], BF16)
    r_WK = Res("WK"); r_WQ = Res("WQ")
    xin = [aview([128, D], F32) for _ in range(2)]; r_xin = [Res("xin%d" % i) for i in range(2)]
    xT = [aview([128, 8, 128], BF16) for _ in range(2)]; r_xT = [Res("xT%d" % i) for i in range(2)]
    hb = [aview([128, KC], F32) for _ in range(3)]; r_hb = [Res("hb%d" % i) for i in range(3)]
    kb16 = [aview([128, 1536], BF16) for _ in range(2)]; r_kb16 = [Res("kb16%d" % i) for i in range(2)]
    vb16 = [aview([128, 2048], BF16) for _ in range(2)]; r_vb16 = [Res("vb16%d" % i) for i in range(2)]
    ktst = [aview([128, 1152], BF16) for _ in range(2)]; r_ktst = [Res("ktst%d" % i) for i in range(2)]
    rp = [aview([128, 96], F32) for _ in range(4)]; r_rp = [Res("rp%d" % i) for i in range(4)]
    d_rp = [dsem() for _ in range(4)]
    rt = [aview([128, 512], F32) for _ in range(4)]; r_rt = [Res("rt%d" % i) for i in range(4)]
    lft = [aview([128, 8], F32) for _ in range(2)]; r_lft = Res("lft")
    cumtmp = aview([128, 3, NB * 8], F32)
    d_w = dsem(); d_x = [dsem(), dsem()]; d_hb = [dsem(), dsem(), dsem()]; d_st = [dsem(), dsem()]

    wkv = wk.rearrange("(k p) n -> p k n", p=128)
    wqv = wq.rearrange("(k p) n -> p k n", p=128)
    for k in range(8):
        P.dma("pool", d_w, WK[:, k, :], wkv[:, k, :], writes=[r_WK])
    for k in range(8):
        P.dma("pool", d_w, WQ[:, k, :], wqv[:, k, :], writes=[r_WQ])
    for s_ in range(2):
        P.op("pool", lambda e, s_=s_: e.memset(kb16[s_][:, :], 0.0), writes=[r_kb16[s_]])
        P.op("pool", lambda e, s_=s_: e.memset(vb16[s_][:, :], 1.0), writes=[r_vb16[s_]])

    def load_x(slot, xrows, roperows, n=128, rs_=0):
        if n < 128:
            P.op("pool", lambda e: e.memset(xin[slot][:, :], 0.0), writes=[r_xin[slot]])
        P.dma("sp", d_x[slot], xin[slot][0:n, :], xrows, writes=[r_xin[slot]])
        P.dma("sp", d_rp[rs_], rp[rs_][:, :], roperows, writes=[r_rp[rs_]])

    def transpose_x(slot):
        for half in range(2):
            bk = half
            for kc in range(4):
                c = half * 4 + kc
                P.op("pe", lambda e, c=c, kc=kc, bk=bk: e.transpose(out=Bk[bk][:, kc * 128:(kc + 1) * 128],
                                                              in_=xin[slot][:, c * 128:(c + 1) * 128], identity=identf),
                     reads=[r_xin[slot], r_cst], writes=[rB[bk]])
            dst = xT[slot][:, half * 4:(half + 1) * 4, :]
            src = Bk[bk][:, :].rearrange("p (a b) -> p a b", b=128)
            if half == 0:
                P.op("act", lambda e, dst=dst, src=src: e.copy(out=dst, in_=src), reads=[rB[bk]], writes=[r_xT[slot]])
            else:
                P.op("dve", lambda e, dst=dst, src=src: e.tensor_copy(out=dst, in_=src), reads=[rB[bk]], writes=[r_xT[slot]])

    def project(slot, W, rW, ncols, dst, rdst):
        c0 = 0
        ci = 0
        while c0 < ncols:
            cw = min(512, ncols - c0)
            bk = 2 + (ci % 2)
            for kc in range(8):
                P.op("pe", lambda e, kc=kc, c0=c0, cw=cw, bk=bk: e.matmul(out=Bk[bk][:, 0:cw], lhsT=xT[slot][:, kc, :],
                                                                      rhs=W[:, kc, c0:c0 + cw], start=(kc == 0), stop=(kc == 7)),
                     reads=[r_xT[slot], rW], writes=[rB[bk]])
            if ci % 2 == 0:
                P.op("act", lambda e, c0=c0, cw=cw, bk=bk: e.copy(out=dst[:, c0:c0 + cw], in_=Bk[bk][:, 0:cw]),
                     reads=[rB[bk]], writes=[rdst])
            else:
                P.op("dve", lambda e, c0=c0, cw=cw, bk=bk: e.tensor_copy(out=dst[:, c0:c0 + cw], in_=Bk[bk][:, 0:cw]),
                     reads=[rB[bk]], writes=[rdst])
            c0 += cw
            ci += 1

    def rope(buf, rbuf, H, half, cosap, sinap, rrope):
        v = buf.rearrange("p (h t d) -> p h t d", t=2, d=half)
        x1 = v[:, :, 0, :]; x2 = v[:, :, 1, :]
        cb = cosap.unsqueeze(1).to_broadcast([128, H, half])
        sb_ = sinap.unsqueeze(1).to_broadcast([128, H, half])
        tv = [rt[i][:, 0:H * half].rearrange("p (h d) -> p h d", d=half) for i in range(4)]
        P.op("dve", lambda e: e.tensor_tensor(out=tv[0], in0=x1, in1=cb, op=ALU.mult), reads=[rbuf, rrope], writes=[r_rt[0]])
        P.op("dve", lambda e: e.tensor_tensor(out=tv[1], in0=x2, in1=sb_, op=ALU.mult), reads=[rbuf, rrope], writes=[r_rt[1]])
        P.op("dve", lambda e: e.tensor_tensor(out=tv[2], in0=x2, in1=cb, op=ALU.mult), reads=[rbuf, rrope], writes=[r_rt[2]])
        P.op("dve", lambda e: e.tensor_tensor(out=tv[3], in0=x1, in1=sb_, op=ALU.mult), reads=[rbuf, rrope], writes=[r_rt[3]])
        P.op("dve", lambda e: e.tensor_tensor(out=x1, in0=tv[0], in1=tv[1], op=ALU.subtract), reads=[r_rt[0], r_rt[1]], writes=[rbuf])
        P.op("dve", lambda e: e.tensor_tensor(out=x2, in0=tv[2], in1=tv[3], op=ALU.add), reads=[r_rt[2], r_rt[3]], writes=[rbuf])

    def kside_finish(hs, rs_):
        h_ = hb[hs]
        z = lft[0]; z2 = lft[1]
        P.op("dve", lambda e: e.tensor_tensor(out=z[:, :], in0=h_[:, 2080:2088], in1=bfb, op=ALU.add),
             reads=[r_hb[hs], r_vec], writes=[r_lft])
        P.op("act", lambda e: e.activation(out=z2[:, :], in_=z[:, :], func=AF.Exp, scale=-1.0), reads=[r_lft], writes=[r_lft])
        P.op("act", lambda e: e.activation(out=z[:, :], in_=z2[:, :], func=AF.Ln, bias=1.0), reads=[r_lft], writes=[r_lft])
        P.op("dve", lambda e: e.tensor_scalar(out=h_[:, 2080:2088], in0=z[:, :], scalar1=-1.0, scalar2=None, op0=ALU.mult),
             reads=[r_lft], writes=[r_hb[hs]])
        rope(h_[:, 512:1024], r_hb[hs], 8, 32, rp[rs_][:, 0:32], rp[rs_][:, 32:64], r_rp[rs_])
        rope(h_[:, 2048:2080], r_hb[hs], 1, 16, rp[rs_][:, 64:80], rp[rs_][:, 80:96], r_rp[rs_])

    def kside_post(hs, ks, seq, blk):
        h_ = hb[hs]
        P.op("dve", lambda e: e.tensor_copy(out=LOGF[seq][:, blk, :], in_=h_[:, 2080:2088]), reads=[r_hb[hs]], writes=[r_LOGF[seq]])
        P.op("act", lambda e: e.copy(out=kb16[ks][:, 0:512], in_=h_[:, 0:512]), reads=[r_hb[hs]], writes=[r_kb16[ks]])
        P.op("dve", lambda e: e.tensor_copy(out=kb16[ks][:, 512:1024], in_=h_[:, 512:1024]), reads=[r_hb[hs]], writes=[r_kb16[ks]])
        ikd = kb16[ks][:, 1024:1152].rearrange("p (a b) -> p a b", b=64)[:, :, 0:32]
        iks = h_[:, 2048:2080].unsqueeze(1).to_broadcast([128, 2, 32])
        P.op("dve", lambda e: e.tensor_copy(out=ikd, in_=iks), reads=[r_hb[hs]], writes=[r_kb16[ks]])
        vdst = vb16[ks][:, :].rearrange("p (t h d) -> p t h d", t=2, d=128)[:, :, :, 0:64]
        vsrc_ = h_[:, 1024:2048].rearrange("p (t h d) -> p t h d", t=2, d=64)
        P.op("act", lambda e: e.copy(out=vdst, in_=vsrc_), reads=[r_hb[hs]], writes=[r_vb16[ks]])
        for c in range(9):
            P.op("pe", lambda e, c=c: e.transpose(out=PT[:, c * 128:(c + 1) * 128], in_=kb16[ks][:, c * 128:(c + 1) * 128],
                                                  identity=identb[:, :]),
                 reads=[r_kb16[ks], r_identb], writes=[rPT])
        P.op("act", lambda e: e.copy(out=ktst[ks][:, :], in_=PT[:, 0:1152]), reads=[rPT], writes=[r_ktst[ks]])
        cols = slice(blk * 128, (blk + 1) * 128)
        P.dma("act", d_st[ks], s_kt[seq].rearrange("h p l -> p h l")[:, :, cols],
              ktst[ks][:, 0:1152].rearrange("p (h l) -> p h l", l=128), reads=[r_ktst[ks]], writes=[r_scr[seq]])
        P.dma("act", d_st[ks], s_v[seq][:, blk, :], vb16[ks][:, 0:2048], reads=[r_vb16[ks]], writes=[r_scr[seq]])

    def kside_outputs(hs, outs, rows, n=128):
        ofk, ofv, olf, odk, odv, oik = outs
        h_ = hb[hs]
        for (o_, a, b) in ((ofk, 0, 512), (odk, 512, 1024), (ofv, 1024, 1536), (odv, 1536, 2048), (oik, 2048, 2080), (olf, 2080, 2088)):
            P.dma("pool", d_hb[hs], o_[rows, :], h_[0:n, a:b], reads=[r_hb[hs]])

    def qside_finish(hs, rs_, ks, tokoff, iwidx, nr=128):
        h_ = hb[hs]; rr = r_hb[hs]
        rope(h_[:, 512:1024], rr, 8, 32, rp[rs_][:, 0:32], rp[rs_][:, 32:64], r_rp[rs_])
        rope(h_[:, 1024:1280], rr, 8, 16, rp[rs_][:, 64:80], rp[rs_][:, 80:96], r_rp[rs_])
        P.op("dve", lambda e: e.tensor_scalar(out=IW[:, iwidx, :], in0=h_[:, 1280:1288], scalar1=1.0 / 16.0, scalar2=None, op0=ALU.mult),
             reads=[rr], writes=[r_Q])
        kq = kb16[ks]; rk = r_kb16[ks]
        P.op("act", lambda e: e.copy(out=kq[:, 0:512], in_=h_[:, 0:512]), reads=[rr], writes=[rk])
        P.op("dve", lambda e: e.tensor_copy(out=kq[:, 512:1024], in_=h_[:, 512:1024]), reads=[rr], writes=[rk])
        iqd = kq[:, 1024:1536].rearrange("p (a b) -> p a b", b=64)[:, :, 0:32]
        iqs = h_[:, 1024:1280].rearrange("p (a b) -> p a b", b=32)
        P.op("dve", lambda e: e.tensor_copy(out=iqd, in_=iqs), reads=[rr], writes=[rk])

    def qside_post(ks, tokoff, nr=128):
        kq = kb16[ks]; rk = r_kb16[ks]
        for c in range(12):
            P.op("pe", lambda e, c=c: e.transpose(out=PT[:, c * 128:(c + 1) * 128], in_=kq[:, c * 128:(c + 1) * 128], identity=identb[:, :]),
                 reads=[rk, r_identb], writes=[rPT])
        for i_, (dst, eng) in enumerate(((QTf, "act"), (QTd, "dve"), (IQT, "act"))):
            src = PT[:, i_ * 512:(i_ + 1) * 512].rearrange("p (a b) -> p a b", b=128)[:, :, 0:nr]
            d_ = dst[:, :, tokoff:tokoff + nr]
            if eng == "act":
                P.op("act", lambda e, d_=d_, src=src: e.copy(out=d_, in_=src), reads=[rPT], writes=[r_Q])
            else:
                P.op("dve", lambda e, d_=d_, src=src: e.tensor_copy(out=d_, in_=src), reads=[rPT], writes=[r_Q])

    prompt_outs = (o_fk, o_fv, o_lf, o_dk, o_dv, o_ik)
    sample_outs = (o_sfk, o_sfv, o_slf, o_sdk, o_sdv, o_sik)
    items = []
    nitem = [0]
    rs_cur = [0]

    def slots():
        i = nitem[0]; nitem[0] += 1
        rs_cur[0] = i % 4
        return i % 3, i % 2, i % 2

    for i in range(NB):
        hs, xs_, ks = slots()
        rs_ = rs_cur[0]
        rows = slice(i * 128, (i + 1) * 128)

        def s1(hs=hs, xs_=xs_, rows=rows, rs_=rs_):
            load_x(xs_, xall[rows, :], ropeall[rows, :], rs_=rs_)
            transpose_x(xs_)
            project(xs_, WK, r_WK, KC, hb[hs], r_hb[hs])

        def s2(hs=hs, rs_=rs_, rows=rows):
            kside_finish(hs, rs_)

        def s3(hs=hs, ks=ks, i=i, rows=rows):
            kside_outputs(hs, prompt_outs, rows)
            kside_post(hs, ks, 0, i)
        items.append((s1, s2, s3))
    for i in range(NOWN):
        hs, xs_, ks = slots()
        rs_ = rs_cur[0]
        rows = slice(i * 128, (i + 1) * 128)

        def s1(hs=hs, xs_=xs_, rows=rows, rs_=rs_):
            load_x(xs_, xown[rows, :], ropeown[rows, :], rs_=rs_)
            transpose_x(xs_)
            project(xs_, WQ, r_WQ, QC, hb[hs], r_hb[hs])

        def s2(hs=hs, rs_=rs_, ks=ks, i=i):
            qside_finish(hs, rs_, ks, i * 128, i)

        def s3(ks=ks, i=i):
            qside_post(ks, i * 128)
        items.append((s1, s2, s3))
    for sbi in range(SB):
        seq = 1 + sbi
        for blk in range(8):
            hs, xs_, ks = slots()
            rows = slice(blk * 128, (blk + 1) * 128)

            def s1(hs=hs, sbi=sbi, rows=rows):
                for (src, a, b) in ((cfk, 0, 512), (cdk, 512, 1024), (cfv, 1024, 1536), (cdv, 1536, 2048), (cik, 2048, 2080), (clf, 2080, 2088)):
                    P.dma("sp", d_hb[hs], hb[hs][:, a:b], src[sbi, rows, :], writes=[r_hb[hs]])

            def s3(hs=hs, ks=ks, seq=seq, blk=blk):
                kside_post(hs, ks, seq, blk)
            items.append((s1, None, s3))
        hs, xs_, ks = slots()
        rs_ = rs_cur[0]
        rows = slice(sbi * T, (sbi + 1) * T)

        def s1(hs=hs, xs_=xs_, rows=rows, rs_=rs_):
            load_x(xs_, xs[rows, :], ropes[:, :], n=T, rs_=rs_)
            transpose_x(xs_)
            project(xs_, WK, r_WK, KC, hb[hs], r_hb[hs])

        def s2(hs=hs, rs_=rs_, rows=rows):
            kside_finish(hs, rs_)

        def s3(hs=hs, ks=ks, seq=seq, rows=rows):
            kside_outputs(hs, sample_outs, rows, n=T)
            kside_post(hs, ks, seq, 8)
        items.append((s1, s2, s3))
        xs_k = xs_
        rs_k = rs_
        hs, _unused, ks = slots()

        def s1(hs=hs, xs_k=xs_k):
            project(xs_k, WQ, r_WQ, QC, hb[hs], r_hb[hs])

        def s2(hs=hs, rs_k=rs_k, ks=ks, sbi=sbi):
            qside_finish(hs, rs_k, ks, NOWN * 128 + sbi * T, NOWN + sbi, nr=T)

        def s3(ks=ks, sbi=sbi):
            qside_post(ks, NOWN * 128 + sbi * T, nr=T)
        items.append((s1, s2, s3))
        nitem[0] += 0
    for t in range(len(items) + 2):
        if t < len(items) and items[t][0] is not None:
            items[t][0]()
        if 0 <= t - 1 < len(items) and items[t - 1][1] is not None:
            items[t - 1][1]()
        if 0 <= t - 2 < len(items) and items[t - 2][2] is not None:
            items[t - 2][2]()

    def cum(seq, nb):
        n8 = nb * 8
        lf2 = LOGF[seq][:, :, :].rearrange("p a b -> p (a b)")
        PF = cumtmp[:, 0, 0:n8]; TBt = cumtmp[:, 1, 0:n8]; INC = cumtmp[:, 2, 0:n8]
        P.op("pe", lambda e: e.matmul(out=Bk[4][:, 0:n8], lhsT=Umat, rhs=lf2, start=True, stop=True),
             reads=[r_LOGF[seq], r_cst], writes=[rB[4]])
        P.op("dve", lambda e: e.tensor_copy(out=PF, in_=Bk[4][:, 0:n8]), reads=[rB[4]], writes=[r_cumtmp])
        P.op("pe", lambda e: e.matmul(out=Bk[5][:, 0:n8], lhsT=Elast, rhs=PF, start=True, stop=True),
             reads=[r_cumtmp, r_cst], writes=[rB[5]])
        P.op("dve", lambda e: e.tensor_copy(out=TBt, in_=Bk[5][:, 0:n8]), reads=[rB[5]], writes=[r_cumtmp])
        TB3 = TBt.rearrange("p (a b) -> p a b", b=8); INC3 = INC.rearrange("p (a b) -> p a b", b=8)
        for h in range(8):
            P.op("dve", lambda e, h=h: e.tensor_tensor_scan(out=INC3[:, :, h], data0=ones32[:, 0:nb], data1=TB3[:, :, h], initial=0.0,
                                                            op0=ALU.mult, op1=ALU.add),
                 reads=[r_cumtmp, r_ones32], writes=[r_cumtmp])
        P.op("dve", lambda e: e.memset(CARRY[seq][:, 0, :], 0.0), writes=[r_cum[seq]])
        P.op("dve", lambda e: e.tensor_copy(out=CARRY[seq][:, 1:nb, :], in_=INC3[:, 0:nb - 1, :]), reads=[r_cumtmp], writes=[r_cum[seq]])
        P.op("dve", lambda e: e.tensor_tensor(out=CK[seq][:, :, :], in0=PF.rearrange("p (a b) -> p a b", b=8), in1=CARRY[seq][:, :, :], op=ALU.add),
             reads=[r_cumtmp, r_cum[seq]], writes=[r_cum[seq]])

    cum(0, NB)
    for sbi in range(SB):
        cum(1 + sbi, NBS)


    def phase_b():
        apos[0] = APOS_AB
        IKTs = aview([128, SEQ], BF16); r_IKT = Res("IKTs")
        KT = [aview([128, SEQ], BF16) for _ in range(2)]; r_KT = [Res("KT%d" % i) for i in range(2)]
        VA = [aview([128, NB, 256], BF16) for _ in range(2)]; r_VA = [Res("VA%d" % i) for i in range(2)]
        SC = aview([128, SEQ], F32); r_SC = Res("SC")
        MQ = aview([128, SEQ], BF16); r_MQ = Res("MQ")
        rtmp = [aview([128, 512], BF16) for _ in range(2)]; r_rtmp = [Res("rtmp%d" % i) for i in range(2)]
        Dw = aview([128, 8, 128], BF16); r_Dw = Res("Dw")
        maskTs = [aview([128, NB, 256], BF16) for _ in range(2)]; r_maskTs = [Res("maskT%d" % i) for i in range(2)]
        NPT = 4
        PTs = [aview([128, 512], BF16) for _ in range(NPT)]; r_PTs = [Res("PTs%d" % i) for i in range(NPT)]
        Qpad = [aview([128, 512], BF16) for _ in range(2)]; r_Qpad = [Res("Qpad%d" % i) for i in range(2)]
        rd = [aview([128, 256], F32) for _ in range(2)]; r_rd = [Res("rd%d" % i) for i in range(2)]
        fmask = aview([128, 2, 4, 256], BF16); smaskb = aview([128, 64], BF16); pencap = aview([128, 2, 2, 512], F32)
        r_msk = Res("masks")
        BIASs = [aview([128, NB, 8], F32) for _ in range(2)]; r_BIASs = [Res("BIAS%d" % i) for i in range(2)]
        bst = aview([128, 64], F32); r_bst = Res("bst")
        pow2 = aview([128, NITER], F32); r_pow2 = Res("pow2")
        mixst = [aview([128, 8, 256], BF16) for _ in range(1)]; r_mixst = [Res("mixst%d" % i) for i in range(1)]
        d_ikt = dsem(); d_kv = [dsem(), dsem()]; d_m = dsem(); d_mx = [dsem(), dsem()]

        P.dma("pool", d_m, fmask[:, :, :, :].rearrange("p a b c -> p (a b c)"), fmask_d[:, :], writes=[r_msk])
        P.dma("pool", d_m, smaskb[:, :], smask_d[:, :], writes=[r_msk])
        P.dma("sp", d_m, pencap[:, :, :, :].rearrange("p a b c -> p (a b c)"), pencap_d[:, :], writes=[r_msk])
        for k in range(NITER):
            P.op("pool", lambda e, k=k: e.memset(pow2[:, k:k + 1], 2.0 ** (-(k + 1))), writes=[r_pow2])
        rmax = bst[:, 0:1]; rmin = bst[:, 1:2]; tcur = bst[:, 2:3]; cntc = bst[:, 3:4]; sg = bst[:, 4:5]; Rr = bst[:, 5:6]
        stp = bst[:, 8:8 + NITER]
        cnts = {"kv": 0, "ix": 0, "pt": 0, "o": 0, "mx": 0, "mk": 0, "rk": 0}

        def qtile(seq, nkb, nq, qoff, slots, cref, jpar, tokoff, tileno, ikt_load=None):
            L = nkb * 128
            prompt = (seq == 0)
            tb_ = tileno % 2
            maskT = maskTs[tb_]; r_maskT = r_maskTs[tb_]; BIAS = BIASs[tb_]; r_BIAS = r_BIASs[tb_]

            def idx():
                yield "u"
                if ikt_load is not None:
                    ikt_load()
                P.op("dve", lambda e: e.tensor_tensor(out=BIAS[:, 0:nkb, :], in0=CARRY[seq][:, cref:cref + 1, :].to_broadcast([128, nkb, 8]),
                                                      in1=CK[seq][:, 0:nkb, :], op=ALU.subtract),
                     reads=[r_cum[seq]], writes=[r_BIAS])
                for si, (nr, iwidx) in enumerate(slots):
                    qc0 = qoff + si * 128
                    for h in range(8):
                        P.op("dve", lambda e, h=h: e.tensor_scalar(out=Dw[0:nr, h, 0:nr], in0=identb[0:nr, 0:nr], scalar1=IW[0:nr, iwidx, h:h + 1],
                                                                   scalar2=None, op0=ALU.mult), reads=[r_identb, r_Q], writes=[r_Dw])
                    for lc in range(0, L, 512):
                        lw = min(512, L - lc)
                        for h in range(8):
                            yield "u"
                            hp = h // 2; po = (h % 2) * 64
                            k = cnts["ix"] % 2; cnts["ix"] += 1
                            P.op("pe", lambda e, hp=hp, po=po, lc=lc, lw=lw: e.matmul(
                                out=Bk[4][0:nr, 0:lw], lhsT=IQT[po:po + 32, hp, qc0:qc0 + nr], rhs=IKTs[po:po + 32, lc:lc + lw],
                                start=True, stop=True), reads=[r_Q, r_IKT], writes=[rB[4]])
                            P.op("act", lambda e, k=k, lw=lw: e.activation(out=rtmp[k][0:nr, 0:lw], in_=Bk[4][0:nr, 0:lw], func=AF.Relu),
                                 reads=[rB[4]], writes=[r_rtmp[k]])
                            P.op("pe", lambda e, k=k, h=h, lw=lw: e.matmul(out=Bk[5][0:nr, 0:lw], lhsT=Dw[0:nr, h, 0:nr], rhs=rtmp[k][0:nr, 0:lw],
                                                                          start=(h == 0), stop=(h == 7)),
                                 reads=[r_Dw, r_rtmp[k]], writes=[rB[5]])
                        P.op("act", lambda e, lc=lc, lw=lw: e.copy(out=SC[0:nr, lc:lc + lw], in_=Bk[5][0:nr, 0:lw]), reads=[rB[5]], writes=[r_SC])
                    P.op("dve", lambda e: e.tensor_reduce(out=rmax[0:nr], in_=SC[0:nr, 0:L], axis=AX.X, op=ALU.max), reads=[r_SC], writes=[r_bst])
                    P.op("dve", lambda e: e.tensor_reduce(out=rmin[0:nr], in_=SC[0:nr, 0:L], axis=AX.X, op=ALU.min), reads=[r_SC], writes=[r_bst])
                    if prompt:
                        P.op("dve", lambda e, si=si: e.tensor_tensor(out=SC[0:nr, L - 512:L], in0=SC[0:nr, L - 512:L], in1=pencap[0:nr, jpar, si, :], op=ALU.min),
                             reads=[r_SC, r_msk], writes=[r_SC])
                    else:
                        P.op("dve", lambda e: e.memset(SC[0:nr, PAST + T:L], NEG), reads=[r_SC], writes=[r_SC])
                    P.op("dve", lambda e: e.scalar_tensor_tensor(out=Rr[0:nr], in0=rmax[0:nr], scalar=2.0, in1=rmin[0:nr], op0=ALU.add, op1=ALU.subtract),
                         reads=[r_bst], writes=[r_bst])
                    P.op("dve", lambda e: e.scalar_tensor_tensor(out=tcur[0:nr], in0=rmax[0:nr], scalar=0.5, in1=rmin[0:nr], op0=ALU.mult, op1=ALU.add),
                         reads=[r_bst], writes=[r_bst])
                    P.op("dve", lambda e: e.scalar_tensor_tensor(out=tcur[0:nr], in0=rmin[0:nr], scalar=-0.5, in1=tcur[0:nr], op0=ALU.mult, op1=ALU.add),
                         reads=[r_bst], writes=[r_bst])
                    P.op("dve", lambda e: e.tensor_scalar(out=stp[0:nr, :], in0=pow2[0:nr, :], scalar1=Rr[0:nr], scalar2=None, op0=ALU.mult),
                         reads=[r_bst, r_pow2], writes=[r_bst])
                    for it in range(NITER):
                        yield "u"
                        P.op("dve", lambda e: e.tensor_scalar(out=MQ[0:nr, 0:L], in0=SC[0:nr, 0:L], scalar1=tcur[0:nr], scalar2=0.0,
                                                              op0=ALU.is_ge, op1=ALU.add, accum_out=cntc[0:nr]),
                             reads=[r_SC, r_bst], writes=[r_MQ, r_bst])
                        P.op("dve", lambda e: e.tensor_scalar(out=sg[0:nr], in0=cntc[0:nr], scalar1=255.5, scalar2=0.5, op0=ALU.is_ge, op1=ALU.subtract),
                             reads=[r_bst], writes=[r_bst])
                        P.op("dve", lambda e, it=it: e.scalar_tensor_tensor(out=tcur[0:nr], in0=sg[0:nr], scalar=stp[0:nr, it:it + 1], in1=tcur[0:nr],
                                                                            op0=ALU.mult, op1=ALU.add), reads=[r_bst], writes=[r_bst])
                    P.op("dve", lambda e: e.tensor_scalar(out=MQ[0:nr, 0:L], in0=SC[0:nr, 0:L], scalar1=tcur[0:nr], scalar2=None, op0=ALU.is_ge),
                         reads=[r_SC, r_bst], writes=[r_MQ])
                    for kb0 in range(0, nkb, 8):
                        yield "T"
                        g = min(8, nkb - kb0)
                        for kk in range(g):
                            kb = kb0 + kk
                            P.op("pe", lambda e, kb=kb, kk=kk: e.transpose(out=PT[:, 1024 + kk * 128:1024 + kk * 128 + nr], in_=MQ[0:nr, kb * 128:(kb + 1) * 128],
                                                                           identity=identb[0:nr, 0:nr]), reads=[r_MQ, r_identb], writes=[rB[7]])
                        src = PT[:, 1024:1024 + g * 128].rearrange("p (a b) -> p a b", b=128)[:, :, 0:nr]
                        dst = maskT[:, kb0:kb0 + g, si * 128:si * 128 + nr]
                        P.op("act", lambda e, src=src, dst=dst: e.activation(out=dst, in_=src, func=AF.Identity, bias=-30000.0, scale=30000.0),
                             reads=[rB[7]], writes=[r_maskT])

            def attn():
                mxs = 0
                sbanks = (0, 1, 6)
                pairs = [(tp, hp) for tp in range(2) for hp in range(4)]
                pair_kv = {}
                n2 = 2 * nq

                def load_pair(pi_):
                    tp, hp = pairs[pi_]
                    kv = cnts["kv"] % 2; cnts["kv"] += 1
                    pair_kv[pi_] = kv
                    P.dma("sp", d_kv[kv], KT[kv][:, 0:L], s_kt[seq][tp * 4 + hp, :, 0:L], reads=[r_scr[seq]], writes=[r_KT[kv]])
                    P.dma("sp", d_kv[kv], VA[kv][:, 0:nkb, :], s_v[seq][:, 0:nkb, tp * 1024 + hp * 256:tp * 1024 + (hp + 1) * 256],
                          reads=[r_scr[seq]], writes=[r_VA[kv]])
                    Qsrc = QTf if tp == 0 else QTd
                    qp = Qpad[kv]
                    P.op("pool", lambda e: e.memset(qp[:, 0:n2], 0.0), writes=[r_Qpad[kv]])
                    P.op("pool", lambda e: e.tensor_copy(out=qp[0:64, 0:nq], in_=Qsrc[0:64, hp, qoff:qoff + nq]), reads=[r_Q], writes=[r_Qpad[kv]])
                    P.op("pool", lambda e: e.tensor_copy(out=qp[64:128, nq:n2], in_=Qsrc[64:128, hp, qoff:qoff + nq]), reads=[r_Q], writes=[r_Qpad[kv]])

                steps = []
                for pi_, (tp, hp) in enumerate(pairs):
                    ob = 2 + (cnts["o"] % 2); cnts["o"] += 1
                    for kb in range(nkb):
                        steps.append((pi_, tp, hp, kb, ob))

                def s_step(st_, si_):
                    pi_, tp, hp, kb, ob = st_
                    kv = pair_kv[pi_]
                    sb_ = sbanks[si_ % 3]
                    pi = cnts["pt"] % NPT; cnts["pt"] += 1
                    P.op("pe", lambda e: e.matmul(out=Bk[sb_][:, 0:n2], lhsT=KT[kv][:, kb * 128:(kb + 1) * 128],
                                                  rhs=Qpad[kv][:, 0:n2], start=True, stop=(tp == 0), skip_group_check=True),
                         reads=[r_KT[kv], r_Qpad[kv]], writes=[rB[sb_]])
                    if tp == 1:
                        for ho in range(2):
                            P.op("pe", lambda e, ho=ho: e.matmul(out=Bk[sb_][:, ho * nq:(ho + 1) * nq], lhsT=identb[:, :], rhs=maskT[:, kb, 0:nq],
                                                                 start=False, stop=(ho == 1), skip_group_check=True),
                                 reads=[r_identb, r_maskT], writes=[rB[sb_]])
                    return pi, sb_

                def e_step(st_, pi, sb_):
                    pi_, tp, hp, kb, ob = st_
                    if tp == 0:
                        for ho in range(2):
                            h = 2 * hp + ho
                            P.op("act", lambda e, ho=ho, h=h: e.activation(out=PTs[pi][:, ho * nq:(ho + 1) * nq], in_=Bk[sb_][:, ho * nq:(ho + 1) * nq],
                                                                        func=AF.Exp, bias=BIAS[:, kb, h:h + 1], scale=0.125),
                                 reads=[rB[sb_], r_BIAS], writes=[r_PTs[pi]])
                    else:
                        P.op("act", lambda e: e.activation(out=PTs[pi][:, 0:n2], in_=Bk[sb_][:, 0:n2], func=AF.Exp, scale=0.125),
                             reads=[rB[sb_]], writes=[r_PTs[pi]])
                    mk = None
                    if tp == 1:
                        pass
                    elif prompt and kb >= nkb - 4:
                        mk = fmask[:, jpar, kb - (nkb - 4), :]; rmk = r_msk
                    elif (not prompt) and kb == nkb - 1:
                        mk = smaskb[:, :]; rmk = r_msk
                    if mk is not None:
                        pv_ = PTs[pi][:, 0:n2].rearrange("p (a b) -> p a b", b=nq)
                        mkb = mk.unsqueeze(1).to_broadcast([128, 2, nq])
                        P.op("dve", lambda e: e.tensor_tensor(out=pv_, in0=pv_, in1=mkb, op=ALU.mult),
                             reads=[r_PTs[pi], rmk], writes=[r_PTs[pi]])

                def pv_step(st_, pi):
                    pi_, tp, hp, kb, ob = st_
                    kv = pair_kv[pi_]
                    for ho in range(2):
                        P.op("pe", lambda e, ho=ho: e.matmul(out=Bk[ob][:, ho * nq:(ho + 1) * nq], lhsT=VA[kv][:, kb, ho * 128:(ho + 1) * 128],
                                                             rhs=PTs[pi][:, ho * nq:(ho + 1) * nq], start=(kb == 0 and ho == 0),
                                                             stop=(kb == nkb - 1 and ho == 1), skip_group_check=True),
                             reads=[r_VA[kv], r_PTs[pi]], writes=[rB[ob]])
                    if kb == nkb - 1:
                        for ho in range(2):
                            po = ho * 64
                            rk = cnts["rk"] % 2; cnts["rk"] += 1
                            P.op("dve", lambda e, ho=ho, rk=rk: e.reciprocal(out=rd[rk][0:64, 0:nq], in_=Bk[ob][64:128, ho * nq:(ho + 1) * nq]),
                                 reads=[rB[ob]], writes=[r_rd[rk]])
                            P.op("dve", lambda e, ho=ho, rk=rk: e.tensor_tensor(out=rd[rk][0:64, 0:nq], in0=Bk[ob][0:64, ho * nq:(ho + 1) * nq],
                                                                                in1=rd[rk][0:64, 0:nq], op=ALU.mult),
                                 reads=[rB[ob], r_rd[rk]], writes=[r_rd[rk]])
                            P.op("act", lambda e, po=po, rk=rk: e.copy(out=mixst[mxs][po:po + 64, tp * 4 + hp, 0:nq], in_=rd[rk][0:64, 0:nq]),
                                 reads=[r_rd[rk]], writes=[r_mixst[mxs]])

                SKEW = 3
                pend = []
                load_pair(0)
                for si_, st_ in enumerate(steps):
                    if si_ > 0 and st_[1] != steps[si_ - 1][1]:
                        while pend:
                            pv_step(*pend.pop(0))
                        yield "drained"
                    pi, sb_ = s_step(st_, si_)
                    if len(pend) >= SKEW:
                        pv_step(*pend.pop(0))
                    e_step(st_, pi, sb_)
                    pend.append((st_, pi))
                    if st_[3] == SKEW and st_[0] + 1 < len(pairs):
                        load_pair(st_[0] + 1)
                    yield st_[1]
                while pend:
                    pv_step(*pend.pop(0))
                P.dma("sp", d_mx[mxs], s_mix.rearrange("c p t -> p c t")[:, :, tokoff:tokoff + nq], mixst[mxs][:, :, 0:nq],
                      reads=[r_mixst[mxs]], writes=[r_smix])
                yield "drained"

            return idx, attn

        def ikt_loader(seq_, ncol):
            def f():
                P.dma("sp", d_ikt, IKTs[:, 0:ncol], s_ikt[seq_][:, :], reads=[r_scr[seq_]], writes=[r_IKT])
            return f
        tiles = []
        tiles_nkb = [4 * j + 4 for j in range(NQT)] + [NBS] * SB
        def _units(nkb_, nslots):
            L_ = nkb_ * 128
            return 1 + nslots * (((L_ + 511) // 512) * 8 + NITER + (nkb_ + 7) // 8)
        tiles_units = [_units(4 * j + 4, 2) for j in range(NQT)] + [_units(NBS, 1)] * SB
        for j in range(NQT):
            tiles.append(qtile(0, 4 * j + 4, 256, j * 256, [(128, 2 * j), (128, 2 * j + 1)], 4 * j, j % 2, j * 256, len(tiles),
                               ikt_loader(0, SEQ) if j == 0 else None))
        for sbi in range(SB):
            tiles.append(qtile(1 + sbi, NBS, T, NOWN * 128 + sbi * T, [(T, NOWN + sbi)], 8, 0, NOWN * 128 + sbi * T, len(tiles),
                               ikt_loader(1 + sbi, LS)))
        class _Idx:
            def __init__(self, gen):
                self.g = gen
                self.up = next(gen)

            def run_one(self):
                try:
                    self.up = next(self.g)
                except StopIteration:
                    self.up = None

            def finish(self):
                while self.up is not None:
                    self.run_one()

        _Idx(tiles[0][0]()).finish()
        for n in range(len(tiles)):
            ag = tiles[n][1]()
            ig = _Idx(tiles[n + 1][0]()) if n + 1 < len(tiles) else None
            credit = 0.0
            nsteps = 8 * (tiles_nkb[n])
            niu = tiles_units[n + 1] if ig is not None else 0
            wsum = 1.5 * nsteps
            for ev in ag:
                if ig is None:
                    continue
                if ev == "drained":
                    while ig.up == "T":
                        ig.run_one()
                    continue
                credit += niu * (2.0 if ev == 0 else 1.0) / wsum
                while credit >= 1.0 and ig.up == "u":
                    credit -= 1.0
                    ig.run_one()
            if ig is not None:
                ig.finish()


    def phase_c():
        apos[0] = 0
        WO = aview([128, 8, D], BF16); WG = aview([128, 8, D], BF16); WPLE = aview([128, 2, D], BF16)
        r_WC = Res("WC")
        vecC = aview([128, 5, D], F32); r_vecC = Res("vecC")
        WUPc = [aview([128, 8, 512], BF16) for _ in range(2)]; r_WUP = [Res("WUP%d" % i) for i in range(2)]
        WDNc = [aview([128, 4, D], BF16) for _ in range(2)]; r_WDN = [Res("WDN%d" % i) for i in range(2)]
        hidT = aview([128, 32, 512], BF16); r_hid = Res("hidT")
        x1 = aview([128, 4, D], F32); r_x1 = [Res("x1_%d" % i) for i in range(4)]
        xT1 = aview([128, 8, 512], BF16); r_xT1 = Res("xT1")
        mixg = aview([128, 8, 512], BF16); r_mixg = Res("mixg")
        xres = [aview([128, D], F32) for _ in range(2)]; r_xres = [Res("xres%d" % i) for i in range(2)]
        pet = [aview([128, 256], F32) for _ in range(2)]; r_pet = [Res("pet%d" % i) for i in range(2)]
        peT = [aview([128, 2, 128], BF16) for _ in range(4)]; r_peT = [Res("peT%d" % i) for i in range(4)]
        tA = [aview([128, D], F32) for _ in range(4)]; r_tA = [Res("tA%d" % i) for i in range(4)]
        rl = [aview([128, 512], F32) for _ in range(2)]; r_rl = [Res("rl%d" % i) for i in range(2)]
        st = aview([128, 32], F32); r_st = Res("st")
        d_wc = dsem(); d_wu = [dsem(), dsem(), dsem()]; d_wd = [dsem(), dsem(), dsem()]; d_mg = dsem(); d_xr = [dsem(), dsem()]; d_y = [dsem(), dsem()]; d_pe = [dsem(), dsem()]
        P.dma("pool", d_wc, WO[:, :, :], wo.rearrange("(k p) n -> p k n", p=128), writes=[r_WC])
        P.dma("pool", d_wc, WG[:, :, :], wg.rearrange("(k p) n -> p k n", p=128), writes=[r_WC])
        P.dma("pool", d_wc, WPLE[:, :, :], wple.rearrange("(k p) n -> p k n", p=128), writes=[r_WC])
        P.dma("sp", d_wc, vecC[:, :, :].rearrange("p a b -> p (a b)"), vecs[:, 8:8 + 5 * D], writes=[r_vecC])
        g1 = vecC[:, 0, :]; b1 = vecC[:, 1, :]; g2 = vecC[:, 2, :]; b2 = vecC[:, 3, :]; bg = vecC[:, 4, :]
        wupv = wup.rearrange("(k p) f -> p k f", p=128)
        wdnv = wdn.rearrange("(c p) n -> p c n", p=128)
        cn = {"xr": 0, "wu": 0, "wd": 0}

        def layer_norm(xa, rxa, gam, bet):
            stats = st[:, 0:12].rearrange("p (a b) -> p a b", b=6); mv = st[:, 12:14]; rstd = st[:, 14:15]
            for c2 in range(2):
                P.op("dve", lambda e, c2=c2: e.bn_stats(out=stats[:, c2, :], in_=xa[:, c2 * 512:(c2 + 1) * 512]), reads=[rxa], writes=[r_st])
            P.op("dve", lambda e: e.bn_aggr(out=mv, in_=st[:, 0:12]), reads=[r_st], writes=[r_st])
            P.op("act", lambda e: e.activation(out=rstd, in_=mv[:, 1:2], func=AF.Ln, bias=1e-5), reads=[r_st], writes=[r_st])
            P.op("act", lambda e: e.activation(out=rstd, in_=rstd, func=AF.Exp, scale=-0.5), reads=[r_st], writes=[r_st])
            P.op("dve", lambda e: e.tensor_scalar(out=xa, in0=xa, scalar1=mv[:, 0:1], scalar2=rstd, op0=ALU.subtract, op1=ALU.mult),
                 reads=[rxa, r_st], writes=[rxa])
            P.op("dve", lambda e: e.tensor_tensor(out=xa, in0=xa, in1=gam, op=ALU.mult), reads=[rxa, r_vecC], writes=[rxa])
            P.op("dve", lambda e: e.tensor_tensor(out=xa, in0=xa, in1=bet, op=ALU.add), reads=[rxa, r_vecC], writes=[rxa])

        def transpose_to(xa, rxa, tb, banks):
            for half in range(2):
                bk = banks[half]
                for kc in range(4):
                    c = half * 4 + kc
                    P.op("pe", lambda e, c=c, kc=kc, bk=bk: e.transpose(out=Bk[bk][:, kc * 128:(kc + 1) * 128], in_=xa[:, c * 128:(c + 1) * 128], identity=identf),
                         reads=[rxa, r_cst], writes=[rB[bk]])
                dst = xT1[:, half * 4:(half + 1) * 4, tb * 128:(tb + 1) * 128]
                src = Bk[bk].rearrange("p (a b) -> p a b", b=128)
                if half == 0:
                    P.op("act", lambda e, dst=dst, src=src: e.copy(out=dst, in_=src), reads=[rB[bk]], writes=[r_xT1])
                else:
                    P.op("dve", lambda e, dst=dst, src=src: e.tensor_copy(out=dst, in_=src), reads=[rB[bk]], writes=[r_xT1])

        def group(G, tok0, xsrc, psrc, ydst, row0):
            ntok = G * 128
            P.dma("sp", d_mg, mixg[:, :, 0:ntok], s_mix.rearrange("c p t -> p c t")[:, :, tok0:tok0 + ntok], reads=[r_smix], writes=[r_mixg])
            for tb in range(G):
                xr = cn["xr"] % 2; cn["xr"] += 1
                rows = slice(row0 + tb * 128, row0 + (tb + 1) * 128)
                P.dma("sp", d_xr[xr], xres[xr][:, :], xsrc[rows, :], writes=[r_xres[xr]])
                xa = x1[:, tb, :]
                for cc in range(2):
                    bk = tb * 2 + cc
                    for kc in range(8):
                        P.op("pe", lambda e, kc=kc, cc=cc, bk=bk: e.matmul(out=Bk[bk][:, 0:512], lhsT=mixg[:, kc, tb * 128:(tb + 1) * 128],
                                                                           rhs=WO[:, kc, cc * 512:(cc + 1) * 512], start=(kc == 0), stop=(kc == 7)),
                             reads=[r_mixg, r_WC], writes=[rB[bk]])
                    P.op("dve", lambda e, cc=cc, bk=bk: e.scalar_tensor_tensor(out=xa[:, cc * 512:(cc + 1) * 512], in0=xres[xr][:, cc * 512:(cc + 1) * 512],
                                                                               scalar=ALPHA, in1=Bk[bk][:, 0:512], op0=ALU.mult, op1=ALU.add),
                         reads=[r_xres[xr], rB[bk]], writes=[r_x1[tb]])
            for tb in range(G):
                layer_norm(x1[:, tb, :], r_x1[tb], g1, b1)
            for tb in range(G):
                transpose_to(x1[:, tb, :], r_x1[tb], tb, (tb * 2, tb * 2 + 1))
            for fch in range(32):
                c = fch // 4
                if fch % 4 == 0:
                    wu = cn["wu"] % 2; cn["wu"] += 1
                    P.dma("pool", d_wu[wu], WUPc[wu][:, :, :], wupv[:, :, c * 512:(c + 1) * 512], writes=[r_WUP[wu]])
                bk = fch % 2
                for kc in range(8):
                    P.op("pe", lambda e, kc=kc, fch=fch, bk=bk, wu=wu: e.matmul(out=Bk[bk][:, 0:ntok], lhsT=WUPc[wu][:, kc, (fch % 4) * 128:(fch % 4 + 1) * 128],
                                                                             rhs=xT1[:, kc, 0:ntok], start=(kc == 0), stop=(kc == 7)),
                         reads=[r_WUP[wu], r_xT1], writes=[rB[bk]])
                P.op("act", lambda e, bk=bk: e.activation(out=rl[bk][:, 0:ntok], in_=Bk[bk][:, 0:ntok], func=AF.Relu), reads=[rB[bk]], writes=[r_rl[bk]])
                P.op("dve", lambda e, bk=bk, fch=fch: e.tensor_tensor(out=hidT[:, fch, 0:ntok], in0=rl[bk][:, 0:ntok], in1=rl[bk][:, 0:ntok], op=ALU.mult),
                     reads=[r_rl[bk]], writes=[r_hid])
            for c in range(8):
                wd = cn["wd"] % 2; cn["wd"] += 1
                P.dma("pool", d_wd[wd], WDNc[wd][:, :, :], wdnv[:, c * 4:(c + 1) * 4, :], writes=[r_WDN[wd]])
                for tb in range(G):
                    for cc in range(2):
                        for f4 in range(4):
                            fc = c * 4 + f4
                            P.op("pe", lambda e, tb=tb, cc=cc, f4=f4, fc=fc, wd=wd: e.matmul(
                                out=Bk[tb * 2 + cc][:, 0:512], lhsT=hidT[:, fc, tb * 128:(tb + 1) * 128], rhs=WDNc[wd][:, f4, cc * 512:(cc + 1) * 512],
                                start=(fc == 0), stop=(fc == 31)), reads=[r_hid, r_WDN[wd]], writes=[rB[tb * 2 + cc]])
            for tb in range(G):
                xa = x1[:, tb, :]
                for cc in range(2):
                    P.op("dve", lambda e, cc=cc, tb=tb: e.scalar_tensor_tensor(out=xa[:, cc * 512:(cc + 1) * 512], in0=xa[:, cc * 512:(cc + 1) * 512], scalar=ALPHA,
                                                                               in1=Bk[tb * 2 + cc][:, 0:512], op0=ALU.mult, op1=ALU.add),
                         reads=[r_x1[tb], rB[tb * 2 + cc]], writes=[r_x1[tb]])
            for tb in range(G):
                layer_norm(x1[:, tb, :], r_x1[tb], g2, b2)
            for tb in range(G):
                transpose_to(x1[:, tb, :], r_x1[tb], tb, (tb * 2, tb * 2 + 1))
            for tb in range(G):
                k = tb
                rows = slice(row0 + tb * 128, row0 + (tb + 1) * 128)
                P.dma("sp", d_pe[tb % 2], pet[tb % 2][:, :], psrc[rows, :], writes=[r_pet[tb % 2]])
                for cc in range(2):
                    bk = tb * 2 + cc
                    for kc in range(8):
                        P.op("pe", lambda e, kc=kc, cc=cc, bk=bk, tb=tb: e.matmul(out=Bk[bk][:, 0:512], lhsT=xT1[:, kc, tb * 128:(tb + 1) * 128],
                                                                               rhs=WG[:, kc, cc * 512:(cc + 1) * 512], start=(kc == 0), stop=(kc == 7)),
                             reads=[r_xT1, r_WC], writes=[rB[bk]])
                    P.op("dve", lambda e, cc=cc, bk=bk, k=k: e.tensor_tensor(out=tA[k][:, cc * 512:(cc + 1) * 512], in0=Bk[bk][:, 0:512], in1=bg[:, cc * 512:(cc + 1) * 512], op=ALU.add),
                         reads=[rB[bk], r_vecC], writes=[r_tA[k]])
                for c2 in range(2):
                    P.op("pe", lambda e, c2=c2, tb=tb: e.transpose(out=Bk[tb * 2][:, c2 * 128:(c2 + 1) * 128], in_=pet[tb % 2][:, c2 * 128:(c2 + 1) * 128], identity=identf),
                         reads=[r_pet[tb % 2], r_cst], writes=[rB[tb * 2]])
                P.op("act", lambda e, tb=tb: e.copy(out=peT[tb][:, :, :], in_=Bk[tb * 2][:, 0:256].rearrange("p (a b) -> p a b", b=128)), reads=[rB[tb * 2]], writes=[r_peT[tb]])
            for tb in range(G):
                P.op("act", lambda e, k=tb: e.activation(out=tA[k][:, :], in_=tA[k][:, :], func=AF.Sigmoid), reads=[r_tA[tb]], writes=[r_tA[tb]])
            for tb in range(G):
                k = tb
                xa = x1[:, tb, :]
                rows = slice(row0 + tb * 128, row0 + (tb + 1) * 128)
                for cc in range(2):
                    bk = tb * 2 + cc
                    for c2 in range(2):
                        P.op("pe", lambda e, c2=c2, cc=cc, bk=bk, tb=tb: e.matmul(out=Bk[bk][:, 0:512], lhsT=peT[tb][:, c2, :], rhs=WPLE[:, c2, cc * 512:(cc + 1) * 512],
                                                                                 start=(c2 == 0), stop=(c2 == 1)), reads=[r_peT[tb], r_WC], writes=[rB[bk]])
                    P.op("dve", lambda e, cc=cc, bk=bk, k=k: e.tensor_tensor(out=tA[k][:, cc * 512:(cc + 1) * 512], in0=tA[k][:, cc * 512:(cc + 1) * 512], in1=Bk[bk][:, 0:512], op=ALU.mult),
                         reads=[r_tA[k], rB[bk]], writes=[r_tA[k]])
                P.op("dve", lambda e, k=k: e.tensor_tensor(out=tA[k][:, :], in0=tA[k][:, :], in1=xa, op=ALU.add), reads=[r_tA[k], r_x1[tb]], writes=[r_tA[k]])
                P.dma("sp", d_y[k % 2], ydst[rows, :], tA[k][:, :], reads=[r_tA[k]])

        for g in range(4):
            group(4, g * 512, xown, pown, o_y, g * 512)
        group(2, NOWN * 128, xs, pss, o_ys, 0)

    if STOP >= 2:
        P.barrier()
        phase_b()
    if STOP >= 3:
        P.barrier()
        phase_c()
    return nc, P, es


def _finish(nc, P, es):
    esem = {}
    for e in Prog.ENGS:
        esem[e] = es.enter_context(nc.semaphore("es_" + e))
    with nc.Block() as block:
        P.emit(nc, block, esem)
    es.close()
    return nc


OWN_PAIRS = {0: [0, 3, 4, 7, 8, 11, 12, 15], 1: [1, 2, 5, 6, 9, 10, 13, 14]}
_SPLITS = (512, 1024, 1536, 1544, 2056, 2568, 3080, 3336, 3368)


def _rope_table(pos):
    out = np.zeros((len(pos), 96), np.float32)
    p = np.asarray(pos, np.float32)[:, None]
    for half, o in ((32, 0), (16, 64)):
        inv = (np.float32(10000.0) ** (-np.arange(half, dtype=np.float32) / np.float32(half))).astype(np.float32)
        ang = (p * inv[None, :]).astype(np.float32)
        out[:, o:o + half] = np.cos(ang)
        out[:, o + half:o + 2 * half] = np.sin(ang)
    return out


def _own_blocks(half):
    bl = []
    for p in OWN_PAIRS[half]:
        bl += [2 * p, 2 * p + 1]
    return bl


def _masks(half):
    l = np.arange(128)[:, None]
    q = np.arange(128)[None, :]
    tri = (l <= q).astype(np.float32)
    one = np.ones((128, 128), np.float32)
    zer = np.zeros((128, 128), np.float32)
    MA = np.concatenate([tri, one], 1); MB = np.concatenate([zer, tri], 1)
    ON = np.concatenate([one, one], 1); ZE = np.concatenate([zer, zer], 1)
    fm = np.zeros((128, 2, 4, 256), np.float32)
    pc = np.zeros((128, 2, 2, 512), np.float32)
    for jpar in range(2):
        hi = (OWN_PAIRS[half][jpar] == 2 * jpar + 1)
        tiles = [ON, ON, MA, MB] if hi else [MA, MB, ZE, ZE]
        for i in range(4):
            fm[:, jpar, i, :] = tiles[i]
        for s in range(2):
            qb_rel = (2 if hi else 0) + s
            row = np.arange(128)[:, None]
            col = np.arange(512)[None, :]
            lim = qb_rel * 128 + np.where(row < 64, 64, 128)
            pc[:, jpar, s, :] = np.where(col < lim, BIG, NEG)
    sm = np.zeros((128, 64), np.float32)
    sm[:64, :] = (np.arange(64)[:, None] <= np.arange(64)[None, :]).astype(np.float32)
    return fm.reshape(128, -1), sm, pc.reshape(128, -1)


def _prep(inputs):
    f = lambda a: np.ascontiguousarray(np.asarray(a, dtype=np.float32))
    w_in = f(inputs["w_in"][0])
    fq, fk, fv, fg, dq, dk, dv, iq, ik, iw = np.split(w_in, _SPLITS, axis=1)
    wk = f(np.concatenate([fk, dk, fv, dv, ik, fg], 1))
    wq = f(np.concatenate([fq, dq, iq, iw], 1))
    shared = dict(
        wk=wk, wq=wq, wo=f(inputs["w_o"][0]), wup=f(inputs["w_up"][0]), wdn=f(inputs["w_down"][0]),
        wg=f(inputs["w_ple_gate"][0]), wple=f(inputs["w_ple"][0]),
        vecs=f(np.broadcast_to(np.concatenate([inputs["b_f"][0], inputs["ln1_g"][0], inputs["ln1_b"][0], inputs["ln2_g"][0],
                                               inputs["ln2_b"][0], inputs["b_ple_gate"][0]])[None, :], (128, 8 + 5 * D))),
        ropeall=_rope_table(np.arange(SEQ)),
        ropes=np.concatenate([_rope_table(PAST + np.arange(T)), np.zeros((64, 96), np.float32)], 0),
    )
    ident = np.eye(128, dtype=np.float32)
    U = (np.arange(128)[:, None] <= np.arange(128)[None, :]).astype(np.float32)
    E = np.zeros((128, 128), np.float32); E[127, :] = 1.0
    shared["consts"] = f(np.concatenate([ident, U, E], 1))
    maps = []
    for c in range(8):
        b, half = c // 2, c % 2
        rows = np.concatenate([np.arange(bl * 128, (bl + 1) * 128) for bl in _own_blocks(half)])
        fm, sm, pc = _masks(half)
        m = dict(shared)
        m.update(
            xall=f(inputs["x_prompt"][b]), xown=f(inputs["x_prompt"][b][rows]), pown=f(inputs["p_prompt"][0, b][rows]),
            xs=f(inputs["x_sample"][4 * c:4 * c + 4].reshape(SB * T, D)), pss=f(inputs["p_sample"][0, 4 * c:4 * c + 4].reshape(SB * T, 256)),
            cfk=f(inputs["cache_fox_k"][0, 4 * c:4 * c + 4].reshape(SB, PAST, 512)),
            cfv=f(inputs["cache_fox_v"][0, 4 * c:4 * c + 4].reshape(SB, PAST, 512)),
            clf=f(inputs["cache_fox_logf"][0, 4 * c:4 * c + 4]),
            cdk=f(inputs["cache_dsa_k"][0, 4 * c:4 * c + 4].reshape(SB, PAST, 512)),
            cdv=f(inputs["cache_dsa_v"][0, 4 * c:4 * c + 4].reshape(SB, PAST, 512)),
            cik=f(inputs["cache_idx_k"][0, 4 * c:4 * c + 4]),
            ropeown=f(shared["ropeall"][rows]), fmask=fm, smask=sm, pencap=pc,
        )
        maps.append(m)
    return maps


def kernel(**inputs):
    maps = _prep(inputs)
    nc, P, es = build()
    nc = _finish(nc, P, es)
    res = run_bass_kernel_spmd(nc, maps, core_ids=list(range(8)))
    R = res.results
    y_p = np.zeros((4, SEQ, D), np.float32)
    y_s = np.zeros((32, T, D), np.float32)
    for c in range(8):
        b, half = c // 2, c % 2
        rows = np.concatenate([np.arange(bl * 128, (bl + 1) * 128) for bl in _own_blocks(half)])
        y_p[b, rows] = R[c]["o_y"]
        y_s[4 * c:4 * c + 4] = R[c]["o_ys"].reshape(SB, T, D)

    def pk(name, last):
        return np.stack([R[2 * b][name] for b in range(4)], 0).reshape((1, 4, SEQ) + last)

    def sk(name, last):
        return np.concatenate([R[c][name].reshape((SB, T) + last) for c in range(8)], 0).reshape((1, 32, T) + last)

    return (y_p, y_s, pk("o_fk", (8, 64)), pk("o_fv", (8, 64)), pk("o_lf", (8,)), pk("o_dk", (8, 64)), pk("o_dv", (8, 64)),
            pk("o_ik", (32,)), sk("o_sfk", (8, 64)), sk("o_sfv", (8, 64)), sk("o_slf", (8,)), sk("o_sdk", (8, 64)),
            sk("o_sdv", (8, 64)), sk("o_sik", (32,)))
```

```python
import os
import types
import contextlib
import numpy as np
import concourse.bass as bass
import concourse.mybir as mybir
from concourse.bass_utils import run_bass_kernel_spmd

F32 = mybir.dt.float32
BF16 = mybir.dt.bfloat16
ALU = mybir.AluOpType
AF = mybir.ActivationFunctionType
AX = mybir.AxisListType

D = 1024
SEQ = 4096
NB = 32
NOWN = 16
NQT = 8
SB = 4
PAST = 1024
T = 64
NBS = 9
LS = NBS * 128
KC = 2088
QC = 1288
NEG = -1.0e30
BIG = 1.0e30
NITER = 16
ALPHA = 2.0 ** 0.25
STOP = 99


class Res:
    __slots__ = ("name", "w", "rs")

    def __init__(self, name):
        self.name = name
        self.w = None
        self.rs = []


class DSem:
    __slots__ = ("h", "total", "mk", "sub")

    def __init__(self, h=None, mk=None):
        self.h = h
        self.total = 0
        self.mk = mk
        self.sub = {}

    def for_queue(self, eng, prog):
        k = "sw" if eng == "pool" else "hw"
        if k not in self.sub:
            d = DSem(self.mk(), None)
            prog.dsems.append(d)
            self.sub[k] = d
        return self.sub[k]


class Op:
    __slots__ = ("eng", "fn", "waits", "needed", "val", "dsem")

    def __init__(self, eng, fn):
        self.eng = eng
        self.fn = fn
        self.waits = []
        self.needed = False
        self.val = 0
        self.dsem = None


def _freeze(fn):
    if fn.__closure__ is None:
        return fn
    cells = []
    for c in fn.__closure__:
        try:
            cells.append(types.CellType(c.cell_contents))
        except ValueError:
            cells.append(c)
    return types.FunctionType(fn.__code__, fn.__globals__, fn.__name__, fn.__defaults__, tuple(cells))


class Prog:
    ENGS = ("pe", "act", "dve", "pool", "sp")

    def __init__(self):
        self.q = {e: [] for e in self.ENGS}
        self.dsems = []
        self.pending = {e: None for e in self.ENGS}

    def new_dsem(self, mk):
        return DSem(None, mk)

    def _mk(self, eng, fn, reads, writes):
        o = Op(eng, _freeze(fn))
        deps = []
        for r in reads:
            if r.w is not None:
                deps.append(r.w)
        for r in writes:
            if r.w is not None:
                deps.append(r.w)
            deps.extend(r.rs)
        for d in deps:
            if d.dsem is not None:
                o.waits.append((d.dsem, d.dsem.total))
            else:
                if d.eng == eng and eng == "pe":
                    continue
                d.needed = True
                o.waits.append(d)
        pb = self.pending[eng]
        if pb is not None:
            o.waits.extend(pb)
            self.pending[eng] = None
        for r in reads:
            r.rs.append(o)
        for r in writes:
            r.w = o
            r.rs = []
        self.q[eng].append(o)
        return o

    def op(self, eng, fn, reads=(), writes=()):
        return self._mk(eng, fn, reads, writes)

    def dma(self, eng, dsem, out, in_, reads=(), writes=()):
        o = self._mk(eng, lambda e: e.dma_start(out=out, in_=in_), reads, writes)
        sub = dsem.for_queue(eng, self)
        o.dsem = sub
        sub.total += 16
        return o

    def barrier(self):
        snap = []
        for e in self.ENGS:
            for o in reversed(self.q[e]):
                if o.dsem is None:
                    o.needed = True
                    snap.append(o)
                    break
        for d in self.dsems:
            if d.total:
                snap.append((d, d.total))
        for e in self.ENGS:
            self.pending[e] = list(snap) + (self.pending[e] or [])

    def emit(self, nc, block, esem):
        for e in self.ENGS:
            c = 0
            for o in self.q[e]:
                if o.dsem is None and o.needed:
                    c += 1
                    o.val = c
        engobj = {"pe": "tensor", "act": "scalar", "dve": "vector", "pool": "gpsimd", "sp": "sync"}
        final = [(d, d.total) for d in self.dsems if d.total]

        def run(ename, eng):
            waited = {}
            for o in self.q[ename]:
                for w in o.waits:
                    if isinstance(w, tuple):
                        sem, val = w[0].h, w[1]
                    else:
                        sem, val = esem[w.eng], w.val
                    key = id(sem)
                    if waited.get(key, 0) >= val:
                        continue
                    waited[key] = val
                    eng.wait_ge(sem, val)
                ins = o.fn(eng)
                if o.dsem is not None:
                    ins.then_inc(o.dsem.h, 16)
                elif o.needed:
                    ins.then_inc(esem[ename], 1)
            if ename == "sp":
                for d, tot in final:
                    eng.wait_ge(d.h, tot)

        for ename in self.ENGS:
            getattr(block, engobj[ename])(lambda eng, _n=ename: run(_n, eng))


def build():
    nc = bass.Bass("TRN2", target_bir_lowering=False)
    es = contextlib.ExitStack()
    P = Prog()

    def din(name, shape):
        return nc.dram_tensor(name, list(shape), F32, kind="ExternalInput").ap()

    def dout(name, shape):
        return nc.dram_tensor(name, list(shape), F32, kind="ExternalOutput").ap()

    def dscr(name, shape, dt=BF16):
        return nc.dram_tensor(name, list(shape), dt).ap()

    def sbt(name, shape, dt):
        return es.enter_context(nc.sbuf_tensor(name, list(shape), dt))

    nsem = [0]

    def _mksem():
        nsem[0] += 1
        return es.enter_context(nc.semaphore("ds%d" % nsem[0]))

    def dsem():
        return P.new_dsem(_mksem)

    xall = din("xall", [SEQ, D]); xown = din("xown", [NOWN * 128, D]); pown = din("pown", [NOWN * 128, 256])
    xs = din("xs", [SB * T, D]); pss = din("pss", [SB * T, 256])
    cfk = din("cfk", [SB, PAST, 512]); cfv = din("cfv", [SB, PAST, 512]); clf = din("clf", [SB, PAST, 8])
    cdk = din("cdk", [SB, PAST, 512]); cdv = din("cdv", [SB, PAST, 512]); cik = din("cik", [SB, PAST, 32])
    wk = din("wk", [D, KC]); wq = din("wq", [D, QC]); wo = din("wo", [D, D]); wup = din("wup", [D, 4 * D])
    wdn = din("wdn", [4 * D, D]); wg = din("wg", [D, D]); wple = din("wple", [256, D])
    vecs = din("vecs", [128, 8 + 5 * D])
    ropeall = din("ropeall", [SEQ, 96]); ropeown = din("ropeown", [NOWN * 128, 96]); ropes = din("ropes", [128, 96])
    consts = din("consts", [128, 384])
    fmask_d = din("fmask", [128, 2 * 4 * 256]); smask_d = din("smask", [128, 64]); pencap_d = din("pencap", [128, 2 * 2 * 512])

    o_y = dout("o_y", [NOWN * 128, D]); o_ys = dout("o_ys", [SB * T, D])
    o_fk = dout("o_fk", [SEQ, 512]); o_fv = dout("o_fv", [SEQ, 512]); o_lf = dout("o_lf", [SEQ, 8])
    o_dk = dout("o_dk", [SEQ, 512]); o_dv = dout("o_dv", [SEQ, 512]); o_ik = dout("o_ik", [SEQ, 32])
    o_sfk = dout("o_sfk", [SB * T, 512]); o_sfv = dout("o_sfv", [SB * T, 512]); o_slf = dout("o_slf", [SB * T, 8])
    o_sdk = dout("o_sdk", [SB * T, 512]); o_sdv = dout("o_sdv", [SB * T, 512]); o_sik = dout("o_sik", [SB * T, 32])

    LMAX = SEQ
    s_kt = [dscr("s_kt%d" % i, [9, 128, SEQ if i == 0 else LS]) for i in range(1 + SB)]
    s_v = [dscr("s_v%d" % i, [128, NB if i == 0 else NBS, 2048]) for i in range(1 + SB)]
    s_ikt = [a_[8] for a_ in s_kt]
    NTOK = NOWN * 128 + SB * T
    s_mix = dscr("s_mix", [8, 128, NTOK])
    r_scr = [Res("scr%d" % i) for i in range(1 + SB)]
    r_smix = Res("smix")

    Bk = [es.enter_context(nc.psum_tensor("pb%d" % i, [128, 512], F32)) for i in range(6)]
    PTf = es.enter_context(nc.psum_tensor("ptf", [128, 1024], F32))
    PT = PTf[:, :].bitcast(BF16)
    Bk = [b_[:, :] for b_ in Bk] + [PTf[:, 0:512], PTf[:, 512:1024]]
    rB = [Res("pb%d" % i) for i in range(8)]
    rPT = Res("PT")

    cst = sbt("cst", [128, 384], F32)
    identb = sbt("identb", [128, 128], BF16)
    vec = sbt("vec", [128, 8], F32)
    r_cst = Res("cst"); r_identb = Res("identb"); r_vec = Res("vec")
    identf = cst[:, 0:128]; Umat = cst[:, 128:256]; Elast = cst[:, 256:384]
    bfb = vec[:, 0:8]
    LOGF = [sbt("LOGF%d" % i, [128, NB if i == 0 else NBS, 8], F32) for i in range(1 + SB)]
    r_LOGF = [Res("LOGF%d" % i) for i in range(1 + SB)]
    CK = [sbt("CK%d" % i, [128, NB if i == 0 else NBS, 8], F32) for i in range(1 + SB)]
    CARRY = [sbt("CARRY%d" % i, [128, NB if i == 0 else NBS, 8], F32) for i in range(1 + SB)]
    r_cum = [Res("cum%d" % i) for i in range(1 + SB)]
    r_cumtmp = Res("cumtmp")
    ones32 = sbt("ones32", [128, 32], F32); r_ones32 = Res("ones32")
    IW = sbt("IW", [128, NOWN + SB, 8], F32)
    r_Q = Res("Qside")

    ARENA_BYTES = 203648
    arena = sbt("arena", [128, ARENA_BYTES // 2], BF16)
    apos = [0]

    def aview(shape, dt):
        n = 1
        for s_ in shape[1:]:
            n *= s_
        nbytes = n * (4 if dt == F32 else 2)
        nbytes = (nbytes + 63) // 64 * 64
        off = apos[0]
        apos[0] += nbytes
        assert apos[0] <= ARENA_BYTES, ("arena overflow", apos[0])
        a = arena[:, off // 2:(off + n * (4 if dt == F32 else 2)) // 2]
        if dt == F32:
            a = a.bitcast(F32)
        if len(shape) == 3:
            a = a.rearrange("p (a b) -> p a b", b=shape[2])
        elif len(shape) == 4:
            a = a.rearrange("p (a b c) -> p a b c", b=shape[2], c=shape[3])
        return a

    d_init = dsem()

    P.dma("sp", d_init, cst[:], consts[:, :], writes=[r_cst])
    P.dma("sp", d_init, vec[:], vecs[:, 0:8], writes=[r_vec])
    P.dma("pool", d_init, identb[:], consts[:, 0:128], writes=[r_identb])
    P.op("dve", lambda e: e.memset(ones32[:], 1.0), writes=[r_ones32])

    apos[0] = 0
    QTf = aview([128, 4, NTOK], BF16); QTd = aview([128, 4, NTOK], BF16); IQT = aview([128, 4, NTOK], BF16)
    APOS_AB = apos[0]
    WK = aview([128, 8, KC], BF16); WQ = aview([128, 8, QC], BF16)
    r_WK = Res("WK"); r_WQ = Res("WQ")
    xin = [aview([128, D], F32) for _ in range(2)]; r_xin = [Res("xin%d" % i) for i in range(2)]
    xT = [aview([128, 8, 128], BF16) for _ in range(2)]; r_xT = [Res("xT%d" % i) for i in range(2)]
    hb = [aview([128, KC], F32) for _ in range(3)]; r_hb = [Res("hb%d" % i) for i in range(3)]
    kb16 = [aview([128, 1536], BF16) for _ in range(2)]; r_kb16 = [Res("kb16%d" % i) for i in range(2)]
    vb16 = [aview([128, 2048], BF16) for _ in range(2)]; r_vb16 = [Res("vb16%d" % i) for i in range(2)]
    ktst = [aview([128, 1152], BF16) for _ in range(2)]; r_ktst = [Res("ktst%d" % i) for i in range(2)]
    rp = [aview([128, 96], F32) for _ in range(4)]; r_rp = [Res("rp%d" % i) for i in range(4)]
    d_rp = [dsem() for _ in range(4)]
    rt = [aview([128, 512], F32) for _ in range(4)]; r_rt = [Res("rt%d" % i) for i in range(4)]
    lft = [aview([128, 8], F32) for _ in range(2)]; r_lft = Res("lft")
    cumtmp = aview([128, 3, NB * 8], F32)
    d_w = dsem(); d_x = [dsem(), dsem()]; d_hb = [dsem(), dsem(), dsem()]; d_st = [dsem(), dsem()]

    wkv = wk.rearrange("(k p) n -> p k n", p=128)
    wqv = wq.rearrange("(k p) n -> p k n", p=128)
    for k in range(8):
        P.dma("pool", d_w, WK[:, k, :], wkv[:, k, :], writes=[r_WK])
    for k in range(8):
        P.dma("pool", d_w, WQ[:, k, :], wqv[:, k, :], writes=[r_WQ])
    for s_ in range(2):
        P.op("pool", lambda e, s_=s_: e.memset(kb16[s_][:, :], 0.0), writes=[r_kb16[s_]])
        P.op("pool", lambda e, s_=s_: e.memset(vb16[s_][:, :], 1.0), writes=[r_vb16[s_]])

    def load_x(slot, xrows, roperows, n=128, rs_=0):
        if n < 128:
            P.op("pool", lambda e: e.memset(xin[slot][:, :], 0.0), writes=[r_xin[slot]])
        P.dma("sp", d_x[slot], xin[slot][0:n, :], xrows, writes=[r_xin[slot]])
        P.dma("sp", d_rp[rs_], rp[rs_][:, :], roperows, writes=[r_rp[rs_]])

    def transpose_x(slot):
        for half in range(2):
            bk = half
            for kc in range(4):
                c = half * 4 + kc
                P.op("pe", lambda e, c=c, kc=kc, bk=bk: e.transpose(out=Bk[bk][:, kc * 128:(kc + 1) * 128],
                                                              in_=xin[slot][:, c * 128:(c + 1) * 128], identity=identf),
                     reads=[r_xin[slot], r_cst], writes=[rB[bk]])
            dst = xT[slot][:, half * 4:(half + 1) * 4, :]
            src = Bk[bk][:, :].rearrange("p (a b) -> p a b", b=128)
            if half == 0:
                P.op("act", lambda e, dst=dst, src=src: e.copy(out=dst, in_=src), reads=[rB[bk]], writes=[r_xT[slot]])
            else:
                P.op("dve", lambda e, dst=dst, src=src: e.tensor_copy(out=dst, in_=src), reads=[rB[bk]], writes=[r_xT[slot]])

    def project(slot, W, rW, ncols, dst, rdst):
        c0 = 0
        ci = 0
        while c0 < ncols:
            cw = min(512, ncols - c0)
            bk = 2 + (ci % 2)
            for kc in range(8):
                P.op("pe", lambda e, kc=kc, c0=c0, cw=cw, bk=bk: e.matmul(out=Bk[bk][:, 0:cw], lhsT=xT[slot][:, kc, :],
                                                                      rhs=W[:, kc, c0:c0 + cw], start=(kc == 0), stop=(kc == 7)),
                     reads=[r_xT[slot], rW], writes=[rB[bk]])
            if ci % 2 == 0:
                P.op("act", lambda e, c0=c0, cw=cw, bk=bk: e.copy(out=dst[:, c0:c0 + cw], in_=Bk[bk][:, 0:cw]),
                     reads=[rB[bk]], writes=[rdst])
            else:
                P.op("dve", lambda e, c0=c0, cw=cw, bk=bk: e.tensor_copy(out=dst[:, c0:c0 + cw], in_=Bk[bk][:, 0:cw]),
                     reads=[rB[bk]], writes=[rdst])
            c0 += cw
            ci += 1

    def rope(buf, rbuf, H, half, cosap, sinap, rrope):
        v = buf.rearrange("p (h t d) -> p h t d", t=2, d=half)
        x1 = v[:, :, 0, :]; x2 = v[:, :, 1, :]
        cb = cosap.unsqueeze(1).to_broadcast([128, H, half])
        sb_ = sinap.unsqueeze(1).to_broadcast([128, H, half])
        tv = [rt[i][:, 0:H * half].rearrange("p (h d) -> p h d", d=half) for i in range(4)]
        P.op("dve", lambda e: e.tensor_tensor(out=tv[0], in0=x1, in1=cb, op=ALU.mult), reads=[rbuf, rrope], writes=[r_rt[0]])
        P.op("dve", lambda e: e.tensor_tensor(out=tv[1], in0=x2, in1=sb_, op=ALU.mult), reads=[rbuf, rrope], writes=[r_rt[1]])
        P.op("dve", lambda e: e.tensor_tensor(out=tv[2], in0=x2, in1=cb, op=ALU.mult), reads=[rbuf, rrope], writes=[r_rt[2]])
        P.op("dve", lambda e: e.tensor_tensor(out=tv[3], in0=x1, in1=sb_, op=ALU.mult), reads=[rbuf, rrope], writes=[r_rt[3]])
        P.op("dve", lambda e: e.tensor_tensor(out=x1, in0=tv[0], in1=tv[1], op=ALU.subtract), reads=[r_rt[0], r_rt[1]], writes=[rbuf])
        P.op("dve", lambda e: e.tensor_tensor(out=x2, in0=tv[2], in1=tv[3], op=ALU.add), reads=[r_rt[2], r_rt[3]], writes=[rbuf])

    def kside_finish(hs, rs_):
        h_ = hb[hs]
        z = lft[0]; z2 = lft[1]
        P.op("dve", lambda e: e.tensor_tensor(out=z[:, :], in0=h_[:, 2080:2088], in1=bfb, op=ALU.add),
             reads=[r_hb[hs], r_vec], writes=[r_lft])
        P.op("act", lambda e: e.activation(out=z2[:, :], in_=z[:, :], func=AF.Exp, scale=-1.0), reads=[r_lft], writes=[r_lft])
        P.op("act", lambda e: e.activation(out=z[:, :], in_=z2[:, :], func=AF.Ln, bias=1.0), reads=[r_lft], writes=[r_lft])
        P.op("dve", lambda e: e.tensor_scalar(out=h_[:, 2080:2088], in0=z[:, :], scalar1=-1.0, scalar2=None, op0=ALU.mult),
             reads=[r_lft], writes=[r_hb[hs]])
        rope(h_[:, 512:1024], r_hb[hs], 8, 32, rp[rs_][:, 0:32], rp[rs_][:, 32:64], r_rp[rs_])
        rope(h_[:, 2048:2080], r_hb[hs], 1, 16, rp[rs_][:, 64:80], rp[rs_][:, 80:96], r_rp[rs_])

    def kside_post(hs, ks, seq, blk):
        h_ = hb[hs]
        P.op("dve", lambda e: e.tensor_copy(out=LOGF[seq][:, blk, :], in_=h_[:, 2080:2088]), reads=[r_hb[hs]], writes=[r_LOGF[seq]])
        P.op("act", lambda e: e.copy(out=kb16[ks][:, 0:512], in_=h_[:, 0:512]), reads=[r_hb[hs]], writes=[r_kb16[ks]])
        P.op("dve", lambda e: e.tensor_copy(out=kb16[ks][:, 512:1024], in_=h_[:, 512:1024]), reads=[r_hb[hs]], writes=[r_kb16[ks]])
        ikd = kb16[ks][:, 1024:1152].rearrange("p (a b) -> p a b", b=64)[:, :, 0:32]
        iks = h_[:, 2048:2080].unsqueeze(1).to_broadcast([128, 2, 32])
        P.op("dve", lambda e: e.tensor_copy(out=ikd, in_=iks), reads=[r_hb[hs]], writes=[r_kb16[ks]])
        vdst = vb16[ks][:, :].rearrange("p (t h d) -> p t h d", t=2, d=128)[:, :, :, 0:64]
        vsrc_ = h_[:, 1024:2048].rearrange("p (t h d) -> p t h d", t=2, d=64)
        P.op("act", lambda e: e.copy(out=vdst, in_=vsrc_), reads=[r_hb[hs]], writes=[r_vb16[ks]])
        for c in range(9):
            P.op("pe", lambda e, c=c: e.transpose(out=PT[:, c * 128:(c + 1) * 128], in_=kb16[ks][:, c * 128:(c + 1) * 128],
                                                  identity=identb[:, :]),
                 reads=[r_kb16[ks], r_identb], writes=[rPT])
        P.op("act", lambda e: e.copy(out=ktst[ks][:, :], in_=PT[:, 0:1152]), reads=[rPT], writes=[r_ktst[ks]])
        cols = slice(blk * 128, (blk + 1) * 128)
        P.dma("act", d_st[ks], s_kt[seq].rearrange("h p l -> p h l")[:, :, cols],
              ktst[ks][:, 0:1152].rearrange("p (h l) -> p h l", l=128), reads=[r_ktst[ks]], writes=[r_scr[seq]])
        P.dma("act", d_st[ks], s_v[seq][:, blk, :], vb16[ks][:, 0:2048], reads=[r_vb16[ks]], writes=[r_scr[seq]])

    def kside_outputs(hs, outs, rows, n=128):
        ofk, ofv, olf, odk, odv, oik = outs
        h_ = hb[hs]
        for (o_, a, b) in ((ofk, 0, 512), (odk, 512, 1024), (ofv, 1024, 1536), (odv, 1536, 2048), (oik, 2048, 2080), (olf, 2080, 2088)):
            P.dma("pool", d_hb[hs], o_[rows, :], h_[0:n, a:b], reads=[r_hb[hs]])

    def qside_finish(hs, rs_, ks, tokoff, iwidx, nr=128):
        h_ = hb[hs]; rr = r_hb[hs]
        rope(h_[:, 512:1024], rr, 8, 32, rp[rs_][:, 0:32], rp[rs_][:, 32:64], r_rp[rs_])
        rope(h_[:, 1024:1280], rr, 8, 16, rp[rs_][:, 64:80], rp[rs_][:, 80:96], r_rp[rs_])
        P.op("dve", lambda e: e.tensor_scalar(out=IW[:, iwidx, :], in0=h_[:, 1280:1288], scalar1=1.0 / 16.0, scalar2=None, op0=ALU.mult),
             reads=[rr], writes=[r_Q])
        kq = kb16[ks]; rk = r_kb16[ks]
        P.op("act", lambda e: e.copy(out=kq[:, 0:512], in_=h_[:, 0:512]), reads=[rr], writes=[rk])
        P.op("dve", lambda e: e.tensor_copy(out=kq[:, 512:1024], in_=h_[:, 512:1024]), reads=[rr], writes=[rk])
        iqd = kq[:, 1024:1536].rearrange("p (a b) -> p a b", b=64)[:, :, 0:32]
        iqs = h_[:, 1024:1280].rearrange("p (a b) -> p a b", b=32)
        P.op("dve", lambda e: e.tensor_copy(out=iqd, in_=iqs), reads=[rr], writes=[rk])

    def qside_post(ks, tokoff, nr=128):
        kq = kb16[ks]; rk = r_kb16[ks]
        for c in range(12):
            P.op("pe", lambda e, c=c: e.transpose(out=PT[:, c * 128:(c + 1) * 128], in_=kq[:, c * 128:(c + 1) * 128], identity=identb[:, :]),
                 reads=[rk, r_identb], writes=[rPT])
        for i_, (dst, eng) in enumerate(((QTf, "act"), (QTd, "dve"), (IQT, "act"))):
            src = PT[:, i_ * 512:(i_ + 1) * 512].rearrange("p (a b) -> p a b", b=128)[:, :, 0:nr]
            d_ = dst[:, :, tokoff:tokoff + nr]
            if eng == "act":
                P.op("act", lambda e, d_=d_, src=src: e.copy(out=d_, in_=src), reads=[rPT], writes=[r_Q])
            else:
                P.op("dve", lambda e, d_=d_, src=src: e.tensor_copy(out=d_, in_=src), reads=[rPT], writes=[r_Q])

    prompt_outs = (o_fk, o_fv, o_lf, o_dk, o_dv, o_ik)
    sample_outs = (o_sfk, o_sfv, o_slf, o_sdk, o_sdv, o_sik)
    items = []
    nitem = [0]
    rs_cur = [0]

    def slots():
        i = nitem[0]; nitem[0] += 1
        rs_cur[0] = i % 4
        return i % 3, i % 2, i % 2

    for i in range(NB):
        hs, xs_, ks = slots()
        rs_ = rs_cur[0]
        rows = slice(i * 128, (i + 1) * 128)

        def s1(hs=hs, xs_=xs_, rows=rows, rs_=rs_):
            load_x(xs_, xall[rows, :], ropeall[rows, :], rs_=rs_)
            transpose_x(xs_)
            project(xs_, WK, r_WK, KC, hb[hs], r_hb[hs])

        def s2(hs=hs, rs_=rs_, rows=rows):
            kside_finish(hs, rs_)

        def s3(hs=hs, ks=ks, i=i, rows=rows):
            kside_outputs(hs, prompt_outs, rows)
            kside_post(hs, ks, 0, i)
        items.append((s1, s2, s3))
    for i in range(NOWN):
        hs, xs_, ks = slots()
        rs_ = rs_cur[0]
        rows = slice(i * 128, (i + 1) * 128)

        def s1(hs=hs, xs_=xs_, rows=rows, rs_=rs_):
            load_x(xs_, xown[rows, :], ropeown[rows, :], rs_=rs_)
            transpose_x(xs_)
            project(xs_, WQ, r_WQ, QC, hb[hs], r_hb[hs])

        def s2(hs=hs, rs_=rs_, ks=ks, i=i):
            qside_finish(hs, rs_, ks, i * 128, i)

        def s3(ks=ks, i=i):
            qside_post(ks, i * 128)
        items.append((s1, s2, s3))
    for sbi in range(SB):
        seq = 1 + sbi
        for blk in range(8):
            hs, xs_, ks = slots()
            rows = slice(blk * 128, (blk + 1) * 128)

            def s1(hs=hs, sbi=sbi, rows=rows):
                for (src, a, b) in ((cfk, 0, 512), (cdk, 512, 1024), (cfv, 1024, 1536), (cdv, 1536, 2048), (cik, 2048, 2080), (clf, 2080, 2088)):
                    P.dma("sp", d_hb[hs], hb[hs][:, a:b], src[sbi, rows, :], writes=[r_hb[hs]])

            def s3(hs=hs, ks=ks, seq=seq, blk=blk):
                kside_post(hs, ks, seq, blk)
            items.append((s1, None, s3))
        hs, xs_, ks = slots()
        rs_ = rs_cur[0]
        rows = slice(sbi * T, (sbi + 1) * T)

        def s1(hs=hs, xs_=xs_, rows=rows, rs_=rs_):
            load_x(xs_, xs[rows, :], ropes[:, :], n=T, rs_=rs_)
            transpose_x(xs_)
            project(xs_, WK, r_WK, KC, hb[hs], r_hb[hs])

        def s2(hs=hs, rs_=rs_, rows=rows):
            kside_finish(hs, rs_)

        def s3(hs=hs, ks=ks, seq=seq, rows=rows):
            kside_outputs(hs, sample_outs, rows, n=T)
            kside_post(hs, ks, seq, 8)
        items.append((s1, s2, s3))
        xs_k = xs_
        rs_k = rs_
        hs, _unused, ks = slots()

        def s1(hs=hs, xs_k=xs_k):
            project(xs_k, WQ, r_WQ, QC, hb[hs], r_hb[hs])

        def s2(hs=hs, rs_k=rs_k, ks=ks, sbi=sbi):
            qside_finish(hs, rs_k, ks, NOWN * 128 + sbi * T, NOWN + sbi, nr=T)

        def s3(ks=ks, sbi=sbi):
            qside_post(ks, NOWN * 128 + sbi * T, nr=T)
        items.append((s1, s2, s3))
        nitem[0] += 0
    for t in range(len(items) + 2):
        if t < len(items) and items[t][0] is not None:
            items[t][0]()
        if 0 <= t - 1 < len(items) and items[t - 1][1] is not None:
            items[t - 1][1]()
        if 0 <= t - 2 < len(items) and items[t - 2][2] is not None:
            items[t - 2][2]()

    def cum(seq, nb):
        n8 = nb * 8
        lf2 = LOGF[seq][:, :, :].rearrange("p a b -> p (a b)")
        PF = cumtmp[:, 0, 0:n8]; TBt = cumtmp[:, 1, 0:n8]; INC = cumtmp[:, 2, 0:n8]
        P.op("pe", lambda e: e.matmul(out=Bk[4][:, 0:n8], lhsT=Umat, rhs=lf2, start=True, stop=True),
             reads=[r_LOGF[seq], r_cst], writes=[rB[4]])
        P.op("dve", lambda e: e.tensor_copy(out=PF, in_=Bk[4][:, 0:n8]), reads=[rB[4]], writes=[r_cumtmp])
        P.op("pe", lambda e: e.matmul(out=Bk[5][:, 0:n8], lhsT=Elast, rhs=PF, start=True, stop=True),
             reads=[r_cumtmp, r_cst], writes=[rB[5]])
        P.op("dve", lambda e: e.tensor_copy(out=TBt, in_=Bk[5][:, 0:n8]), reads=[rB[5]], writes=[r_cumtmp])
        TB3 = TBt.rearrange("p (a b) -> p a b", b=8); INC3 = INC.rearrange("p (a b) -> p a b", b=8)
        for h in range(8):
            P.op("dve", lambda e, h=h: e.tensor_tensor_scan(out=INC3[:, :, h], data0=ones32[:, 0:nb], data1=TB3[:, :, h], initial=0.0,
                                                            op0=ALU.mult, op1=ALU.add),
                 reads=[r_cumtmp, r_ones32], writes=[r_cumtmp])
        P.op("dve", lambda e: e.memset(CARRY[seq][:, 0, :], 0.0), writes=[r_cum[seq]])
        P.op("dve", lambda e: e.tensor_copy(out=CARRY[seq][:, 1:nb, :], in_=INC3[:, 0:nb - 1, :]), reads=[r_cumtmp], writes=[r_cum[seq]])
        P.op("dve", lambda e: e.tensor_tensor(out=CK[seq][:, :, :], in0=PF.rearrange("p (a b) -> p a b", b=8), in1=CARRY[seq][:, :, :], op=ALU.add),
             reads=[r_cumtmp, r_cum[seq]], writes=[r_cum[seq]])

    cum(0, NB)
    for sbi in range(SB):
        cum(1 + sbi, NBS)


    def phase_b():
        apos[0] = APOS_AB
        IKTs = aview([128, SEQ], BF16); r_IKT = Res("IKTs")
        KT = [aview([128, SEQ], BF16) for _ in range(2)]; r_KT = [Res("KT%d" % i) for i in range(2)]
        VA = [aview([128, NB, 256], BF16) for _ in range(2)]; r_VA = [Res("VA%d" % i) for i in range(2)]
        SC = aview([128, SEQ], F32); r_SC = Res("SC")
        MQ = aview([128, SEQ], BF16); r_MQ = Res("MQ")
        rtmp = [aview([128, 512], F32) for _ in range(2)]; r_rtmp = [Res("rtmp%d" % i) for i in range(2)]
        maskTs = [aview([128, NB, 256], BF16) for _ in range(2)]; r_maskTs = [Res("maskT%d" % i) for i in range(2)]
        NPT = 4
        PTs = [aview([128, 512], BF16) for _ in range(NPT)]; r_PTs = [Res("PTs%d" % i) for i in range(NPT)]
        Qpad = [aview([128, 512], BF16) for _ in range(2)]; r_Qpad = [Res("Qpad%d" % i) for i in range(2)]
        rd = [aview([128, 256], F32) for _ in range(2)]; r_rd = [Res("rd%d" % i) for i in range(2)]
        fmask = aview([128, 2, 4, 256], BF16); smaskb = aview([128, 64], BF16); pencap = aview([128, 2, 2, 512], F32)
        r_msk = Res("masks")
        BIASs = [aview([128, NB, 8], F32) for _ in range(2)]; r_BIASs = [Res("BIAS%d" % i) for i in range(2)]
        bst = aview([128, 64], F32); r_bst = Res("bst")
        pow2 = aview([128, NITER], F32); r_pow2 = Res("pow2")
        mixst = [aview([128, 8, 256], BF16) for _ in range(1)]; r_mixst = [Res("mixst%d" % i) for i in range(1)]
        d_ikt = dsem(); d_kv = [dsem(), dsem()]; d_m = dsem(); d_mx = [dsem(), dsem()]

        P.dma("pool", d_m, fmask[:, :, :, :].rearrange("p a b c -> p (a b c)"), fmask_d[:, :], writes=[r_msk])
        P.dma("pool", d_m, smaskb[:, :], smask_d[:, :], writes=[r_msk])
        P.dma("sp", d_m, pencap[:, :, :, :].rearrange("p a b c -> p (a b c)"), pencap_d[:, :], writes=[r_msk])
        for k in range(NITER):
            P.op("pool", lambda e, k=k: e.memset(pow2[:, k:k + 1], 2.0 ** (-(k + 1))), writes=[r_pow2])
        rmax = bst[:, 0:1]; rmin = bst[:, 1:2]; tcur = bst[:, 2:3]; cntc = bst[:, 3:4]; sg = bst[:, 4:5]; Rr = bst[:, 5:6]
        stp = bst[:, 8:8 + NITER]
        cnts = {"kv": 0, "ix": 0, "pt": 0, "o": 0, "mx": 0, "mk": 0, "rk": 0}

        def qtile(seq, nkb, nq, qoff, slots, cref, jpar, tokoff, tileno, ikt_load=None):
            L = nkb * 128
            prompt = (seq == 0)
            tb_ = tileno % 2
            maskT = maskTs[tb_]; r_maskT = r_maskTs[tb_]; BIAS = BIASs[tb_]; r_BIAS = r_BIASs[tb_]

            def idx():
                yield "u"
                if ikt_load is not None:
                    ikt_load()
                P.op("dve", lambda e: e.tensor_tensor(out=BIAS[:, 0:nkb, :], in0=CARRY[seq][:, cref:cref + 1, :].to_broadcast([128, nkb, 8]),
                                                      in1=CK[seq][:, 0:nkb, :], op=ALU.subtract),
                     reads=[r_cum[seq]], writes=[r_BIAS])
                for si, (nr, iwidx) in enumerate(slots):
                    qc0 = qoff + si * 128
                    for lc in range(0, L, 512):
                        lw = min(512, L - lc)
                        for h in range(8):
                            yield "u"
                            hp = h // 2; po = (h % 2) * 64
                            k = cnts["ix"] % 2; cnts["ix"] += 1
                            bk = 4 + k
                            P.op("pe", lambda e, hp=hp, po=po, bk=bk, lc=lc, lw=lw: e.matmul(
                                out=Bk[bk][0:nr, 0:lw], lhsT=IQT[po:po + 32, hp, qc0:qc0 + nr], rhs=IKTs[po:po + 32, lc:lc + lw],
                                start=True, stop=True), reads=[r_Q, r_IKT], writes=[rB[bk]])
                            P.op("act", lambda e, k=k, bk=bk, lw=lw: e.activation(out=rtmp[k][0:nr, 0:lw], in_=Bk[bk][0:nr, 0:lw], func=AF.Relu),
                                 reads=[rB[bk]], writes=[r_rtmp[k]])
                            wcol = IW[0:nr, iwidx, h:h + 1]
                            if h == 0:
                                P.op("dve", lambda e, k=k, lc=lc, lw=lw, wcol=wcol: e.tensor_scalar(
                                    out=SC[0:nr, lc:lc + lw], in0=rtmp[k][0:nr, 0:lw], scalar1=wcol, scalar2=None, op0=ALU.mult),
                                    reads=[r_rtmp[k], r_Q], writes=[r_SC])
                            else:
                                eng = "dve"
                                P.op(eng, lambda e, k=k, lc=lc, lw=lw, wcol=wcol: e.scalar_tensor_tensor(
                                    out=SC[0:nr, lc:lc + lw], in0=rtmp[k][0:nr, 0:lw], scalar=wcol, in1=SC[0:nr, lc:lc + lw],
                                    op0=ALU.mult, op1=ALU.add), reads=[r_rtmp[k], r_Q, r_SC], writes=[r_SC])
                    P.op("dve", lambda e: e.tensor_reduce(out=rmax[0:nr], in_=SC[0:nr, 0:L], axis=AX.X, op=ALU.max), reads=[r_SC], writes=[r_bst])
                    P.op("dve", lambda e: e.tensor_reduce(out=rmin[0:nr], in_=SC[0:nr, 0:L], axis=AX.X, op=ALU.min), reads=[r_SC], writes=[r_bst])
                    if prompt:
                        P.op("dve", lambda e, si=si: e.tensor_tensor(out=SC[0:nr, L - 512:L], in0=SC[0:nr, L - 512:L], in1=pencap[0:nr, jpar, si, :], op=ALU.min),
                             reads=[r_SC, r_msk], writes=[r_SC])
                    else:
                        P.op("dve", lambda e: e.memset(SC[0:nr, PAST + T:L], NEG), reads=[r_SC], writes=[r_SC])
                    P.op("dve", lambda e: e.scalar_tensor_tensor(out=Rr[0:nr], in0=rmax[0:nr], scalar=2.0, in1=rmin[0:nr], op0=ALU.add, op1=ALU.subtract),
                         reads=[r_bst], writes=[r_bst])
                    P.op("dve", lambda e: e.scalar_tensor_tensor(out=tcur[0:nr], in0=rmax[0:nr], scalar=0.5, in1=rmin[0:nr], op0=ALU.mult, op1=ALU.add),
                         reads=[r_bst], writes=[r_bst])
                    P.op("dve", lambda e: e.scalar_tensor_tensor(out=tcur[0:nr], in0=rmin[0:nr], scalar=-0.5, in1=tcur[0:nr], op0=ALU.mult, op1=ALU.add),
                         reads=[r_bst], writes=[r_bst])
                    P.op("dve", lambda e: e.tensor_scalar(out=stp[0:nr, :], in0=pow2[0:nr, :], scalar1=Rr[0:nr], scalar2=None, op0=ALU.mult),
                         reads=[r_bst, r_pow2], writes=[r_bst])
                    for it in range(NITER):
                        yield "u"
                        P.op("dve", lambda e: e.tensor_scalar(out=MQ[0:nr, 0:L], in0=SC[0:nr, 0:L], scalar1=tcur[0:nr], scalar2=0.0,
                                                              op0=ALU.is_ge, op1=ALU.add, accum_out=cntc[0:nr]),
                             reads=[r_SC, r_bst], writes=[r_MQ, r_bst])
                        P.op("dve", lambda e: e.tensor_scalar(out=sg[0:nr], in0=cntc[0:nr], scalar1=255.5, scalar2=0.5, op0=ALU.is_ge, op1=ALU.subtract),
                             reads=[r_bst], writes=[r_bst])
                        P.op("dve", lambda e, it=it: e.scalar_tensor_tensor(out=tcur[0:nr], in0=sg[0:nr], scalar=stp[0:nr, it:it + 1], in1=tcur[0:nr],
                                                                            op0=ALU.mult, op1=ALU.add), reads=[r_bst], writes=[r_bst])
                    P.op("dve", lambda e: e.tensor_scalar(out=MQ[0:nr, 0:L], in0=SC[0:nr, 0:L], scalar1=tcur[0:nr], scalar2=None, op0=ALU.is_ge),
                         reads=[r_SC, r_bst], writes=[r_MQ])
                    for kb0 in range(0, nkb, 8):
                        yield "T"
                        g = min(8, nkb - kb0)
                        for kk in range(g):
                            kb = kb0 + kk
                            P.op("pe", lambda e, kb=kb, kk=kk: e.transpose(out=PT[:, 1024 + kk * 128:1024 + kk * 128 + nr], in_=MQ[0:nr, kb * 128:(kb + 1) * 128],
                                                                           identity=identb[0:nr, 0:nr]), reads=[r_MQ, r_identb], writes=[rB[7]])
                        src = PT[:, 1024:1024 + g * 128].rearrange("p (a b) -> p a b", b=128)[:, :, 0:nr]
                        dst = maskT[:, kb0:kb0 + g, si * 128:si * 128 + nr]
                        P.op("act", lambda e, src=src, dst=dst: e.activation(out=dst, in_=src, func=AF.Identity, bias=-30000.0, scale=30000.0),
                             reads=[rB[7]], writes=[r_maskT])

            def attn():
                mxs = 0
                sbanks = (0, 1, 6)
                pairs = [(tp, hp) for tp in range(2) for hp in range(4)]
                pair_kv = {}
                n2 = 2 * nq

                def load_pair(pi_):
                    tp, hp = pairs[pi_]
                    kv = cnts["kv"] % 2; cnts["kv"] += 1
                    pair_kv[pi_] = kv
                    P.dma("sp", d_kv[kv], KT[kv][:, 0:L], s_kt[seq][tp * 4 + hp, :, 0:L], reads=[r_scr[seq]], writes=[r_KT[kv]])
                    P.dma("sp", d_kv[kv], VA[kv][:, 0:nkb, :], s_v[seq][:, 0:nkb, tp * 1024 + hp * 256:tp * 1024 + (hp + 1) * 256],
                          reads=[r_scr[seq]], writes=[r_VA[kv]])
                    Qsrc = QTf if tp == 0 else QTd
                    qp = Qpad[kv]
                    P.op("pool", lambda e: e.memset(qp[:, 0:n2], 0.0), writes=[r_Qpad[kv]])
                    P.op("pool", lambda e: e.tensor_copy(out=qp[0:64, 0:nq], in_=Qsrc[0:64, hp, qoff:qoff + nq]), reads=[r_Q], writes=[r_Qpad[kv]])
                    P.op("pool", lambda e: e.tensor_copy(out=qp[64:128, nq:n2], in_=Qsrc[64:128, hp, qoff:qoff + nq]), reads=[r_Q], writes=[r_Qpad[kv]])

                steps = []
                for pi_, (tp, hp) in enumerate(pairs):
                    ob = 2 + (cnts["o"] % 2); cnts["o"] += 1
                    for kb in range(nkb):
                        steps.append((pi_, tp, hp, kb, ob))

                def s_step(st_, si_):
                    pi_, tp, hp, kb, ob = st_
                    kv = pair_kv[pi_]
                    sb_ = sbanks[si_ % 3]
                    pi = cnts["pt"] % NPT; cnts["pt"] += 1
                    P.op("pe", lambda e: e.matmul(out=Bk[sb_][:, 0:n2], lhsT=KT[kv][:, kb * 128:(kb + 1) * 128],
                                                  rhs=Qpad[kv][:, 0:n2], start=True, stop=(tp == 0), skip_group_check=True),
                         reads=[r_KT[kv], r_Qpad[kv]], writes=[rB[sb_]])
                    if tp == 1:
                        for ho in range(2):
                            P.op("pe", lambda e, ho=ho: e.matmul(out=Bk[sb_][:, ho * nq:(ho + 1) * nq], lhsT=identb[:, :], rhs=maskT[:, kb, 0:nq],
                                                                 start=False, stop=(ho == 1), skip_group_check=True),
                                 reads=[r_identb, r_maskT], writes=[rB[sb_]])
                    return pi, sb_

                def e_step(st_, pi, sb_):
                    pi_, tp, hp, kb, ob = st_
                    if tp == 0:
                        for ho in range(2):
                            h = 2 * hp + ho
                            P.op("act", lambda e, ho=ho, h=h: e.activation(out=PTs[pi][:, ho * nq:(ho + 1) * nq], in_=Bk[sb_][:, ho * nq:(ho + 1) * nq],
                                                                        func=AF.Exp, bias=BIAS[:, kb, h:h + 1], scale=0.125),
                                 reads=[rB[sb_], r_BIAS], writes=[r_PTs[pi]])
                    else:
                        P.op("act", lambda e: e.activation(out=PTs[pi][:, 0:n2], in_=Bk[sb_][:, 0:n2], func=AF.Exp, scale=0.125),
                             reads=[rB[sb_]], writes=[r_PTs[pi]])
                    mk = None
                    if tp == 1:
                        pass
                    elif prompt and kb >= nkb - 4:
                        mk = fmask[:, jpar, kb - (nkb - 4), :]; rmk = r_msk
                    elif (not prompt) and kb == nkb - 1:
                        mk = smaskb[:, :]; rmk = r_msk
                    if mk is not None:
                        pv_ = PTs[pi][:, 0:n2].rearrange("p (a b) -> p a b", b=nq)
                        mkb = mk.unsqueeze(1).to_broadcast([128, 2, nq])
                        P.op("dve", lambda e: e.tensor_tensor(out=pv_, in0=pv_, in1=mkb, op=ALU.mult),
                             reads=[r_PTs[pi], rmk], writes=[r_PTs[pi]])

                def pv_step(st_, pi):
                    pi_, tp, hp, kb, ob = st_
                    kv = pair_kv[pi_]
                    for ho in range(2):
                        P.op("pe", lambda e, ho=ho: e.matmul(out=Bk[ob][:, ho * nq:(ho + 1) * nq], lhsT=VA[kv][:, kb, ho * 128:(ho + 1) * 128],
                                                             rhs=PTs[pi][:, ho * nq:(ho + 1) * nq], start=(kb == 0 and ho == 0),
                                                             stop=(kb == nkb - 1 and ho == 1), skip_group_check=True),
                             reads=[r_VA[kv], r_PTs[pi]], writes=[rB[ob]])
                    if kb == nkb - 1:
                        for ho in range(2):
                            po = ho * 64
                            rk = cnts["rk"] % 2; cnts["rk"] += 1
                            if tp == 1:
                                P.op("act", lambda e, ho=ho, rk=rk: e.activation(out=rd[rk][0:64, 0:nq], in_=Bk[ob][64:128, ho * nq:(ho + 1) * nq], func=AF.Ln),
                                     reads=[rB[ob]], writes=[r_rd[rk]])
                                P.op("act", lambda e, rk=rk: e.activation(out=rd[rk][0:64, 0:nq], in_=rd[rk][0:64, 0:nq], func=AF.Exp, scale=-1.0),
                                     reads=[r_rd[rk]], writes=[r_rd[rk]])
                            else:
                                P.op("dve", lambda e, ho=ho, rk=rk: e.reciprocal(out=rd[rk][0:64, 0:nq], in_=Bk[ob][64:128, ho * nq:(ho + 1) * nq]),
                                     reads=[rB[ob]], writes=[r_rd[rk]])
                            P.op("dve", lambda e, ho=ho, rk=rk: e.tensor_tensor(out=rd[rk][0:64, 0:nq], in0=Bk[ob][0:64, ho * nq:(ho + 1) * nq],
                                                                                in1=rd[rk][0:64, 0:nq], op=ALU.mult),
                                 reads=[rB[ob], r_rd[rk]], writes=[r_rd[rk]])
                            P.op("act", lambda e, po=po, rk=rk: e.copy(out=mixst[mxs][po:po + 64, tp * 4 + hp, 0:nq], in_=rd[rk][0:64, 0:nq]),
                                 reads=[r_rd[rk]], writes=[r_mixst[mxs]])

                SKEW = 3
                pend = []
                load_pair(0)
                for si_, st_ in enumerate(steps):
                    if si_ > 0 and st_[1] != steps[si_ - 1][1]:
                        while pend:
                            pv_step(*pend.pop(0))
                        yield "drained"
                    pi, sb_ = s_step(st_, si_)
                    if len(pend) >= SKEW:
                        pv_step(*pend.pop(0))
                    e_step(st_, pi, sb_)
                    pend.append((st_, pi))
                    if st_[3] == SKEW and st_[0] + 1 < len(pairs):
                        load_pair(st_[0] + 1)
                    yield st_[1]
                while pend:
                    pv_step(*pend.pop(0))
                P.dma("sp", d_mx[mxs], s_mix.rearrange("c p t -> p c t")[:, :, tokoff:tokoff + nq], mixst[mxs][:, :, 0:nq],
                      reads=[r_mixst[mxs]], writes=[r_smix])
                yield "drained"

            return idx, attn

        def ikt_loader(seq_, ncol):
            def f():
                P.dma("sp", d_ikt, IKTs[:, 0:ncol], s_ikt[seq_][:, :], reads=[r_scr[seq_]], writes=[r_IKT])
            return f
        tiles = []
        tiles_nkb = [4 * j + 4 for j in range(NQT)] + [NBS] * SB
        def _units(nkb_, nslots):
            L_ = nkb_ * 128
            return 1 + nslots * (((L_ + 511) // 512) * 8 + NITER + (nkb_ + 7) // 8)
        tiles_units = [_units(4 * j + 4, 2) for j in range(NQT)] + [_units(NBS, 1)] * SB
        for j in range(NQT):
            tiles.append(qtile(0, 4 * j + 4, 256, j * 256, [(128, 2 * j), (128, 2 * j + 1)], 4 * j, j % 2, j * 256, len(tiles),
                               ikt_loader(0, SEQ) if j == 0 else None))
        for sbi in range(SB):
            tiles.append(qtile(1 + sbi, NBS, T, NOWN * 128 + sbi * T, [(T, NOWN + sbi)], 8, 0, NOWN * 128 + sbi * T, len(tiles),
                               ikt_loader(1 + sbi, LS)))
        class _Idx:
            def __init__(self, gen):
                self.g = gen
                self.up = next(gen)

            def run_one(self):
                try:
                    self.up = next(self.g)
                except StopIteration:
                    self.up = None

            def finish(self):
                while self.up is not None:
                    self.run_one()

        _Idx(tiles[0][0]()).finish()
        for n in range(len(tiles)):
            ag = tiles[n][1]()
            ig = _Idx(tiles[n + 1][0]()) if n + 1 < len(tiles) else None
            credit = 0.0
            nsteps = 8 * (tiles_nkb[n])
            niu = tiles_units[n + 1] if ig is not None else 0
            wsum = 1.5 * nsteps
            for ev in ag:
                if ig is None:
                    continue
                if ev == "drained":
                    while ig.up == "T":
                        ig.run_one()
                    continue
                credit += niu * (2.0 if ev == 0 else 1.0) / wsum
                while credit >= 1.0 and ig.up == "u":
                    credit -= 1.0
                    ig.run_one()
            if ig is not None:
                ig.finish()


    def phase_c():
        apos[0] = 0
        WO = aview([128, 8, D], BF16); WG = aview([128, 8, D], BF16); WPLE = aview([128, 2, D], BF16)
        r_WC = Res("WC")
        vecC = aview([128, 5, D], F32); r_vecC = Res("vecC")
        WUPc = [aview([128, 8, 512], BF16) for _ in range(2)]; r_WUP = [Res("WUP%d" % i) for i in range(2)]
        WDNc = [aview([128, 4, D], BF16) for _ in range(2)]; r_WDN = [Res("WDN%d" % i) for i in range(2)]
        hidT = aview([128, 32, 512], BF16); r_hid = Res("hidT")
        x1 = aview([128, 4, D], F32); r_x1 = [Res("x1_%d" % i) for i in range(4)]
        xT1 = aview([128, 8, 512], BF16); r_xT1 = Res("xT1")
        mixg = aview([128, 8, 512], BF16); r_mixg = Res("mixg")
        xres = [aview([128, D], F32) for _ in range(2)]; r_xres = [Res("xres%d" % i) for i in range(2)]
        pet = [aview([128, 256], F32) for _ in range(2)]; r_pet = [Res("pet%d" % i) for i in range(2)]
        peT = [aview([128, 2, 128], BF16) for _ in range(4)]; r_peT = [Res("peT%d" % i) for i in range(4)]
        tA = [aview([128, D], F32) for _ in range(4)]; r_tA = [Res("tA%d" % i) for i in range(4)]
        rl = [aview([128, 512], F32) for _ in range(2)]; r_rl = [Res("rl%d" % i) for i in range(2)]
        st = aview([128, 32], F32); r_st = Res("st")
        d_wc = dsem(); d_wu = [dsem(), dsem(), dsem()]; d_wd = [dsem(), dsem(), dsem()]; d_mg = dsem(); d_xr = [dsem(), dsem()]; d_y = [dsem(), dsem()]; d_pe = [dsem(), dsem()]
        P.dma("pool", d_wc, WO[:, :, :], wo.rearrange("(k p) n -> p k n", p=128), writes=[r_WC])
        P.dma("pool", d_wc, WG[:, :, :], wg.rearrange("(k p) n -> p k n", p=128), writes=[r_WC])
        P.dma("pool", d_wc, WPLE[:, :, :], wple.rearrange("(k p) n -> p k n", p=128), writes=[r_WC])
        P.dma("sp", d_wc, vecC[:, :, :].rearrange("p a b -> p (a b)"), vecs[:, 8:8 + 5 * D], writes=[r_vecC])
        g1 = vecC[:, 0, :]; b1 = vecC[:, 1, :]; g2 = vecC[:, 2, :]; b2 = vecC[:, 3, :]; bg = vecC[:, 4, :]
        wupv = wup.rearrange("(k p) f -> p k f", p=128)
        wdnv = wdn.rearrange("(c p) n -> p c n", p=128)
        cn = {"xr": 0, "wu": 0, "wd": 0}

        def layer_norm(xa, rxa, gam, bet):
            stats = st[:, 0:12].rearrange("p (a b) -> p a b", b=6); mv = st[:, 12:14]; rstd = st[:, 14:15]
            for c2 in range(2):
                P.op("dve", lambda e, c2=c2: e.bn_stats(out=stats[:, c2, :], in_=xa[:, c2 * 512:(c2 + 1) * 512]), reads=[rxa], writes=[r_st])
            P.op("dve", lambda e: e.bn_aggr(out=mv, in_=st[:, 0:12]), reads=[r_st], writes=[r_st])
            P.op("act", lambda e: e.activation(out=rstd, in_=mv[:, 1:2], func=AF.Ln, bias=1e-5), reads=[r_st], writes=[r_st])
            P.op("act", lambda e: e.activation(out=rstd, in_=rstd, func=AF.Exp, scale=-0.5), reads=[r_st], writes=[r_st])
            P.op("dve", lambda e: e.tensor_scalar(out=xa, in0=xa, scalar1=mv[:, 0:1], scalar2=rstd, op0=ALU.subtract, op1=ALU.mult),
                 reads=[rxa, r_st], writes=[rxa])
            P.op("dve", lambda e: e.tensor_tensor(out=xa, in0=xa, in1=gam, op=ALU.mult), reads=[rxa, r_vecC], writes=[rxa])
            P.op("dve", lambda e: e.tensor_tensor(out=xa, in0=xa, in1=bet, op=ALU.add), reads=[rxa, r_vecC], writes=[rxa])

        def transpose_to(xa, rxa, tb, banks):
            for half in range(2):
                bk = banks[half]
                for kc in range(4):
                    c = half * 4 + kc
                    P.op("pe", lambda e, c=c, kc=kc, bk=bk: e.transpose(out=Bk[bk][:, kc * 128:(kc + 1) * 128], in_=xa[:, c * 128:(c + 1) * 128], identity=identf),
                         reads=[rxa, r_cst], writes=[rB[bk]])
                dst = xT1[:, half * 4:(half + 1) * 4, tb * 128:(tb + 1) * 128]
                src = Bk[bk].rearrange("p (a b) -> p a b", b=128)
                if half == 0:
                    P.op("act", lambda e, dst=dst, src=src: e.copy(out=dst, in_=src), reads=[rB[bk]], writes=[r_xT1])
                else:
                    P.op("dve", lambda e, dst=dst, src=src: e.tensor_copy(out=dst, in_=src), reads=[rB[bk]], writes=[r_xT1])

        def group(G, tok0, xsrc, psrc, ydst, row0):
            ntok = G * 128
            P.dma("sp", d_mg, mixg[:, :, 0:ntok], s_mix.rearrange("c p t -> p c t")[:, :, tok0:tok0 + ntok], reads=[r_smix], writes=[r_mixg])
            for tb in range(G):
                xr = cn["xr"] % 2; cn["xr"] += 1
                rows = slice(row0 + tb * 128, row0 + (tb + 1) * 128)
                P.dma("sp", d_xr[xr], xres[xr][:, :], xsrc[rows, :], writes=[r_xres[xr]])
                xa = x1[:, tb, :]
                for cc in range(2):
                    bk = tb * 2 + cc
                    for kc in range(8):
                        P.op("pe", lambda e, kc=kc, cc=cc, bk=bk: e.matmul(out=Bk[bk][:, 0:512], lhsT=mixg[:, kc, tb * 128:(tb + 1) * 128],
                                                                           rhs=WO[:, kc, cc * 512:(cc + 1) * 512], start=(kc == 0), stop=(kc == 7)),
                             reads=[r_mixg, r_WC], writes=[rB[bk]])
                    P.op("dve", lambda e, cc=cc, bk=bk: e.scalar_tensor_tensor(out=xa[:, cc * 512:(cc + 1) * 512], in0=xres[xr][:, cc * 512:(cc + 1) * 512],
                                                                               scalar=ALPHA, in1=Bk[bk][:, 0:512], op0=ALU.mult, op1=ALU.add),
                         reads=[r_xres[xr], rB[bk]], writes=[r_x1[tb]])
            for tb in range(G):
                layer_norm(x1[:, tb, :], r_x1[tb], g1, b1)
            for tb in range(G):
                transpose_to(x1[:, tb, :], r_x1[tb], tb, (tb * 2, tb * 2 + 1))
            for fch in range(32):
                c = fch // 4
                if fch % 4 == 0:
                    wu = cn["wu"] % 2; cn["wu"] += 1
                    P.dma("pool", d_wu[wu], WUPc[wu][:, :, :], wupv[:, :, c * 512:(c + 1) * 512], writes=[r_WUP[wu]])
                bk = fch % 2
                for kc in range(8):
                    P.op("pe", lambda e, kc=kc, fch=fch, bk=bk, wu=wu: e.matmul(out=Bk[bk][:, 0:ntok], lhsT=WUPc[wu][:, kc, (fch % 4) * 128:(fch % 4 + 1) * 128],
                                                                             rhs=xT1[:, kc, 0:ntok], start=(kc == 0), stop=(kc == 7)),
                         reads=[r_WUP[wu], r_xT1], writes=[rB[bk]])
                P.op("act", lambda e, bk=bk: e.activation(out=rl[bk][:, 0:ntok], in_=Bk[bk][:, 0:ntok], func=AF.Relu), reads=[rB[bk]], writes=[r_rl[bk]])
                P.op("dve", lambda e, bk=bk, fch=fch: e.tensor_tensor(out=hidT[:, fch, 0:ntok], in0=rl[bk][:, 0:ntok], in1=rl[bk][:, 0:ntok], op=ALU.mult),
                     reads=[r_rl[bk]], writes=[r_hid])
            for c in range(8):
                wd = cn["wd"] % 2; cn["wd"] += 1
                P.dma("pool", d_wd[wd], WDNc[wd][:, :, :], wdnv[:, c * 4:(c + 1) * 4, :], writes=[r_WDN[wd]])
                for tb in range(G):
                    for cc in range(2):
                        for f4 in range(4):
                            fc = c * 4 + f4
                            P.op("pe", lambda e, tb=tb, cc=cc, f4=f4, fc=fc, wd=wd: e.matmul(
                                out=Bk[tb * 2 + cc][:, 0:512], lhsT=hidT[:, fc, tb * 128:(tb + 1) * 128], rhs=WDNc[wd][:, f4, cc * 512:(cc + 1) * 512],
                                start=(fc == 0), stop=(fc == 31)), reads=[r_hid, r_WDN[wd]], writes=[rB[tb * 2 + cc]])
            for tb in range(G):
                xa = x1[:, tb, :]
                for cc in range(2):
                    P.op("dve", lambda e, cc=cc, tb=tb: e.scalar_tensor_tensor(out=xa[:, cc * 512:(cc + 1) * 512], in0=xa[:, cc * 512:(cc + 1) * 512], scalar=ALPHA,
                                                                               in1=Bk[tb * 2 + cc][:, 0:512], op0=ALU.mult, op1=ALU.add),
                         reads=[r_x1[tb], rB[tb * 2 + cc]], writes=[r_x1[tb]])
            for tb in range(G):
                layer_norm(x1[:, tb, :], r_x1[tb], g2, b2)
            for tb in range(G):
                transpose_to(x1[:, tb, :], r_x1[tb], tb, (tb * 2, tb * 2 + 1))
            for tb in range(G):
                k = tb
                rows = slice(row0 + tb * 128, row0 + (tb + 1) * 128)
                P.dma("sp", d_pe[tb % 2], pet[tb % 2][:, :], psrc[rows, :], writes=[r_pet[tb % 2]])
                for cc in range(2):
                    bk = tb * 2 + cc
                    for kc in range(8):
                        P.op("pe", lambda e, kc=kc, cc=cc, bk=bk, tb=tb: e.matmul(out=Bk[bk][:, 0:512], lhsT=xT1[:, kc, tb * 128:(tb + 1) * 128],
                                                                               rhs=WG[:, kc, cc * 512:(cc + 1) * 512], start=(kc == 0), stop=(kc == 7)),
                             reads=[r_xT1, r_WC], writes=[rB[bk]])
                    P.op("dve", lambda e, cc=cc, bk=bk, k=k: e.tensor_tensor(out=tA[k][:, cc * 512:(cc + 1) * 512], in0=Bk[bk][:, 0:512], in1=bg[:, cc * 512:(cc + 1) * 512], op=ALU.add),
                         reads=[rB[bk], r_vecC], writes=[r_tA[k]])
                for c2 in range(2):
                    P.op("pe", lambda e, c2=c2, tb=tb: e.transpose(out=Bk[tb * 2][:, c2 * 128:(c2 + 1) * 128], in_=pet[tb % 2][:, c2 * 128:(c2 + 1) * 128], identity=identf),
                         reads=[r_pet[tb % 2], r_cst], writes=[rB[tb * 2]])
                P.op("act", lambda e, tb=tb: e.copy(out=peT[tb][:, :, :], in_=Bk[tb * 2][:, 0:256].rearrange("p (a b) -> p a b", b=128)), reads=[rB[tb * 2]], writes=[r_peT[tb]])
            for tb in range(G):
                P.op("act", lambda e, k=tb: e.activation(out=tA[k][:, :], in_=tA[k][:, :], func=AF.Sigmoid), reads=[r_tA[tb]], writes=[r_tA[tb]])
            for tb in range(G):
                k = tb
                xa = x1[:, tb, :]
                rows = slice(row0 + tb * 128, row0 + (tb + 1) * 128)
                for cc in range(2):
                    bk = tb * 2 + cc
                    for c2 in range(2):
                        P.op("pe", lambda e, c2=c2, cc=cc, bk=bk, tb=tb: e.matmul(out=Bk[bk][:, 0:512], lhsT=peT[tb][:, c2, :], rhs=WPLE[:, c2, cc * 512:(cc + 1) * 512],
                                                                                 start=(c2 == 0), stop=(c2 == 1)), reads=[r_peT[tb], r_WC], writes=[rB[bk]])
                    P.op("dve", lambda e, cc=cc, bk=bk, k=k: e.tensor_tensor(out=tA[k][:, cc * 512:(cc + 1) * 512], in0=tA[k][:, cc * 512:(cc + 1) * 512], in1=Bk[bk][:, 0:512], op=ALU.mult),
                         reads=[r_tA[k], rB[bk]], writes=[r_tA[k]])
                P.op("dve", lambda e, k=k: e.tensor_tensor(out=tA[k][:, :], in0=tA[k][:, :], in1=xa, op=ALU.add), reads=[r_tA[k], r_x1[tb]], writes=[r_tA[k]])
                P.dma("sp", d_y[k % 2], ydst[rows, :], tA[k][:, :], reads=[r_tA[k]])

        for g in range(4):
            group(4, g * 512, xown, pown, o_y, g * 512)
        group(2, NOWN * 128, xs, pss, o_ys, 0)

    if STOP >= 2:
        P.barrier()
        phase_b()
    if STOP >= 3:
        P.barrier()
        phase_c()
    return nc, P, es


def _finish(nc, P, es):
    esem = {}
    for e in Prog.ENGS:
        esem[e] = es.enter_context(nc.semaphore("es_" + e))
    with nc.Block() as block:
        P.emit(nc, block, esem)
    es.close()
    return nc


OWN_PAIRS = {0: [0, 3, 4, 7, 8, 11, 12, 15], 1: [1, 2, 5, 6, 9, 10, 13, 14]}
_SPLITS = (512, 1024, 1536, 1544, 2056, 2568, 3080, 3336, 3368)


def _rope_table(pos):
    out = np.zeros((len(pos), 96), np.float32)
    p = np.asarray(pos, np.float32)[:, None]
    for half, o in ((32, 0), (16, 64)):
        inv = (np.float32(10000.0) ** (-np.arange(half, dtype=np.float32) / np.float32(half))).astype(np.float32)
        ang = (p * inv[None, :]).astype(np.float32)
        out[:, o:o + half] = np.cos(ang)
        out[:, o + half:o + 2 * half] = np.sin(ang)
    return out


def _own_blocks(half):
    bl = []
    for p in OWN_PAIRS[half]:
        bl += [2 * p, 2 * p + 1]
    return bl


def _masks(half):
    l = np.arange(128)[:, None]
    q = np.arange(128)[None, :]
    tri = (l <= q).astype(np.float32)
    one = np.ones((128, 128), np.float32)
    zer = np.zeros((128, 128), np.float32)
    MA = np.concatenate([tri, one], 1); MB = np.concatenate([zer, tri], 1)
    ON = np.concatenate([one, one], 1); ZE = np.concatenate([zer, zer], 1)
    fm = np.zeros((128, 2, 4, 256), np.float32)
    pc = np.zeros((128, 2, 2, 512), np.float32)
    for jpar in range(2):
        hi = (OWN_PAIRS[half][jpar] == 2 * jpar + 1)
        tiles = [ON, ON, MA, MB] if hi else [MA, MB, ZE, ZE]
        for i in range(4):
            fm[:, jpar, i, :] = tiles[i]
        for s in range(2):
            qb_rel = (2 if hi else 0) + s
            row = np.arange(128)[:, None]
            col = np.arange(512)[None, :]
            lim = qb_rel * 128 + np.where(row < 64, 64, 128)
            pc[:, jpar, s, :] = np.where(col < lim, BIG, NEG)
    sm = np.zeros((128, 64), np.float32)
    sm[:64, :] = (np.arange(64)[:, None] <= np.arange(64)[None, :]).astype(np.float32)
    return fm.reshape(128, -1), sm, pc.reshape(128, -1)


def _prep(inputs):
    f = lambda a: np.ascontiguousarray(np.asarray(a, dtype=np.float32))
    w_in = f(inputs["w_in"][0])
    fq, fk, fv, fg, dq, dk, dv, iq, ik, iw = np.split(w_in, _SPLITS, axis=1)
    wk = f(np.concatenate([fk, dk, fv, dv, ik, fg], 1))
    wq = f(np.concatenate([fq, dq, iq, iw], 1))
    shared = dict(
        wk=wk, wq=wq, wo=f(inputs["w_o"][0]), wup=f(inputs["w_up"][0]), wdn=f(inputs["w_down"][0]),
        wg=f(inputs["w_ple_gate"][0]), wple=f(inputs["w_ple"][0]),
        vecs=f(np.broadcast_to(np.concatenate([inputs["b_f"][0], inputs["ln1_g"][0], inputs["ln1_b"][0], inputs["ln2_g"][0],
                                               inputs["ln2_b"][0], inputs["b_ple_gate"][0]])[None, :], (128, 8 + 5 * D))),
        ropeall=_rope_table(np.arange(SEQ)),
        ropes=np.concatenate([_rope_table(PAST + np.arange(T)), np.zeros((64, 96), np.float32)], 0),
    )
    ident = np.eye(128, dtype=np.float32)
    U = (np.arange(128)[:, None] <= np.arange(128)[None, :]).astype(np.float32)
    E = np.zeros((128, 128), np.float32); E[127, :] = 1.0
    shared["consts"] = f(np.concatenate([ident, U, E], 1))
    maps = []
    for c in range(8):
        b, half = c // 2, c % 2
        rows = np.concatenate([np.arange(bl * 128, (bl + 1) * 128) for bl in _own_blocks(half)])
        fm, sm, pc = _masks(half)
        m = dict(shared)
        m.update(
            xall=f(inputs["x_prompt"][b]), xown=f(inputs["x_prompt"][b][rows]), pown=f(inputs["p_prompt"][0, b][rows]),
            xs=f(inputs["x_sample"][4 * c:4 * c + 4].reshape(SB * T, D)), pss=f(inputs["p_sample"][0, 4 * c:4 * c + 4].reshape(SB * T, 256)),
            cfk=f(inputs["cache_fox_k"][0, 4 * c:4 * c + 4].reshape(SB, PAST, 512)),
            cfv=f(inputs["cache_fox_v"][0, 4 * c:4 * c + 4].reshape(SB, PAST, 512)),
            clf=f(inputs["cache_fox_logf"][0, 4 * c:4 * c + 4]),
            cdk=f(inputs["cache_dsa_k"][0, 4 * c:4 * c + 4].reshape(SB, PAST, 512)),
            cdv=f(inputs["cache_dsa_v"][0, 4 * c:4 * c + 4].reshape(SB, PAST, 512)),
            cik=f(inputs["cache_idx_k"][0, 4 * c:4 * c + 4]),
            ropeown=f(shared["ropeall"][rows]), fmask=fm, smask=sm, pencap=pc,
        )
        maps.append(m)
    return maps


def kernel(**inputs):
    maps = _prep(inputs)
    nc, P, es = build()
    nc = _finish(nc, P, es)
    res = run_bass_kernel_spmd(nc, maps, core_ids=list(range(8)))
    R = res.results
    y_p = np.zeros((4, SEQ, D), np.float32)
    y_s = np.zeros((32, T, D), np.float32)
    for c in range(8):
        b, half = c // 2, c % 2
        rows = np.concatenate([np.arange(bl * 128, (bl + 1) * 128) for bl in _own_blocks(half)])
        y_p[b, rows] = R[c]["o_y"]
        y_s[4 * c:4 * c + 4] = R[c]["o_ys"].reshape(SB, T, D)

    def pk(name, last):
        return np.stack([R[2 * b][name] for b in range(4)], 0).reshape((1, 4, SEQ) + last)

    def sk(name, last):
        return np.concatenate([R[c][name].reshape((SB, T) + last) for c in range(8)], 0).reshape((1, 32, T) + last)

    return (y_p, y_s, pk("o_fk", (8, 64)), pk("o_fv", (8, 64)), pk("o_lf", (8,)), pk("o_dk", (8, 64)), pk("o_dv", (8, 64)),
            pk("o_ik", (32,)), sk("o_sfk", (8, 64)), sk("o_sfv", (8, 64)), sk("o_slf", (8,)), sk("o_sdk", (8, 64)),
            sk("o_sdv", (8, 64)), sk("o_sik", (32,)))
```

```python
import os
import types
import contextlib
import numpy as np
import concourse.bass as bass
import concourse.mybir as mybir
from concourse.bass_utils import run_bass_kernel_spmd

F32 = mybir.dt.float32
BF16 = mybir.dt.bfloat16
ALU = mybir.AluOpType
AF = mybir.ActivationFunctionType
AX = mybir.AxisListType

D = 1024
SEQ = 4096
NB = 32
NOWN = 16
NQT = 8
SB = 4
PAST = 1024
T = 64
NBS = 9
LS = NBS * 128
KC = 2088
QC = 1288
NEG = -1.0e30
BIG = 1.0e30
NITER = 16
ALPHA = 2.0 ** 0.25
STOP = 99


class Res:
    __slots__ = ("name", "w", "rs")

    def __init__(self, name):
        self.name = name
        self.w = None
        self.rs = []


class DSem:
    __slots__ = ("h", "total", "mk", "sub")

    def __init__(self, h=None, mk=None):
        self.h = h
        self.total = 0
        self.mk = mk
        self.sub = {}

    def for_queue(self, eng, prog):
        k = "sw" if eng == "pool" else "hw"
        if k not in self.sub:
            d = DSem(self.mk(), None)
            prog.dsems.append(d)
            self.sub[k] = d
        return self.sub[k]


class Op:
    __slots__ = ("eng", "fn", "waits", "needed", "val", "dsem")

    def __init__(self, eng, fn):
        self.eng = eng
        self.fn = fn
        self.waits = []
        self.needed = False
        self.val = 0
        self.dsem = None


def _freeze(fn):
    if fn.__closure__ is None:
        return fn
    cells = []
    for c in fn.__closure__:
        try:
            cells.append(types.CellType(c.cell_contents))
        except ValueError:
            cells.append(c)
    return types.FunctionType(fn.__code__, fn.__globals__, fn.__name__, fn.__defaults__, tuple(cells))


class Prog:
    ENGS = ("pe", "act", "dve", "pool", "sp")

    def __init__(self):
        self.q = {e: [] for e in self.ENGS}
        self.dsems = []
        self.pending = {e: None for e in self.ENGS}

    def new_dsem(self, mk):
        return DSem(None, mk)

    def _mk(self, eng, fn, reads, writes):
        o = Op(eng, _freeze(fn))
        deps = []
        for r in reads:
            if r.w is not None:
                deps.append(r.w)
        for r in writes:
            if r.w is not None:
                deps.append(r.w)
            deps.extend(r.rs)
        for d in deps:
            if d.dsem is not None:
                o.waits.append((d.dsem, d.dsem.total))
            else:
                if d.eng == eng and eng == "pe":
                    continue
                d.needed = True
                o.waits.append(d)
        pb = self.pending[eng]
        if pb is not None:
            o.waits.extend(pb)
            self.pending[eng] = None
        for r in reads:
            r.rs.append(o)
        for r in writes:
            r.w = o
            r.rs = []
        self.q[eng].append(o)
        return o

    def op(self, eng, fn, reads=(), writes=()):
        return self._mk(eng, fn, reads, writes)

    def dma(self, eng, dsem, out, in_, reads=(), writes=()):
        o = self._mk(eng, lambda e: e.dma_start(out=out, in_=in_), reads, writes)
        sub = dsem.for_queue(eng, self)
        o.dsem = sub
        sub.total += 16
        return o

    def barrier(self):
        snap = []
        for e in self.ENGS:
            for o in reversed(self.q[e]):
                if o.dsem is None:
                    o.needed = True
                    snap.append(o)
                    break
        for d in self.dsems:
            if d.total:
                snap.append((d, d.total))
        for e in self.ENGS:
            self.pending[e] = list(snap) + (self.pending[e] or [])

    def emit(self, nc, block, esem):
        for e in self.ENGS:
            c = 0
            for o in self.q[e]:
                if o.dsem is None and o.needed:
                    c += 1
                    o.val = c
        engobj = {"pe": "tensor", "act": "scalar", "dve": "vector", "pool": "gpsimd", "sp": "sync"}
        final = [(d, d.total) for d in self.dsems if d.total]

        def run(ename, eng):
            waited = {}
            for o in self.q[ename]:
                for w in o.waits:
                    if isinstance(w, tuple):
                        sem, val = w[0].h, w[1]
                    else:
                        sem, val = esem[w.eng], w.val
                    key = id(sem)
                    if waited.get(key, 0) >= val:
                        continue
                    waited[key] = val
                    eng.wait_ge(sem, val)
                ins = o.fn(eng)
                if o.dsem is not None:
                    ins.then_inc(o.dsem.h, 16)
                elif o.needed:
                    ins.then_inc(esem[ename], 1)
            if ename == "sp":
                for d, tot in final:
                    eng.wait_ge(d.h, tot)

        for ename in self.ENGS:
            getattr(block, engobj[ename])(lambda eng, _n=ename: run(_n, eng))


def build():
    nc = bass.Bass("TRN2", target_bir_lowering=False)
    es = contextlib.ExitStack()
    P = Prog()

    def din(name, shape):
        return nc.dram_tensor(name, list(shape), F32, kind="ExternalInput").ap()

    def dout(name, shape):
        return nc.dram_tensor(name, list(shape), F32, kind="ExternalOutput").ap()

    def dscr(name, shape, dt=BF16):
        return nc.dram_tensor(name, list(shape), dt).ap()

    def sbt(name, shape, dt):
        return es.enter_context(nc.sbuf_tensor(name, list(shape), dt))

    nsem = [0]

    def _mksem():
        nsem[0] += 1
        return es.enter_context(nc.semaphore("ds%d" % nsem[0]))

    def dsem():
        return P.new_dsem(_mksem)

    xall = din("xall", [SEQ, D]); xown = din("xown", [NOWN * 128, D]); pown = din("pown", [NOWN * 128, 256])
    xs = din("xs", [SB * T, D]); pss = din("pss", [SB * T, 256])
    cfk = din("cfk", [SB, PAST, 512]); cfv = din("cfv", [SB, PAST, 512]); clf = din("clf", [SB, PAST, 8])
    cdk = din("cdk", [SB, PAST, 512]); cdv = din("cdv", [SB, PAST, 512]); cik = din("cik", [SB, PAST, 32])
    wk = din("wk", [D, KC]); wq = din("wq", [D, QC]); wo = din("wo", [D, D]); wup = din("wup", [D, 4 * D])
    wdn = din("wdn", [4 * D, D]); wg = din("wg", [D, D]); wple = din("wple", [256, D])
    vecs = din("vecs", [128, 8 + 5 * D])
    ropeall = din("ropeall", [SEQ, 96]); ropeown = din("ropeown", [NOWN * 128, 96]); ropes = din("ropes", [128, 96])
    consts = din("consts", [128, 384])
    fmask_d = din("fmask", [128, 2 * 4 * 256]); smask_d = din("smask", [128, 64]); pencap_d = din("pencap", [128, 2 * 2 * 512])

    o_y = dout("o_y", [NOWN * 128, D]); o_ys = dout("o_ys", [SB * T, D])
    o_fk = dout("o_fk", [SEQ, 512]); o_fv = dout("o_fv", [SEQ, 512]); o_lf = dout("o_lf", [SEQ, 8])
    o_dk = dout("o_dk", [SEQ, 512]); o_dv = dout("o_dv", [SEQ, 512]); o_ik = dout("o_ik", [SEQ, 32])
    o_sfk = dout("o_sfk", [SB * T, 512]); o_sfv = dout("o_sfv", [SB * T, 512]); o_slf = dout("o_slf", [SB * T, 8])
    o_sdk = dout("o_sdk", [SB * T, 512]); o_sdv = dout("o_sdv", [SB * T, 512]); o_sik = dout("o_sik", [SB * T, 32])

    LMAX = SEQ
    s_kt = [dscr("s_kt%d" % i, [9, 128, SEQ if i == 0 else LS]) for i in range(1 + SB)]
    s_v = [dscr("s_v%d" % i, [128, NB if i == 0 else NBS, 2048]) for i in range(1 + SB)]
    s_ikt = [a_[8] for a_ in s_kt]
    NTOK = NOWN * 128 + SB * T
    s_mix = dscr("s_mix", [8, 128, NTOK])
    r_scr = [Res("scr%d" % i) for i in range(1 + SB)]
    r_smix = Res("smix")

    Bk = [es.enter_context(nc.psum_tensor("pb%d" % i, [128, 512], F32)) for i in range(6)]
    PTf = es.enter_context(nc.psum_tensor("ptf", [128, 1024], F32))
    PT = PTf[:, :].bitcast(BF16)
    Bk = [b_[:, :] for b_ in Bk] + [PTf[:, 0:512], PTf[:, 512:1024]]
    rB = [Res("pb%d" % i) for i in range(8)]
    rPT = Res("PT")

    cst = sbt("cst", [128, 384], F32)
    identb = sbt("identb", [128, 128], BF16)
    vec = sbt("vec", [128, 8], F32)
    r_cst = Res("cst"); r_identb = Res("identb"); r_vec = Res("vec")
    identf = cst[:, 0:128]; Umat = cst[:, 128:256]; Elast = cst[:, 256:384]
    bfb = vec[:, 0:8]
    LOGF = [sbt("LOGF%d" % i, [128, NB if i == 0 else NBS, 8], F32) for i in range(1 + SB)]
    r_LOGF = [Res("LOGF%d" % i) for i in range(1 + SB)]
    CK = [sbt("CK%d" % i, [128, NB if i == 0 else NBS, 8], F32) for i in range(1 + SB)]
    CARRY = [sbt("CARRY%d" % i, [128, NB if i == 0 else NBS, 8], F32) for i in range(1 + SB)]
    r_cum = [Res("cum%d" % i) for i in range(1 + SB)]
    r_cumtmp = Res("cumtmp")
    ones32 = sbt("ones32", [128, 32], F32); r_ones32 = Res("ones32")
    IW = sbt("IW", [128, NOWN + SB, 8], F32)
    r_Q = Res("Qside")

    ARENA_BYTES = 203648
    arena = sbt("arena", [128, ARENA_BYTES // 2], BF16)
    apos = [0]

    def aview(shape, dt):
        n = 1
        for s_ in shape[1:]:
            n *= s_
        nbytes = n * (4 if dt == F32 else 2)
        nbytes = (nbytes + 63) // 64 * 64
        off = apos[0]
        apos[0] += nbytes
        assert apos[0] <= ARENA_BYTES, ("arena overflow", apos[0])
        a = arena[:, off // 2:(off + n * (4 if dt == F32 else 2)) // 2]
        if dt == F32:
            a = a.bitcast(F32)
        if len(shape) == 3:
            a = a.rearrange("p (a b) -> p a b", b=shape[2])
        elif len(shape) == 4:
            a = a.rearrange("p (a b c) -> p a b c", b=shape[2], c=shape[3])
        return a

    d_init = dsem()

    P.dma("sp", d_init, cst[:], consts[:, :], writes=[r_cst])
    P.dma("sp", d_init, vec[:], vecs[:, 0:8], writes=[r_vec])
    P.dma("pool", d_init, identb[:], consts[:, 0:128], writes=[r_identb])
    P.op("dve", lambda e: e.memset(ones32[:], 1.0), writes=[r_ones32])

    apos[0] = 0
    QTf = aview([128, 4, NTOK], BF16); QTd = aview([128, 4, NTOK], BF16); IQT = aview([128, 4, NTOK], BF16)
    APOS_AB = apos[0]
    WK = aview([128, 8, KC], BF16); WQ = aview([128, 8, QC], BF16)
    r_WK = Res("WK"); r_WQ = Res("WQ")
    xin = [aview([128, D], F32) for _ in range(2)]; r_xin = [Res("xin%d" % i) for i in range(2)]
    xT = [aview([128, 8, 128], BF16) for _ in range(2)]; r_xT = [Res("xT%d" % i) for i in range(2)]
    hb = [aview([128, KC], F32) for _ in range(3)]; r_hb = [Res("hb%d" % i) for i in range(3)]
    kb16 = [aview([128, 1536], BF16) for _ in range(2)]; r_kb16 = [Res("kb16%d" % i) for i in range(2)]
    vb16 = [aview([128, 2048], BF16) for _ in range(2)]; r_vb16 = [Res("vb16%d" % i) for i in range(2)]
    ktst = [aview([128, 1152], BF16) for _ in range(2)]; r_ktst = [Res("ktst%d" % i) for i in range(2)]
    rp = [aview([128, 96], F32) for _ in range(4)]; r_rp = [Res("rp%d" % i) for i in range(4)]
    d_rp = [dsem() for _ in range(4)]
    rt = [aview([128, 512], F32) for _ in range(4)]; r_rt = [Res("rt%d" % i) for i in range(4)]
    lft = [aview([128, 8], F32) for _ in range(2)]; r_lft = Res("lft")
    cumtmp = aview([128, 3, NB * 8], F32)
    d_w = dsem(); d_x = [dsem(), dsem()]; d_hb = [dsem(), dsem(), dsem()]; d_st = [dsem(), dsem()]

    wkv = wk.rearrange("(k p) n -> p k n", p=128)
    wqv = wq.rearrange("(k p) n -> p k n", p=128)
    for k in range(8):
        P.dma("pool", d_w, WK[:, k, :], wkv[:, k, :], writes=[r_WK])
    for k in range(8):
        P.dma("pool", d_w, WQ[:, k, :], wqv[:, k, :], writes=[r_WQ])
    for s_ in range(2):
        P.op("pool", lambda e, s_=s_: e.memset(kb16[s_][:, :], 0.0), writes=[r_kb16[s_]])
        P.op("pool", lambda e, s_=s_: e.memset(vb16[s_][:, :], 1.0), writes=[r_vb16[s_]])

    def load_x(slot, xrows, roperows, n=128, rs_=0):
        if n < 128:
            P.op("pool", lambda e: e.memset(xin[slot][:, :], 0.0), writes=[r_xin[slot]])
        P.dma("sp", d_x[slot], xin[slot][0:n, :], xrows, writes=[r_xin[slot]])
        P.dma("sp", d_rp[rs_], rp[rs_][:, :], roperows, writes=[r_rp[rs_]])

    def transpose_x(slot):
        for half in range(2):
            bk = half
            for kc in range(4):
                c = half * 4 + kc
                P.op("pe", lambda e, c=c, kc=kc, bk=bk: e.transpose(out=Bk[bk][:, kc * 128:(kc + 1) * 128],
                                                              in_=xin[slot][:, c * 128:(c + 1) * 128], identity=identf),
                     reads=[r_xin[slot], r_cst], writes=[rB[bk]])
            dst = xT[slot][:, half * 4:(half + 1) * 4, :]
            src = Bk[bk][:, :].rearrange("p (a b) -> p a b", b=128)
            if half == 0:
                P.op("act", lambda e, dst=dst, src=src: e.copy(out=dst, in_=src), reads=[rB[bk]], writes=[r_xT[slot]])
            else:
                P.op("dve", lambda e, dst=dst, src=src: e.tensor_copy(out=dst, in_=src), reads=[rB[bk]], writes=[r_xT[slot]])

    def project(slot, W, rW, ncols, dst, rdst):
        c0 = 0
        ci = 0
        while c0 < ncols:
            cw = min(512, ncols - c0)
            bk = 2 + (ci % 2)
            for kc in range(8):
                P.op("pe", lambda e, kc=kc, c0=c0, cw=cw, bk=bk: e.matmul(out=Bk[bk][:, 0:cw], lhsT=xT[slot][:, kc, :],
                                                                      rhs=W[:, kc, c0:c0 + cw], start=(kc == 0), stop=(kc == 7)),
                     reads=[r_xT[slot], rW], writes=[rB[bk]])
            if ci % 2 == 0:
                P.op("act", lambda e, c0=c0, cw=cw, bk=bk: e.copy(out=dst[:, c0:c0 + cw], in_=Bk[bk][:, 0:cw]),
                     reads=[rB[bk]], writes=[rdst])
            else:
                P.op("dve", lambda e, c0=c0, cw=cw, bk=bk: e.tensor_copy(out=dst[:, c0:c0 + cw], in_=Bk[bk][:, 0:cw]),
                     reads=[rB[bk]], writes=[rdst])
            c0 += cw
            ci += 1

    def rope(buf, rbuf, H, half, cosap, sinap, rrope):
        v = buf.rearrange("p (h t d) -> p h t d", t=2, d=half)
        x1 = v[:, :, 0, :]; x2 = v[:, :, 1, :]
        cb = cosap.unsqueeze(1).to_broadcast([128, H, half])
        sb_ = sinap.unsqueeze(1).to_broadcast([128, H, half])
        tv = [rt[i][:, 0:H * half].rearrange("p (h d) -> p h d", d=half) for i in range(4)]
        P.op("dve", lambda e: e.tensor_tensor(out=tv[0], in0=x1, in1=cb, op=ALU.mult), reads=[rbuf, rrope], writes=[r_rt[0]])
        P.op("dve", lambda e: e.tensor_tensor(out=tv[1], in0=x2, in1=sb_, op=ALU.mult), reads=[rbuf, rrope], writes=[r_rt[1]])
        P.op("dve", lambda e: e.tensor_tensor(out=tv[2], in0=x2, in1=cb, op=ALU.mult), reads=[rbuf, rrope], writes=[r_rt[2]])
        P.op("dve", lambda e: e.tensor_tensor(out=tv[3], in0=x1, in1=sb_, op=ALU.mult), reads=[rbuf, rrope], writes=[r_rt[3]])
        P.op("dve", lambda e: e.tensor_tensor(out=x1, in0=tv[0], in1=tv[1], op=ALU.subtract), reads=[r_rt[0], r_rt[1]], writes=[rbuf])
        P.op("dve", lambda e: e.tensor_tensor(out=x2, in0=tv[2], in1=tv[3], op=ALU.add), reads=[r_rt[2], r_rt[3]], writes=[rbuf])

    def kside_finish(hs, rs_):
        h_ = hb[hs]
        z = lft[0]; z2 = lft[1]
        P.op("dve", lambda e: e.tensor_tensor(out=z[:, :], in0=h_[:, 2080:2088], in1=bfb, op=ALU.add),
             reads=[r_hb[hs], r_vec], writes=[r_lft])
        P.op("act", lambda e: e.activation(out=z2[:, :], in_=z[:, :], func=AF.Exp, scale=-1.0), reads=[r_lft], writes=[r_lft])
        P.op("act", lambda e: e.activation(out=z[:, :], in_=z2[:, :], func=AF.Ln, bias=1.0), reads=[r_lft], writes=[r_lft])
        P.op("dve", lambda e: e.tensor_scalar(out=h_[:, 2080:2088], in0=z[:, :], scalar1=-1.0, scalar2=None, op0=ALU.mult),
             reads=[r_lft], writes=[r_hb[hs]])
        rope(h_[:, 512:1024], r_hb[hs], 8, 32, rp[rs_][:, 0:32], rp[rs_][:, 32:64], r_rp[rs_])
        rope(h_[:, 2048:2080], r_hb[hs], 1, 16, rp[rs_][:, 64:80], rp[rs_][:, 80:96], r_rp[rs_])

    def kside_post(hs, ks, seq, blk):
        h_ = hb[hs]
        P.op("dve", lambda e: e.tensor_copy(out=LOGF[seq][:, blk, :], in_=h_[:, 2080:2088]), reads=[r_hb[hs]], writes=[r_LOGF[seq]])
        P.op("act", lambda e: e.copy(out=kb16[ks][:, 0:512], in_=h_[:, 0:512]), reads=[r_hb[hs]], writes=[r_kb16[ks]])
        P.op("dve", lambda e: e.tensor_copy(out=kb16[ks][:, 512:1024], in_=h_[:, 512:1024]), reads=[r_hb[hs]], writes=[r_kb16[ks]])
        ikd = kb16[ks][:, 1024:1152].rearrange("p (a b) -> p a b", b=64)[:, :, 0:32]
        iks = h_[:, 2048:2080].unsqueeze(1).to_broadcast([128, 2, 32])
        P.op("dve", lambda e: e.tensor_copy(out=ikd, in_=iks), reads=[r_hb[hs]], writes=[r_kb16[ks]])
        vdst = vb16[ks][:, :].rearrange("p (t h d) -> p t h d", t=2, d=128)[:, :, :, 0:64]
        vsrc_ = h_[:, 1024:2048].rearrange("p (t h d) -> p t h d", t=2, d=64)
        P.op("act", lambda e: e.copy(out=vdst, in_=vsrc_), reads=[r_hb[hs]], writes=[r_vb16[ks]])
        for c in range(9):
            P.op("pe", lambda e, c=c: e.transpose(out=PT[:, c * 128:(c + 1) * 128], in_=kb16[ks][:, c * 128:(c + 1) * 128],
                                                  identity=identb[:, :]),
                 reads=[r_kb16[ks], r_identb], writes=[rPT])
        P.op("act", lambda e: e.copy(out=ktst[ks][:, :], in_=PT[:, 0:1152]), reads=[rPT], writes=[r_ktst[ks]])
        cols = slice(blk * 128, (blk + 1) * 128)
        P.dma("act", d_st[ks], s_kt[seq].rearrange("h p l -> p h l")[:, :, cols],
              ktst[ks][:, 0:1152].rearrange("p (h l) -> p h l", l=128), reads=[r_ktst[ks]], writes=[r_scr[seq]])
        P.dma("act", d_st[ks], s_v[seq][:, blk, :], vb16[ks][:, 0:2048], reads=[r_vb16[ks]], writes=[r_scr[seq]])

    def kside_outputs(hs, outs, rows, n=128):
        ofk, ofv, olf, odk, odv, oik = outs
        h_ = hb[hs]
        for (o_, a, b) in ((ofk, 0, 512), (odk, 512, 1024), (ofv, 1024, 1536), (odv, 1536, 2048), (oik, 2048, 2080), (olf, 2080, 2088)):
            P.dma("pool", d_hb[hs], o_[rows, :], h_[0:n, a:b], reads=[r_hb[hs]])

    def qside_finish(hs, rs_, ks, tokoff, iwidx, nr=128):
        h_ = hb[hs]; rr = r_hb[hs]
        rope(h_[:, 512:1024], rr, 8, 32, rp[rs_][:, 0:32], rp[rs_][:, 32:64], r_rp[rs_])
        rope(h_[:, 1024:1280], rr, 8, 16, rp[rs_][:, 64:80], rp[rs_][:, 80:96], r_rp[rs_])
        P.op("dve", lambda e: e.tensor_scalar(out=IW[:, iwidx, :], in0=h_[:, 1280:1288], scalar1=1.0 / 16.0, scalar2=None, op0=ALU.mult),
             reads=[rr], writes=[r_Q])
        kq = kb16[ks]; rk = r_kb16[ks]
        P.op("act", lambda e: e.copy(out=kq[:, 0:512], in_=h_[:, 0:512]), reads=[rr], writes=[rk])
        P.op("dve", lambda e: e.tensor_copy(out=kq[:, 512:1024], in_=h_[:, 512:1024]), reads=[rr], writes=[rk])
        iqd = kq[:, 1024:1536].rearrange("p (a b) -> p a b", b=64)[:, :, 0:32]
        iqs = h_[:, 1024:1280].rearrange("p (a b) -> p a b", b=32)
        P.op("dve", lambda e: e.tensor_copy(out=iqd, in_=iqs), reads=[rr], writes=[rk])

    def qside_post(ks, tokoff, nr=128):
        kq = kb16[ks]; rk = r_kb16[ks]
        for c in range(12):
            P.op("pe", lambda e, c=c: e.transpose(out=PT[:, c * 128:(c + 1) * 128], in_=kq[:, c * 128:(c + 1) * 128], identity=identb[:, :]),
                 reads=[rk, r_identb], writes=[rPT])
        for i_, (dst, eng) in enumerate(((QTf, "act"), (QTd, "dve"), (IQT, "act"))):
            src = PT[:, i_ * 512:(i_ + 1) * 512].rearrange("p (a b) -> p a b", b=128)[:, :, 0:nr]
            d_ = dst[:, :, tokoff:tokoff + nr]
            if eng == "act":
                P.op("act", lambda e, d_=d_, src=src: e.copy(out=d_, in_=src), reads=[rPT], writes=[r_Q])
            else:
                P.op("dve", lambda e, d_=d_, src=src: e.tensor_copy(out=d_, in_=src), reads=[rPT], writes=[r_Q])

    prompt_outs = (o_fk, o_fv, o_lf, o_dk, o_dv, o_ik)
    sample_outs = (o_sfk, o_sfv, o_slf, o_sdk, o_sdv, o_sik)
    items = []
    nitem = [0]
    rs_cur = [0]

    def slots():
        i = nitem[0]; nitem[0] += 1
        rs_cur[0] = i % 4
        return i % 3, i % 2, i % 2

    for i in range(NB):
        hs, xs_, ks = slots()
        rs_ = rs_cur[0]
        rows = slice(i * 128, (i + 1) * 128)

        def s1(hs=hs, xs_=xs_, rows=rows, rs_=rs_):
            load_x(xs_, xall[rows, :], ropeall[rows, :], rs_=rs_)
            transpose_x(xs_)
            project(xs_, WK, r_WK, KC, hb[hs], r_hb[hs])

        def s2(hs=hs, rs_=rs_, rows=rows):
            kside_finish(hs, rs_)

        def s3(hs=hs, ks=ks, i=i, rows=rows):
            kside_outputs(hs, prompt_outs, rows)
            kside_post(hs, ks, 0, i)
        items.append((s1, s2, s3))
    for i in range(NOWN):
        hs, xs_, ks = slots()
        rs_ = rs_cur[0]
        rows = slice(i * 128, (i + 1) * 128)

        def s1(hs=hs, xs_=xs_, rows=rows, rs_=rs_):
            load_x(xs_, xown[rows, :], ropeown[rows, :], rs_=rs_)
            transpose_x(xs_)
            project(xs_, WQ, r_WQ, QC, hb[hs], r_hb[hs])

        def s2(hs=hs, rs_=rs_, ks=ks, i=i):
            qside_finish(hs, rs_, ks, i * 128, i)

        def s3(ks=ks, i=i):
            qside_post(ks, i * 128)
        items.append((s1, s2, s3))
    for sbi in range(SB):
        seq = 1 + sbi
        for blk in range(8):
            hs, xs_, ks = slots()
            rows = slice(blk * 128, (blk + 1) * 128)

            def s1(hs=hs, sbi=sbi, rows=rows):
                for (src, a, b) in ((cfk, 0, 512), (cdk, 512, 1024), (cfv, 1024, 1536), (cdv, 1536, 2048), (cik, 2048, 2080), (clf, 2080, 2088)):
                    P.dma("sp", d_hb[hs], hb[hs][:, a:b], src[sbi, rows, :], writes=[r_hb[hs]])

            def s3(hs=hs, ks=ks, seq=seq, blk=blk):
                kside_post(hs, ks, seq, blk)
            items.append((s1, None, s3))
        hs, xs_, ks = slots()
        rs_ = rs_cur[0]
        rows = slice(sbi * T, (sbi + 1) * T)

        def s1(hs=hs, xs_=xs_, rows=rows, rs_=rs_):
            load_x(xs_, xs[rows, :], ropes[:, :], n=T, rs_=rs_)
            transpose_x(xs_)
            project(xs_, WK, r_WK, KC, hb[hs], r_hb[hs])

        def s2(hs=hs, rs_=rs_, rows=rows):
            kside_finish(hs, rs_)

        def s3(hs=hs, ks=ks, seq=seq, rows=rows):
            kside_outputs(hs, sample_outs, rows, n=T)
            kside_post(hs, ks, seq, 8)
        items.append((s1, s2, s3))
        xs_k = xs_
        rs_k = rs_
        hs, _unused, ks = slots()

        def s1(hs=hs, xs_k=xs_k):
            project(xs_k, WQ, r_WQ, QC, hb[hs], r_hb[hs])

        def s2(hs=hs, rs_k=rs_k, ks=ks, sbi=sbi):
            qside_finish(hs, rs_k, ks, NOWN * 128 + sbi * T, NOWN + sbi, nr=T)

        def s3(ks=ks, sbi=sbi):
            qside_post(ks, NOWN * 128 + sbi * T, nr=T)
        items.append((s1, s2, s3))
        nitem[0] += 0
    for t in range(len(items) + 2):
        if t < len(items) and items[t][0] is not None:
            items[t][0]()
        if 0 <= t - 1 < len(items) and items[t - 1][1] is not None:
            items[t - 1][1]()
        if 0 <= t - 2 < len(items) and items[t - 2][2] is not None:
            items[t - 2][2]()

    def cum(seq, nb):
        n8 = nb * 8
        lf2 = LOGF[seq][:, :, :].rearrange("p a b -> p (a b)")
        PF = cumtmp[:, 0, 0:n8]; TBt = cumtmp[:, 1, 0:n8]; INC = cumtmp[:, 2, 0:n8]
        P.op("pe", lambda e: e.matmul(out=Bk[4][:, 0:n8], lhsT=Umat, rhs=lf2, start=True, stop=True),
             reads=[r_LOGF[seq], r_cst], writes=[rB[4]])
        P.op("dve", lambda e: e.tensor_copy(out=PF, in_=Bk[4][:, 0:n8]), reads=[rB[4]], writes=[r_cumtmp])
        P.op("pe", lambda e: e.matmul(out=Bk[5][:, 0:n8], lhsT=Elast, rhs=PF, start=True, stop=True),
             reads=[r_cumtmp, r_cst], writes=[rB[5]])
        P.op("dve", lambda e: e.tensor_copy(out=TBt, in_=Bk[5][:, 0:n8]), reads=[rB[5]], writes=[r_cumtmp])
        TB3 = TBt.rearrange("p (a b) -> p a b", b=8); INC3 = INC.rearrange("p (a b) -> p a b", b=8)
        for h in range(8):
            P.op("dve", lambda e, h=h: e.tensor_tensor_scan(out=INC3[:, :, h], data0=ones32[:, 0:nb], data1=TB3[:, :, h], initial=0.0,
                                                            op0=ALU.mult, op1=ALU.add),
                 reads=[r_cumtmp, r_ones32], writes=[r_cumtmp])
        P.op("dve", lambda e: e.memset(CARRY[seq][:, 0, :], 0.0), writes=[r_cum[seq]])
        P.op("dve", lambda e: e.tensor_copy(out=CARRY[seq][:, 1:nb, :], in_=INC3[:, 0:nb - 1, :]), reads=[r_cumtmp], writes=[r_cum[seq]])
        P.op("dve", lambda e: e.tensor_tensor(out=CK[seq][:, :, :], in0=PF.rearrange("p (a b) -> p a b", b=8), in1=CARRY[seq][:, :, :], op=ALU.add),
             reads=[r_cumtmp, r_cum[seq]], writes=[r_cum[seq]])

    cum(0, NB)
    for sbi in range(SB):
        cum(1 + sbi, NBS)


    def phase_b():
        apos[0] = APOS_AB
        IKTs = aview([128, SEQ], BF16); r_IKT = Res("IKTs")
        KT = [aview([128, SEQ], BF16) for _ in range(2)]; r_KT = [Res("KT%d" % i) for i in range(2)]
        VA = [aview([128, NB, 256], BF16) for _ in range(2)]; r_VA = [Res("VA%d" % i) for i in range(2)]
        SC = aview([128, SEQ], F32); r_SC = Res("SC")
        MQ = aview([128, SEQ], BF16); r_MQ = Res("MQ")
        rtmp = [aview([128, 512], F32) for _ in range(2)]; r_rtmp = [Res("rtmp%d" % i) for i in range(2)]
        maskTs = [aview([128, NB, 256], BF16) for _ in range(2)]; r_maskTs = [Res("maskT%d" % i) for i in range(2)]
        NPT = 4
        PTs = [aview([128, 512], BF16) for _ in range(NPT)]; r_PTs = [Res("PTs%d" % i) for i in range(NPT)]
        Qpad = [aview([128, 512], BF16) for _ in range(2)]; r_Qpad = [Res("Qpad%d" % i) for i in range(2)]
        rd = [aview([128, 256], F32) for _ in range(2)]; r_rd = [Res("rd%d" % i) for i in range(2)]
        fmask = aview([128, 2, 4, 256], BF16); smaskb = aview([128, 64], BF16); pencap = aview([128, 2, 2, 512], F32)
        r_msk = Res("masks")
        BIASs = [aview([128, NB, 8], F32) for _ in range(2)]; r_BIASs = [Res("BIAS%d" % i) for i in range(2)]
        bst = aview([128, 64], F32); r_bst = Res("bst")
        pow2 = aview([128, NITER], F32); r_pow2 = Res("pow2")
        mixst = [aview([128, 8, 256], BF16) for _ in range(1)]; r_mixst = [Res("mixst%d" % i) for i in range(1)]
        d_ikt = dsem(); d_kv = [dsem(), dsem()]; d_m = dsem(); d_mx = [dsem(), dsem()]

        P.dma("pool", d_m, fmask[:, :, :, :].rearrange("p a b c -> p (a b c)"), fmask_d[:, :], writes=[r_msk])
        P.dma("pool", d_m, smaskb[:, :], smask_d[:, :], writes=[r_msk])
        P.dma("sp", d_m, pencap[:, :, :, :].rearrange("p a b c -> p (a b c)"), pencap_d[:, :], writes=[r_msk])
        for k in range(NITER):
            P.op("pool", lambda e, k=k: e.memset(pow2[:, k:k + 1], 2.0 ** (-(k + 1))), writes=[r_pow2])
        rmax = bst[:, 0:1]; rmin = bst[:, 1:2]; tcur = bst[:, 2:3]; cntc = bst[:, 3:4]; sg = bst[:, 4:5]; Rr = bst[:, 5:6]
        stp = bst[:, 8:8 + NITER]
        cnts = {"kv": 0, "ix": 0, "pt": 0, "o": 0, "mx": 0, "mk": 0, "rk": 0}

        def qtile(seq, nkb, nq, qoff, slots, cref, jpar, tokoff, tileno, ikt_load=None):
            L = nkb * 128
            prompt = (seq == 0)
            tb_ = tileno % 2
            maskT = maskTs[tb_]; r_maskT = r_maskTs[tb_]; BIAS = BIASs[tb_]; r_BIAS = r_BIASs[tb_]

            def idx():
                yield "u"
                if ikt_load is not None:
                    ikt_load()
                P.op("dve", lambda e: e.tensor_tensor(out=BIAS[:, 0:nkb, :], in0=CARRY[seq][:, cref:cref + 1, :].to_broadcast([128, nkb, 8]),
                                                      in1=CK[seq][:, 0:nkb, :], op=ALU.subtract),
                     reads=[r_cum[seq]], writes=[r_BIAS])
                for si, (nr, iwidx) in enumerate(slots):
                    qc0 = qoff + si * 128
                    for lc in range(0, L, 512):
                        lw = min(512, L - lc)
                        for h in range(8):
                            yield "u"
                            hp = h // 2; po = (h % 2) * 64
                            k = cnts["ix"] % 2; cnts["ix"] += 1
                            bk = 4 + k
                            P.op("pe", lambda e, hp=hp, po=po, bk=bk, lc=lc, lw=lw: e.matmul(
                                out=Bk[bk][0:nr, 0:lw], lhsT=IQT[po:po + 32, hp, qc0:qc0 + nr], rhs=IKTs[po:po + 32, lc:lc + lw],
                                start=True, stop=True), reads=[r_Q, r_IKT], writes=[rB[bk]])
                            P.op("act", lambda e, k=k, bk=bk, lw=lw: e.activation(out=rtmp[k][0:nr, 0:lw], in_=Bk[bk][0:nr, 0:lw], func=AF.Relu),
                                 reads=[rB[bk]], writes=[r_rtmp[k]])
                            wcol = IW[0:nr, iwidx, h:h + 1]
                            if h == 0:
                                P.op("dve", lambda e, k=k, lc=lc, lw=lw, wcol=wcol: e.tensor_scalar(
                                    out=SC[0:nr, lc:lc + lw], in0=rtmp[k][0:nr, 0:lw], scalar1=wcol, scalar2=None, op0=ALU.mult),
                                    reads=[r_rtmp[k], r_Q], writes=[r_SC])
                            else:
                                eng = "dve"
                                P.op(eng, lambda e, k=k, lc=lc, lw=lw, wcol=wcol: e.scalar_tensor_tensor(
                                    out=SC[0:nr, lc:lc + lw], in0=rtmp[k][0:nr, 0:lw], scalar=wcol, in1=SC[0:nr, lc:lc + lw],
                                    op0=ALU.mult, op1=ALU.add), reads=[r_rtmp[k], r_Q, r_SC], writes=[r_SC])
                    P.op("dve", lambda e: e.tensor_reduce(out=rmax[0:nr], in_=SC[0:nr, 0:L], axis=AX.X, op=ALU.max), reads=[r_SC], writes=[r_bst])
                    P.op("dve", lambda e: e.tensor_reduce(out=rmin[0:nr], in_=SC[0:nr, 0:L], axis=AX.X, op=ALU.min), reads=[r_SC], writes=[r_bst])
                    if prompt:
                        P.op("dve", lambda e, si=si: e.tensor_tensor(out=SC[0:nr, L - 512:L], in0=SC[0:nr, L - 512:L], in1=pencap[0:nr, jpar, si, :], op=ALU.min),
                             reads=[r_SC, r_msk], writes=[r_SC])
                    else:
                        P.op("dve", lambda e: e.memset(SC[0:nr, PAST + T:L], NEG), reads=[r_SC], writes=[r_SC])
                    P.op("dve", lambda e: e.scalar_tensor_tensor(out=Rr[0:nr], in0=rmax[0:nr], scalar=2.0, in1=rmin[0:nr], op0=ALU.add, op1=ALU.subtract),
                         reads=[r_bst], writes=[r_bst])
                    P.op("dve", lambda e: e.scalar_tensor_tensor(out=tcur[0:nr], in0=rmax[0:nr], scalar=0.5, in1=rmin[0:nr], op0=ALU.mult, op1=ALU.add),
                         reads=[r_bst], writes=[r_bst])
                    P.op("dve", lambda e: e.scalar_tensor_tensor(out=tcur[0:nr], in0=rmin[0:nr], scalar=-0.5, in1=tcur[0:nr], op0=ALU.mult, op1=ALU.add),
                         reads=[r_bst], writes=[r_bst])
                    P.op("dve", lambda e: e.tensor_scalar(out=stp[0:nr, :], in0=pow2[0:nr, :], scalar1=Rr[0:nr], scalar2=None, op0=ALU.mult),
                         reads=[r_bst, r_pow2], writes=[r_bst])
                    for it in range(NITER):
                        yield "u"
                        P.op("dve", lambda e: e.tensor_scalar(out=MQ[0:nr, 0:L], in0=SC[0:nr, 0:L], scalar1=tcur[0:nr], scalar2=0.0,
                                                              op0=ALU.is_ge, op1=ALU.add, accum_out=cntc[0:nr]),
                             reads=[r_SC, r_bst], writes=[r_MQ, r_bst])
                        P.op("dve", lambda e: e.tensor_scalar(out=sg[0:nr], in0=cntc[0:nr], scalar1=255.5, scalar2=0.5, op0=ALU.is_ge, op1=ALU.subtract),
                             reads=[r_bst], writes=[r_bst])
                        P.op("dve", lambda e, it=it: e.scalar_tensor_tensor(out=tcur[0:nr], in0=sg[0:nr], scalar=stp[0:nr, it:it + 1], in1=tcur[0:nr],
                                                                            op0=ALU.mult, op1=ALU.add), reads=[r_bst], writes=[r_bst])
                    P.op("dve", lambda e: e.tensor_scalar(out=MQ[0:nr, 0:L], in0=SC[0:nr, 0:L], scalar1=tcur[0:nr], scalar2=None, op0=ALU.is_ge),
                         reads=[r_SC, r_bst], writes=[r_MQ])
                    for kb0 in range(0, nkb, 8):
                        yield "T"
                        g = min(8, nkb - kb0)
                        for kk in range(g):
                            kb = kb0 + kk
                            P.op("pe", lambda e, kb=kb, kk=kk: e.transpose(out=PT[:, 1024 + kk * 128:1024 + kk * 128 + nr], in_=MQ[0:nr, kb * 128:(kb + 1) * 128],
                                                                           identity=identb[0:nr, 0:nr]), reads=[r_MQ, r_identb], writes=[rB[7]])
                        src = PT[:, 1024:1024 + g * 128].rearrange("p (a b) -> p a b", b=128)[:, :, 0:nr]
                        dst = maskT[:, kb0:kb0 + g, si * 128:si * 128 + nr]
                        P.op("act", lambda e, src=src, dst=dst: e.activation(out=dst, in_=src, func=AF.Identity, bias=-30000.0, scale=30000.0),
                             reads=[rB[7]], writes=[r_maskT])

            def attn():
                mxs = 0
                sbanks = (0, 1, 6)
                pairs = [(tp, hp) for tp in range(2) for hp in range(4)]
                pair_kv = {}
                n2 = 2 * nq

                def load_pair(pi_):
                    tp, hp = pairs[pi_]
                    kv = cnts["kv"] % 2; cnts["kv"] += 1
                    pair_kv[pi_] = kv
                    P.dma("sp", d_kv[kv], KT[kv][:, 0:L], s_kt[seq][tp * 4 + hp, :, 0:L], reads=[r_scr[seq]], writes=[r_KT[kv]])
                    P.dma("sp", d_kv[kv], VA[kv][:, 0:nkb, :], s_v[seq][:, 0:nkb, tp * 1024 + hp * 256:tp * 1024 + (hp + 1) * 256],
                          reads=[r_scr[seq]], writes=[r_VA[kv]])
                    Qsrc = QTf if tp == 0 else QTd
                    qp = Qpad[kv]
                    P.op("pool", lambda e: e.memset(qp[:, 0:n2], 0.0), writes=[r_Qpad[kv]])
                    P.op("pool", lambda e: e.tensor_copy(out=qp[0:64, 0:nq], in_=Qsrc[0:64, hp, qoff:qoff + nq]), reads=[r_Q], writes=[r_Qpad[kv]])
                    P.op("pool", lambda e: e.tensor_copy(out=qp[64:128, nq:n2], in_=Qsrc[64:128, hp, qoff:qoff + nq]), reads=[r_Q], writes=[r_Qpad[kv]])

                steps = []
                for pi_, (tp, hp) in enumerate(pairs):
                    ob = 2 + (cnts["o"] % 2); cnts["o"] += 1
                    for kb in range(nkb):
                        steps.append((pi_, tp, hp, kb, ob))

                def s_step(st_, si_):
                    pi_, tp, hp, kb, ob = st_
                    kv = pair_kv[pi_]
                    sb_ = sbanks[si_ % 3]
                    pi = cnts["pt"] % NPT; cnts["pt"] += 1
                    P.op("pe", lambda e: e.matmul(out=Bk[sb_][:, 0:n2], lhsT=KT[kv][:, kb * 128:(kb + 1) * 128],
                                                  rhs=Qpad[kv][:, 0:n2], start=True, stop=(tp == 0), skip_group_check=True),
                         reads=[r_KT[kv], r_Qpad[kv]], writes=[rB[sb_]])
                    if tp == 1:
                        for ho in range(2):
                            P.op("pe", lambda e, ho=ho: e.matmul(out=Bk[sb_][:, ho * nq:(ho + 1) * nq], lhsT=identb[:, :], rhs=maskT[:, kb, 0:nq],
                                                                 start=False, stop=(ho == 1), skip_group_check=True),
                                 reads=[r_identb, r_maskT], writes=[rB[sb_]])
                    return pi, sb_

                def e_step(st_, pi, sb_):
                    pi_, tp, hp, kb, ob = st_
                    if tp == 0:
                        for ho in range(2):
                            h = 2 * hp + ho
                            P.op("act", lambda e, ho=ho, h=h: e.activation(out=PTs[pi][:, ho * nq:(ho + 1) * nq], in_=Bk[sb_][:, ho * nq:(ho + 1) * nq],
                                                                        func=AF.Exp, bias=BIAS[:, kb, h:h + 1], scale=0.125),
                                 reads=[rB[sb_], r_BIAS], writes=[r_PTs[pi]])
                    else:
                        P.op("act", lambda e: e.activation(out=PTs[pi][:, 0:n2], in_=Bk[sb_][:, 0:n2], func=AF.Exp, scale=0.125),
                             reads=[rB[sb_]], writes=[r_PTs[pi]])
                    mk = None
                    if tp == 1:
                        pass
                    elif prompt and kb >= nkb - 4:
                        mk = fmask[:, jpar, kb - (nkb - 4), :]; rmk = r_msk
                    elif (not prompt) and kb == nkb - 1:
                        mk = smaskb[:, :]; rmk = r_msk
                    if mk is not None:
                        pv_ = PTs[pi][:, 0:n2].rearrange("p (a b) -> p a b", b=nq)
                        mkb = mk.unsqueeze(1).to_broadcast([128, 2, nq])
                        P.op("dve", lambda e: e.tensor_tensor(out=pv_, in0=pv_, in1=mkb, op=ALU.mult),
                             reads=[r_PTs[pi], rmk], writes=[r_PTs[pi]])

                def pv_step(st_, pi):
                    pi_, tp, hp, kb, ob = st_
                    kv = pair_kv[pi_]
                    for ho in range(2):
                        P.op("pe", lambda e, ho=ho: e.matmul(out=Bk[ob][:, ho * nq:(ho + 1) * nq], lhsT=VA[kv][:, kb, ho * 128:(ho + 1) * 128],
                                                             rhs=PTs[pi][:, ho * nq:(ho + 1) * nq], start=(kb == 0 and ho == 0),
                                                             stop=(kb == nkb - 1 and ho == 1), skip_group_check=True),
                             reads=[r_VA[kv], r_PTs[pi]], writes=[rB[ob]])
                    if kb == nkb - 1:
                        for ho in range(2):
                            po = ho * 64
                            rk = cnts["rk"] % 2; cnts["rk"] += 1
                            if tp == 1:
                                P.op("act", lambda e, ho=ho, rk=rk: e.activation(out=rd[rk][0:64, 0:nq], in_=Bk[ob][64:128, ho * nq:(ho + 1) * nq], func=AF.Ln),
                                     reads=[rB[ob]], writes=[r_rd[rk]])
                                P.op("act", lambda e, rk=rk: e.activation(out=rd[rk][0:64, 0:nq], in_=rd[rk][0:64, 0:nq], func=AF.Exp, scale=-1.0),
                                     reads=[r_rd[rk]], writes=[r_rd[rk]])
                            else:
                                P.op("act", lambda e, ho=ho, rk=rk: e.activation(out=rd[rk][0:64, 0:nq], in_=Bk[ob][64:128, ho * nq:(ho + 1) * nq], func=AF.Ln,
                                                                                  scale=2.0 ** -24),
                                     reads=[rB[ob]], writes=[r_rd[rk]])
                                P.op("act", lambda e, rk=rk: e.activation(out=rd[rk][0:64, 0:nq], in_=rd[rk][0:64, 0:nq], func=AF.Exp, scale=-1.0),
                                     reads=[r_rd[rk]], writes=[r_rd[rk]])
                            if tp == 1:
                                P.op("dve", lambda e, ho=ho, rk=rk: e.tensor_tensor(out=rd[rk][0:64, 0:nq], in0=Bk[ob][0:64, ho * nq:(ho + 1) * nq],
                                                                                    in1=rd[rk][0:64, 0:nq], op=ALU.mult),
                                     reads=[rB[ob], r_rd[rk]], writes=[r_rd[rk]])
                            else:
                                P.op("dve", lambda e, ho=ho, rk=rk: e.scalar_tensor_tensor(out=rd[rk][0:64, 0:nq], in0=Bk[ob][0:64, ho * nq:(ho + 1) * nq],
                                                                                           scalar=2.0 ** -24, in1=rd[rk][0:64, 0:nq], op0=ALU.mult, op1=ALU.mult),
                                     reads=[rB[ob], r_rd[rk]], writes=[r_rd[rk]])
                            P.op("act", lambda e, po=po, rk=rk: e.copy(out=mixst[mxs][po:po + 64, tp * 4 + hp, 0:nq], in_=rd[rk][0:64, 0:nq]),
                                 reads=[r_rd[rk]], writes=[r_mixst[mxs]])

                SKEW = 3
                pend = []
                load_pair(0)
                for si_, st_ in enumerate(steps):
                    if si_ > 0 and st_[1] != steps[si_ - 1][1]:
                        while pend:
                            pv_step(*pend.pop(0))
                        yield "drained"
                    pi, sb_ = s_step(st_, si_)
                    if len(pend) >= SKEW:
                        pv_step(*pend.pop(0))
                    e_step(st_, pi, sb_)
                    pend.append((st_, pi))
                    if st_[3] == SKEW and st_[0] + 1 < len(pairs):
                        load_pair(st_[0] + 1)
                    yield st_[1]
                while pend:
                    pv_step(*pend.pop(0))
                P.dma("sp", d_mx[mxs], s_mix.rearrange("c p t -> p c t")[:, :, tokoff:tokoff + nq], mixst[mxs][:, :, 0:nq],
                      reads=[r_mixst[mxs]], writes=[r_smix])
                yield "drained"

            return idx, attn

        def ikt_loader(seq_, ncol):
            def f():
                P.dma("sp", d_ikt, IKTs[:, 0:ncol], s_ikt[seq_][:, :], reads=[r_scr[seq_]], writes=[r_IKT])
            return f
        tiles = []
        tiles_nkb = [4 * j + 4 for j in range(NQT)] + [NBS] * SB
        def _units(nkb_, nslots):
            L_ = nkb_ * 128
            return 1 + nslots * (((L_ + 511) // 512) * 8 + NITER + (nkb_ + 7) // 8)
        tiles_units = [_units(4 * j + 4, 2) for j in range(NQT)] + [_units(NBS, 1)] * SB
        for j in range(NQT):
            tiles.append(qtile(0, 4 * j + 4, 256, j * 256, [(128, 2 * j), (128, 2 * j + 1)], 4 * j, j % 2, j * 256, len(tiles),
                               ikt_loader(0, SEQ) if j == 0 else None))
        for sbi in range(SB):
            tiles.append(qtile(1 + sbi, NBS, T, NOWN * 128 + sbi * T, [(T, NOWN + sbi)], 8, 0, NOWN * 128 + sbi * T, len(tiles),
                               ikt_loader(1 + sbi, LS)))
        class _Idx:
            def __init__(self, gen):
                self.g = gen
                self.up = next(gen)

            def run_one(self):
                try:
                    self.up = next(self.g)
                except StopIteration:
                    self.up = None

            def finish(self):
                while self.up is not None:
                    self.run_one()

        _Idx(tiles[0][0]()).finish()
        for n in range(len(tiles)):
            ag = tiles[n][1]()
            ig = _Idx(tiles[n + 1][0]()) if n + 1 < len(tiles) else None
            credit = 0.0
            nsteps = 8 * (tiles_nkb[n])
            niu = tiles_units[n + 1] if ig is not None else 0
            wsum = 1.5 * nsteps
            for ev in ag:
                if ig is None:
                    continue
                if ev == "drained":
                    while ig.up == "T":
                        ig.run_one()
                    continue
                credit += niu * (2.0 if ev == 0 else 1.0) / wsum
                while credit >= 1.0 and ig.up == "u":
                    credit -= 1.0
                    ig.run_one()
            if ig is not None:
                ig.finish()


    def phase_c():
        apos[0] = 0
        WO = aview([128, 8, D], BF16); WG = aview([128, 8, D], BF16); WPLE = aview([128, 2, D], BF16)
        r_WC = Res("WC")
        vecC = aview([128, 5, D], F32); r_vecC = Res("vecC")
        WUPc = [aview([128, 8, 512], BF16) for _ in range(2)]; r_WUP = [Res("WUP%d" % i) for i in range(2)]
        WDNc = [aview([128, 4, D], BF16) for _ in range(2)]; r_WDN = [Res("WDN%d" % i) for i in range(2)]
        hidT = aview([128, 32, 512], BF16); r_hid = Res("hidT")
        x1 = aview([128, 4, D], F32); r_x1 = [Res("x1_%d" % i) for i in range(4)]
        xT1 = aview([128, 8, 512], BF16); r_xT1 = Res("xT1")
        mixg = aview([128, 8, 512], BF16); r_mixg = Res("mixg")
        xres = [aview([128, D], F32) for _ in range(2)]; r_xres = [Res("xres%d" % i) for i in range(2)]
        pet = [aview([128, 256], F32) for _ in range(2)]; r_pet = [Res("pet%d" % i) for i in range(2)]
        peT = [aview([128, 2, 128], BF16) for _ in range(4)]; r_peT = [Res("peT%d" % i) for i in range(4)]
        tA = [aview([128, D], F32) for _ in range(4)]; r_tA = [Res("tA%d" % i) for i in range(4)]
        rl = [aview([128, 512], F32) for _ in range(2)]; r_rl = [Res("rl%d" % i) for i in range(2)]
        st = aview([128, 32], F32); r_st = Res("st")
        d_wc = dsem(); d_wu = [dsem(), dsem(), dsem()]; d_wd = [dsem(), dsem(), dsem()]; d_mg = dsem(); d_xr = [dsem(), dsem()]; d_y = [dsem(), dsem()]; d_pe = [dsem(), dsem()]
        P.dma("pool", d_wc, WO[:, :, :], wo.rearrange("(k p) n -> p k n", p=128), writes=[r_WC])
        P.dma("pool", d_wc, WG[:, :, :], wg.rearrange("(k p) n -> p k n", p=128), writes=[r_WC])
        P.dma("pool", d_wc, WPLE[:, :, :], wple.rearrange("(k p) n -> p k n", p=128), writes=[r_WC])
        P.dma("sp", d_wc, vecC[:, :, :].rearrange("p a b -> p (a b)"), vecs[:, 8:8 + 5 * D], writes=[r_vecC])
        g1 = vecC[:, 0, :]; b1 = vecC[:, 1, :]; g2 = vecC[:, 2, :]; b2 = vecC[:, 3, :]; bg = vecC[:, 4, :]
        wupv = wup.rearrange("(k p) f -> p k f", p=128)
        wdnv = wdn.rearrange("(c p) n -> p c n", p=128)
        cn = {"xr": 0, "wu": 0, "wd": 0}

        def layer_norm(xa, rxa, gam, bet):
            stats = st[:, 0:12].rearrange("p (a b) -> p a b", b=6); mv = st[:, 12:14]; rstd = st[:, 14:15]
            for c2 in range(2):
                P.op("dve", lambda e, c2=c2: e.bn_stats(out=stats[:, c2, :], in_=xa[:, c2 * 512:(c2 + 1) * 512]), reads=[rxa], writes=[r_st])
            P.op("dve", lambda e: e.bn_aggr(out=mv, in_=st[:, 0:12]), reads=[r_st], writes=[r_st])
            P.op("act", lambda e: e.activation(out=rstd, in_=mv[:, 1:2], func=AF.Ln, bias=1e-5), reads=[r_st], writes=[r_st])
            P.op("act", lambda e: e.activation(out=rstd, in_=rstd, func=AF.Exp, scale=-0.5), reads=[r_st], writes=[r_st])
            P.op("dve", lambda e: e.tensor_scalar(out=xa, in0=xa, scalar1=mv[:, 0:1], scalar2=rstd, op0=ALU.subtract, op1=ALU.mult),
                 reads=[rxa, r_st], writes=[rxa])
            P.op("dve", lambda e: e.tensor_tensor(out=xa, in0=xa, in1=gam, op=ALU.mult), reads=[rxa, r_vecC], writes=[rxa])
            P.op("dve", lambda e: e.tensor_tensor(out=xa, in0=xa, in1=bet, op=ALU.add), reads=[rxa, r_vecC], writes=[rxa])

        def transpose_to(xa, rxa, tb, banks):
            for half in range(2):
                bk = banks[half]
                for kc in range(4):
                    c = half * 4 + kc
                    P.op("pe", lambda e, c=c, kc=kc, bk=bk: e.transpose(out=Bk[bk][:, kc * 128:(kc + 1) * 128], in_=xa[:, c * 128:(c + 1) * 128], identity=identf),
                         reads=[rxa, r_cst], writes=[rB[bk]])
                dst = xT1[:, half * 4:(half + 1) * 4, tb * 128:(tb + 1) * 128]
                src = Bk[bk].rearrange("p (a b) -> p a b", b=128)
                if half == 0:
                    P.op("act", lambda e, dst=dst, src=src: e.copy(out=dst, in_=src), reads=[rB[bk]], writes=[r_xT1])
                else:
                    P.op("dve", lambda e, dst=dst, src=src: e.tensor_copy(out=dst, in_=src), reads=[rB[bk]], writes=[r_xT1])

        def group(G, tok0, xsrc, psrc, ydst, row0):
            ntok = G * 128
            P.dma("sp", d_mg, mixg[:, :, 0:ntok], s_mix.rearrange("c p t -> p c t")[:, :, tok0:tok0 + ntok], reads=[r_smix], writes=[r_mixg])
            for tb in range(G):
                xr = cn["xr"] % 2; cn["xr"] += 1
                rows = slice(row0 + tb * 128, row0 + (tb + 1) * 128)
                P.dma("sp", d_xr[xr], xres[xr][:, :], xsrc[rows, :], writes=[r_xres[xr]])
                xa = x1[:, tb, :]
                for cc in range(2):
                    bk = tb * 2 + cc
                    for kc in range(8):
                        P.op("pe", lambda e, kc=kc, cc=cc, bk=bk: e.matmul(out=Bk[bk][:, 0:512], lhsT=mixg[:, kc, tb * 128:(tb + 1) * 128],
                                                                           rhs=WO[:, kc, cc * 512:(cc + 1) * 512], start=(kc == 0), stop=(kc == 7)),
                             reads=[r_mixg, r_WC], writes=[rB[bk]])
                    P.op("dve", lambda e, cc=cc, bk=bk: e.scalar_tensor_tensor(out=xa[:, cc * 512:(cc + 1) * 512], in0=xres[xr][:, cc * 512:(cc + 1) * 512],
                                                                               scalar=ALPHA, in1=Bk[bk][:, 0:512], op0=ALU.mult, op1=ALU.add),
                         reads=[r_xres[xr], rB[bk]], writes=[r_x1[tb]])
            for tb in range(G):
                layer_norm(x1[:, tb, :], r_x1[tb], g1, b1)
            for tb in range(G):
                transpose_to(x1[:, tb, :], r_x1[tb], tb, (tb * 2, tb * 2 + 1))
            for fch in range(32):
                c = fch // 4
                if fch % 4 == 0:
                    wu = cn["wu"] % 2; cn["wu"] += 1
                    P.dma("pool", d_wu[wu], WUPc[wu][:, :, :], wupv[:, :, c * 512:(c + 1) * 512], writes=[r_WUP[wu]])
                bk = fch % 2
                for kc in range(8):
                    P.op("pe", lambda e, kc=kc, fch=fch, bk=bk, wu=wu: e.matmul(out=Bk[bk][:, 0:ntok], lhsT=WUPc[wu][:, kc, (fch % 4) * 128:(fch % 4 + 1) * 128],
                                                                             rhs=xT1[:, kc, 0:ntok], start=(kc == 0), stop=(kc == 7)),
                         reads=[r_WUP[wu], r_xT1], writes=[rB[bk]])
                P.op("act", lambda e, bk=bk: e.activation(out=rl[bk][:, 0:ntok], in_=Bk[bk][:, 0:ntok], func=AF.Relu), reads=[rB[bk]], writes=[r_rl[bk]])
                P.op("dve", lambda e, bk=bk, fch=fch: e.tensor_tensor(out=hidT[:, fch, 0:ntok], in0=rl[bk][:, 0:ntok], in1=rl[bk][:, 0:ntok], op=ALU.mult),
                     reads=[r_rl[bk]], writes=[r_hid])
            for c in range(8):
                wd = cn["wd"] % 2; cn["wd"] += 1
                P.dma("pool", d_wd[wd], WDNc[wd][:, :, :], wdnv[:, c * 4:(c + 1) * 4, :], writes=[r_WDN[wd]])
                for tb in range(G):
                    for cc in range(2):
                        for f4 in range(4):
                            fc = c * 4 + f4
                            P.op("pe", lambda e, tb=tb, cc=cc, f4=f4, fc=fc, wd=wd: e.matmul(
                                out=Bk[tb * 2 + cc][:, 0:512], lhsT=hidT[:, fc, tb * 128:(tb + 1) * 128], rhs=WDNc[wd][:, f4, cc * 512:(cc + 1) * 512],
                                start=(fc == 0), stop=(fc == 31)), reads=[r_hid, r_WDN[wd]], writes=[rB[tb * 2 + cc]])
            for tb in range(G):
                xa = x1[:, tb, :]
                for cc in range(2):
                    P.op("dve", lambda e, cc=cc, tb=tb: e.scalar_tensor_tensor(out=xa[:, cc * 512:(cc + 1) * 512], in0=xa[:, cc * 512:(cc + 1) * 512], scalar=ALPHA,
                                                                               in1=Bk[tb * 2 + cc][:, 0:512], op0=ALU.mult, op1=ALU.add),
                         reads=[r_x1[tb], rB[tb * 2 + cc]], writes=[r_x1[tb]])
            for tb in range(G):
                layer_norm(x1[:, tb, :], r_x1[tb], g2, b2)
            for tb in range(G):
                transpose_to(x1[:, tb, :], r_x1[tb], tb, (tb * 2, tb * 2 + 1))
            for tb in range(G):
                k = tb
                rows = slice(row0 + tb * 128, row0 + (tb + 1) * 128)
                P.dma("sp", d_pe[tb % 2], pet[tb % 2][:, :], psrc[rows, :], writes=[r_pet[tb % 2]])
                for cc in range(2):
                    bk = tb * 2 + cc
                    for kc in range(8):
                        P.op("pe", lambda e, kc=kc, cc=cc, bk=bk, tb=tb: e.matmul(out=Bk[bk][:, 0:512], lhsT=xT1[:, kc, tb * 128:(tb + 1) * 128],
                                                                               rhs=WG[:, kc, cc * 512:(cc + 1) * 512], start=(kc == 0), stop=(kc == 7)),
                             reads=[r_xT1, r_WC], writes=[rB[bk]])
                    P.op("dve", lambda e, cc=cc, bk=bk, k=k: e.tensor_tensor(out=tA[k][:, cc * 512:(cc + 1) * 512], in0=Bk[bk][:, 0:512], in1=bg[:, cc * 512:(cc + 1) * 512], op=ALU.add),
                         reads=[rB[bk], r_vecC], writes=[r_tA[k]])
                for c2 in range(2):
                    P.op("pe", lambda e, c2=c2, tb=tb: e.transpose(out=Bk[tb * 2][:, c2 * 128:(c2 + 1) * 128], in_=pet[tb % 2][:, c2 * 128:(c2 + 1) * 128], identity=identf),
                         reads=[r_pet[tb % 2], r_cst], writes=[rB[tb * 2]])
                P.op("act", lambda e, tb=tb: e.copy(out=peT[tb][:, :, :], in_=Bk[tb * 2][:, 0:256].rearrange("p (a b) -> p a b", b=128)), reads=[rB[tb * 2]], writes=[r_peT[tb]])
            for tb in range(G):
                P.op("act", lambda e, k=tb: e.activation(out=tA[k][:, :], in_=tA[k][:, :], func=AF.Sigmoid), reads=[r_tA[tb]], writes=[r_tA[tb]])
            for tb in range(G):
                k = tb
                xa = x1[:, tb, :]
                rows = slice(row0 + tb * 128, row0 + (tb + 1) * 128)
                for cc in range(2):
                    bk = tb * 2 + cc
                    for c2 in range(2):
                        P.op("pe", lambda e, c2=c2, cc=cc, bk=bk, tb=tb: e.matmul(out=Bk[bk][:, 0:512], lhsT=peT[tb][:, c2, :], rhs=WPLE[:, c2, cc * 512:(cc + 1) * 512],
                                                                                 start=(c2 == 0), stop=(c2 == 1)), reads=[r_peT[tb], r_WC], writes=[rB[bk]])
                    P.op("dve", lambda e, cc=cc, bk=bk, k=k: e.tensor_tensor(out=tA[k][:, cc * 512:(cc + 1) * 512], in0=tA[k][:, cc * 512:(cc + 1) * 512], in1=Bk[bk][:, 0:512], op=ALU.mult),
                         reads=[r_tA[k], rB[bk]], writes=[r_tA[k]])
                P.op("dve", lambda e, k=k: e.tensor_tensor(out=tA[k][:, :], in0=tA[k][:, :], in1=xa, op=ALU.add), reads=[r_tA[k], r_x1[tb]], writes=[r_tA[k]])
                P.dma("sp", d_y[k % 2], ydst[rows, :], tA[k][:, :], reads=[r_tA[k]])

        for g in range(4):
            group(4, g * 512, xown, pown, o_y, g * 512)
        group(2, NOWN * 128, xs, pss, o_ys, 0)

    if STOP >= 2:
        P.barrier()
        phase_b()
    if STOP >= 3:
        P.barrier()
        phase_c()
    return nc, P, es


def _finish(nc, P, es):
    esem = {}
    for e in Prog.ENGS:
        esem[e] = es.enter_context(nc.semaphore("es_" + e))
    with nc.Block() as block:
        P.emit(nc, block, esem)
    es.close()
    return nc


OWN_PAIRS = {0: [0, 3, 4, 7, 8, 11, 12, 15], 1: [1, 2, 5, 6, 9, 10, 13, 14]}
_SPLITS = (512, 1024, 1536, 1544, 2056, 2568, 3080, 3336, 3368)


def _rope_table(pos):
    out = np.zeros((len(pos), 96), np.float32)
    p = np.asarray(pos, np.float32)[:, None]
    for half, o in ((32, 0), (16, 64)):
        inv = (np.float32(10000.0) ** (-np.arange(half, dtype=np.float32) / np.float32(half))).astype(np.float32)
        ang = (p * inv[None, :]).astype(np.float32)
        out[:, o:o + half] = np.cos(ang)
        out[:, o + half:o + 2 * half] = np.sin(ang)
    return out


def _own_blocks(half):
    bl = []
    for p in OWN_PAIRS[half]:
        bl += [2 * p, 2 * p + 1]
    return bl


def _masks(half):
    l = np.arange(128)[:, None]
    q = np.arange(128)[None, :]
    tri = (l <= q).astype(np.float32)
    one = np.ones((128, 128), np.float32)
    zer = np.zeros((128, 128), np.float32)
    MA = np.concatenate([tri, one], 1); MB = np.concatenate([zer, tri], 1)
    ON = np.concatenate([one, one], 1); ZE = np.concatenate([zer, zer], 1)
    fm = np.zeros((128, 2, 4, 256), np.float32)
    pc = np.zeros((128, 2, 2, 512), np.float32)
    for jpar in range(2):
        hi = (OWN_PAIRS[half][jpar] == 2 * jpar + 1)
        tiles = [ON, ON, MA, MB] if hi else [MA, MB, ZE, ZE]
        for i in range(4):
            fm[:, jpar, i, :] = tiles[i]
        for s in range(2):
            qb_rel = (2 if hi else 0) + s
            row = np.arange(128)[:, None]
            col = np.arange(512)[None, :]
            lim = qb_rel * 128 + np.where(row < 64, 64, 128)
            pc[:, jpar, s, :] = np.where(col < lim, BIG, NEG)
    sm = np.zeros((128, 64), np.float32)
    sm[:64, :] = (np.arange(64)[:, None] <= np.arange(64)[None, :]).astype(np.float32)
    return fm.reshape(128, -1), sm, pc.reshape(128, -1)


def _prep(inputs):
    f = lambda a: np.ascontiguousarray(np.asarray(a, dtype=np.float32))
    w_in = f(inputs["w_in"][0])
    fq, fk, fv, fg, dq, dk, dv, iq, ik, iw = np.split(w_in, _SPLITS, axis=1)
    wk = f(np.concatenate([fk, dk, fv, dv, ik, fg], 1))
    wq = f(np.concatenate([fq, dq, iq, iw], 1))
    shared = dict(
        wk=wk, wq=wq, wo=f(inputs["w_o"][0]), wup=f(inputs["w_up"][0]), wdn=f(inputs["w_down"][0]),
        wg=f(inputs["w_ple_gate"][0]), wple=f(inputs["w_ple"][0]),
        vecs=f(np.broadcast_to(np.concatenate([inputs["b_f"][0], inputs["ln1_g"][0], inputs["ln1_b"][0], inputs["ln2_g"][0],
                                               inputs["ln2_b"][0], inputs["b_ple_gate"][0]])[None, :], (128, 8 + 5 * D))),
        ropeall=_rope_table(np.arange(SEQ)),
        ropes=np.concatenate([_rope_table(PAST + np.arange(T)), np.zeros((64, 96), np.float32)], 0),
    )
    ident = np.eye(128, dtype=np.float32)
    U = (np.arange(128)[:, None] <= np.arange(128)[None, :]).astype(np.float32)
    E = np.zeros((128, 128), np.float32); E[127, :] = 1.0
    shared["consts"] = f(np.concatenate([ident, U, E], 1))
    maps = []
    for c in range(8):
        b, half = c // 2, c % 2
        rows = np.concatenate([np.arange(bl * 128, (bl + 1) * 128) for bl in _own_blocks(half)])
        fm, sm, pc = _masks(half)
        m = dict(shared)
        m.update(
            xall=f(inputs["x_prompt"][b]), xown=f(inputs["x_prompt"][b][rows]), pown=f(inputs["p_prompt"][0, b][rows]),
            xs=f(inputs["x_sample"][4 * c:4 * c + 4].reshape(SB * T, D)), pss=f(inputs["p_sample"][0, 4 * c:4 * c + 4].reshape(SB * T, 256)),
            cfk=f(inputs["cache_fox_k"][0, 4 * c:4 * c + 4].reshape(SB, PAST, 512)),
            cfv=f(inputs["cache_fox_v"][0, 4 * c:4 * c + 4].reshape(SB, PAST, 512)),
            clf=f(inputs["cache_fox_logf"][0, 4 * c:4 * c + 4]),
            cdk=f(inputs["cache_dsa_k"][0, 4 * c:4 * c + 4].reshape(SB, PAST, 512)),
            cdv=f(inputs["cache_dsa_v"][0, 4 * c:4 * c + 4].reshape(SB, PAST, 512)),
            cik=f(inputs["cache_idx_k"][0, 4 * c:4 * c + 4]),
            ropeown=f(shared["ropeall"][rows]), fmask=fm, smask=sm, pencap=pc,
        )
        maps.append(m)
    return maps


def kernel(**inputs):
    maps = _prep(inputs)
    nc, P, es = build()
    nc = _finish(nc, P, es)
    res = run_bass_kernel_spmd(nc, maps, core_ids=list(range(8)))
    R = res.results
    y_p = np.zeros((4, SEQ, D), np.float32)
    y_s = np.zeros((32, T, D), np.float32)
    for c in range(8):
        b, half = c // 2, c % 2
        rows = np.concatenate([np.arange(bl * 128, (bl + 1) * 128) for bl in _own_blocks(half)])
        y_p[b, rows] = R[c]["o_y"]
        y_s[4 * c:4 * c + 4] = R[c]["o_ys"].reshape(SB, T, D)

    def pk(name, last):
        return np.stack([R[2 * b][name] for b in range(4)], 0).reshape((1, 4, SEQ) + last)

    def sk(name, last):
        return np.concatenate([R[c][name].reshape((SB, T) + last) for c in range(8)], 0).reshape((1, 32, T) + last)

    return (y_p, y_s, pk("o_fk", (8, 64)), pk("o_fv", (8, 64)), pk("o_lf", (8,)), pk("o_dk", (8, 64)), pk("o_dv", (8, 64)),
            pk("o_ik", (32,)), sk("o_sfk", (8, 64)), sk("o_sfv", (8, 64)), sk("o_slf", (8,)), sk("o_sdk", (8, 64)),
            sk("o_sdv", (8, 64)), sk("o_sik", (32,)))
```
